# Optimizing a Trainium2 kernel written in Bass

```python
import math
import jax, jax.numpy as jnp
from jax import lax
import numpy as np

D_MODEL = 1024
BATCH = 4
SEQ = 8192
DEPTH = 2

GRID_W = 64
D_MIX = D_MODEL
GDN_WIDTH = D_MIX // 2
NA_WIDTH = D_MIX - GDN_WIDTH
GDN_HEAD_DIM = 128
GDN_HEADS = GDN_WIDTH // GDN_HEAD_DIM
NA_HEAD_DIM = 64
NA_HEADS = NA_WIDTH // NA_HEAD_DIM
CONV_WIDTH = 5
GDN_CHUNK = 64
GDN_CONV_CH = 3 * GDN_WIDTH
NA_WIN_R = 8
NA_WIN_C = 16
NA_QBLK_C = 16
NA_KSPAN_C = NA_QBLK_C + NA_WIN_C
D_FF = 4 * D_MODEL
DEEPNORM_ALPHA = (2 * DEPTH) ** 0.25
DEEPNORM_BETA = (8 * DEPTH) ** -0.25
LN_EPS = 1e-5
RMS_EPS = 1e-6
SPLIT_POINTS = (3 * GDN_WIDTH, 4 * GDN_WIDTH, 4 * GDN_WIDTH + 2 * GDN_HEADS, 4 * GDN_WIDTH + 4 * GDN_HEADS)
D_IN = 4 * GDN_WIDTH + 4 * GDN_HEADS + 3 * NA_WIDTH

kernel_name = "hybrid_gdn_natten_deepnorm_encoder"


def layer_norm(x, g, b):
    xf = x.astype(jnp.float32)
    mu = jnp.mean(xf, axis=-1, keepdims=True)
    var = jnp.mean(jnp.square(xf - mu), axis=-1, keepdims=True)
    y = (xf - mu) * lax.rsqrt(var + LN_EPS)
    return (y * g + b).astype(x.dtype)


def rms_norm(x, g):
    xf = x.astype(jnp.float32)
    y = xf * lax.rsqrt(jnp.mean(jnp.square(xf), axis=-1, keepdims=True) + RMS_EPS)
    return y * g


def l2_normalize(x):
    xf = x.astype(jnp.float32)
    return xf * lax.rsqrt(jnp.sum(jnp.square(xf), axis=-1, keepdims=True) + RMS_EPS)


def short_conv(x, w):
    pad = CONV_WIDTH // 2
    T = x.shape[1]
    xp = jnp.pad(x, ((0, 0), (pad, pad), (0, 0)))
    return sum(xp[:, i:i + T] * w[i] for i in range(CONV_WIDTH))


def gated_delta_chunked(q, k, v, g, beta):
    out_dtype = v.dtype
    B, T, H, dk = q.shape
    dv = v.shape[-1]
    C = GDN_CHUNK
    N = T // C
    f32 = jnp.float32

    def chunk(t):
        t = t.astype(f32).reshape((B, N, C, H) + t.shape[3:])
        return jnp.moveaxis(t, 3, 1)

    q = chunk(q) * (dk ** -0.5)
    k = chunk(k)
    v = chunk(v)
    g = chunk(g)
    beta = chunk(beta)
    gc = jnp.cumsum(g, axis=-1)
    lower = jnp.tril(jnp.ones((C, C), dtype=bool))
    strict = jnp.tril(jnp.ones((C, C), dtype=bool), -1)
    gamma = jnp.exp(jnp.where(lower, gc[..., :, None] - gc[..., None, :], -jnp.inf))
    kb = k * beta[..., None]
    a_mat = jnp.where(strict, jnp.einsum('bhncd,bhnsd->bhncs', kb, k) * gamma, 0.0)
    t_mat = a_mat + jnp.eye(C, dtype=f32)
    u = lax.linalg.triangular_solve(t_mat, v * beta[..., None], left_side=True, lower=True, unit_diagonal=True)
    w = lax.linalg.triangular_solve(t_mat, kb * jnp.exp(gc)[..., None], left_side=True, lower=True, unit_diagonal=True)
    qk = jnp.einsum('bhncd,bhnsd->bhncs', q, k) * gamma
    q_dec = q * jnp.exp(gc)[..., None]
    k_dec = k * jnp.exp(gc[..., -1:] - gc)[..., None]
    g_last = jnp.exp(gc[..., -1])
    xs = tuple(jnp.moveaxis(t, 2, 0) for t in (q_dec, k_dec, u, w, qk, g_last))

    def step(S, inp):
        qd, kd, u_n, w_n, qk_n, gl = inp
        v_new = u_n - jnp.einsum('bhck,bhkv->bhcv', w_n, S)
        o = jnp.einsum('bhck,bhkv->bhcv', qd, S) + jnp.einsum('bhcs,bhsv->bhcv', qk_n, v_new)
        S = S * gl[..., None, None] + jnp.einsum('bhck,bhcv->bhkv', kd, v_new)
        return S, o

    S0 = jnp.zeros((B, H, dk, dv), f32)
    _, o = lax.scan(step, S0, xs)
    o = jnp.transpose(o, (1, 0, 3, 2, 4)).reshape(B, T, H, dv)
    return o.astype(out_dtype)


def gdn_mixer(h_qkv, h_z, h_a, h_b, conv_w, a_log, dt_bias, norm_g):
    B, T, _ = h_qkv.shape
    qkv = jax.nn.silu(short_conv(h_qkv, conv_w))
    q, k, v = jnp.split(qkv, 3, axis=-1)
    q = l2_normalize(q.reshape(B, T, GDN_HEADS, GDN_HEAD_DIM))
    k = l2_normalize(k.reshape(B, T, GDN_HEADS, GDN_HEAD_DIM))
    v = v.reshape(B, T, GDN_HEADS, GDN_HEAD_DIM)
    a = h_a.astype(jnp.float32).reshape(B, T, 2, GDN_HEADS)
    b = h_b.astype(jnp.float32).reshape(B, T, 2, GDN_HEADS)
    g = -jnp.exp(a_log.astype(jnp.float32)) * jax.nn.softplus(a + dt_bias.astype(jnp.float32))
    beta = jax.nn.sigmoid(b)
    q2 = jnp.concatenate([q, q[:, ::-1]], axis=0)
    k2 = jnp.concatenate([k, k[:, ::-1]], axis=0)
    v2 = jnp.concatenate([v, v[:, ::-1]], axis=0)
    g2 = jnp.concatenate([g[:, :, 0], g[:, ::-1, 1]], axis=0)
    beta2 = jnp.concatenate([beta[:, :, 0], beta[:, ::-1, 1]], axis=0)
    o2 = gated_delta_chunked(q2, k2, v2, g2, beta2)
    o = o2[:B].astype(jnp.float32) + o2[B:, ::-1].astype(jnp.float32)
    z = h_z.reshape(B, T, GDN_HEADS, GDN_HEAD_DIM).astype(jnp.float32)
    o = rms_norm(o, norm_g) * jax.nn.silu(z)
    return o.reshape(B, T, GDN_WIDTH).astype(h_qkv.dtype)


def _na_column_tables():
    n_blk = GRID_W // NA_QBLK_C
    j = np.arange(n_blk)
    span_start = np.clip(j * NA_QBLK_C - NA_WIN_C // 2, 0, GRID_W - NA_KSPAN_C)
    key_cols = span_start[:, None] + np.arange(NA_KSPAN_C)[None, :]
    q_cols = j[:, None] * NA_QBLK_C + np.arange(NA_QBLK_C)[None, :]
    win_start = np.clip(q_cols - NA_WIN_C // 2, 0, GRID_W - NA_WIN_C)
    kc = key_cols[:, None, :]
    in_win = (kc >= win_start[:, :, None]) & (kc < win_start[:, :, None] + NA_WIN_C)
    rel_idx = np.clip(kc - q_cols[:, :, None] + NA_WIN_C - 1, 0, 2 * NA_WIN_C - 2)
    return key_cols, in_win, rel_idx


def neighborhood_attention(q, k, v, rpb):
    B, T, H, d = q.shape
    rows = T // GRID_W
    kr = min(NA_WIN_R, rows)
    n_blk = GRID_W // NA_QBLK_C
    key_cols, in_win, rel_idx = _na_column_tables()
    qg = q.reshape(B, rows, n_blk, NA_QBLK_C, H, d)
    kg = k.reshape(B, rows, GRID_W, H, d)
    vg = v.reshape(B, rows, GRID_W, H, d)
    scale = d ** -0.5
    mask = jnp.asarray(in_win)[None, None, :, :, None, :]
    rpb_cols = rpb.astype(jnp.float32)[:, :, rel_idx]

    def one_row(r):
        r0 = jnp.clip(r - kr // 2, 0, rows - kr)
        k_band = lax.dynamic_slice_in_dim(kg, r0, kr, axis=1)[:, :, key_cols]
        v_band = lax.dynamic_slice_in_dim(vg, r0, kr, axis=1)[:, :, key_cols]
        q_row = lax.dynamic_index_in_dim(qg, r, axis=1, keepdims=False)
        s = jnp.einsum('bjqhd,brjkhd->bhjqrk', q_row, k_band).astype(jnp.float32) * scale
        dr = r0 + jnp.arange(kr) - r + NA_WIN_R - 1
        bias = jnp.take(rpb_cols, dr, axis=1)
        s = s + jnp.transpose(bias, (0, 2, 3, 1, 4))[None]
        s = jnp.where(mask, s, -jnp.inf)
        p = jax.nn.softmax(s.reshape(s.shape[:4] + (kr * NA_KSPAN_C,)), axis=-1).reshape(s.shape)
        o = jnp.einsum('bhjqrk,brjkhd->bjqhd', p.astype(v.dtype), v_band)
        return o.reshape(B, GRID_W, H, d)

    out = lax.map(one_row, jnp.arange(rows))
    return jnp.transpose(out, (1, 0, 2, 3, 4)).reshape(B, T, H, d)


def setup_inputs(seed: int = 0) -> dict:
    key = jax.random.key(seed)
    ks = jax.random.split(key, 20)
    f32 = jnp.float32

    def nrm(k, shape, scale):
        return jax.random.normal(k, shape, f32) * scale

    x = nrm(ks[0], (BATCH, SEQ, D_MODEL), 1.0)
    ln_in_g = 1.0 + nrm(ks[1], (D_MODEL,), 0.02)
    ln_in_b = nrm(ks[2], (D_MODEL,), 0.02)
    w_in = nrm(ks[3], (DEPTH, D_MODEL, D_IN), D_MODEL ** -0.5)
    conv_w = nrm(ks[4], (DEPTH, CONV_WIDTH, GDN_CONV_CH), CONV_WIDTH ** -0.5)
    a_log = jnp.log(jax.random.uniform(ks[5], (DEPTH, 2, GDN_HEADS), f32, 1.0, 16.0))
    dt = jnp.exp(jax.random.uniform(ks[6], (DEPTH, 2, GDN_HEADS), f32, math.log(1e-3), math.log(1e-1)))
    dt_bias = dt + jnp.log(-jnp.expm1(-dt))
    gdn_norm_g = 1.0 + nrm(ks[7], (DEPTH, GDN_HEAD_DIM), 0.02)
    rpb = nrm(ks[8], (DEPTH, NA_HEADS, 2 * NA_WIN_R - 1, 2 * NA_WIN_C - 1), 0.1)
    na_norm_g = 1.0 + nrm(ks[9], (DEPTH, NA_HEAD_DIM), 0.02)
    w_out = nrm(ks[10], (DEPTH, D_MIX, D_MODEL), D_MIX ** -0.5 * DEEPNORM_BETA)
    ln1_g = 1.0 + nrm(ks[11], (DEPTH, D_MODEL), 0.02)
    ln1_b = nrm(ks[12], (DEPTH, D_MODEL), 0.02)
    w1 = nrm(ks[13], (DEPTH, D_MODEL, D_FF), D_MODEL ** -0.5)
    b1 = nrm(ks[14], (DEPTH, D_FF), 0.01)
    w2 = nrm(ks[15], (DEPTH, D_FF, D_MODEL), D_FF ** -0.5 * DEEPNORM_BETA)
    b2 = nrm(ks[16], (DEPTH, D_MODEL), 0.01)
    ln2_g = 1.0 + nrm(ks[17], (DEPTH, D_MODEL), 0.02)
    ln2_b = nrm(ks[18], (DEPTH, D_MODEL), 0.02)
    return {"x": x, "ln_in_g": ln_in_g, "ln_in_b": ln_in_b, "w_in": w_in, "conv_w": conv_w,
            "a_log": a_log, "dt_bias": dt_bias, "gdn_norm_g": gdn_norm_g, "rpb": rpb,
            "na_norm_g": na_norm_g, "w_out": w_out, "ln1_g": ln1_g, "ln1_b": ln1_b,
            "w1": w1, "b1": b1, "w2": w2, "b2": b2, "ln2_g": ln2_g, "ln2_b": ln2_b}


def reference(x, ln_in_g, ln_in_b, w_in, conv_w, a_log, dt_bias, gdn_norm_g, rpb,
              na_norm_g, w_out, ln1_g, ln1_b, w1, b1, w2, b2, ln2_g, ln2_b):
    B, T, _ = x.shape
    x = layer_norm(x, ln_in_g, ln_in_b)
    for l in range(DEPTH):
        h = jnp.einsum('btd,de->bte', x, w_in[l])
        h_qkv, h_z, h_a, h_b, h_na = jnp.split(h, SPLIT_POINTS, axis=-1)
        o_gdn = gdn_mixer(h_qkv, h_z, h_a, h_b, conv_w[l], a_log[l], dt_bias[l], gdn_norm_g[l])
        q_na, k_na, v_na = (t.reshape(B, T, NA_HEADS, NA_HEAD_DIM) for t in jnp.split(h_na, 3, axis=-1))
        o_na = neighborhood_attention(q_na, k_na, v_na, rpb[l])
        o_na = rms_norm(o_na, na_norm_g[l]).reshape(B, T, NA_WIDTH).astype(x.dtype)
        mix = jnp.einsum('bte,ed->btd', jnp.concatenate([o_gdn, o_na], axis=-1), w_out[l])
        x = layer_norm(DEEPNORM_ALPHA * x + mix, ln1_g[l], ln1_b[l])
        ff = jnp.square(jax.nn.relu(jnp.einsum('btd,df->btf', x, w1[l]) + b1[l]))
        ff = jnp.einsum('btf,fd->btd', ff, w2[l]) + b2[l]
        x = layer_norm(DEEPNORM_ALPHA * x + ff, ln2_g[l], ln2_b[l])
    return x
```

```python
import numpy as np
import ml_dtypes
from contextlib import ExitStack
import concourse.bass as bass
import concourse.mybir as mybir
from concourse.bass_utils import run_bass_kernel_spmd

F32 = mybir.dt.float32
BF16 = mybir.dt.bfloat16
ALU = mybir.AluOpType
AF = mybir.ActivationFunctionType
AX = mybir.AxisListType

D = 1024
T = 4096
TH = 4352
NT = 32
DIN = 3600
DFF = 4096
ALPHA = 4.0 ** 0.25
LN_EPS = 1e-5
RMS_EPS = 1e-6
NEG = -30000.0


class Prog:
    ENGS = ("pe", "act", "dve", "pool", "sp")

    def __init__(self, nc):
        self.nc = nc
        self.ops = {e: [] for e in self.ENGS}
        self.cnt = {}
        self.seen = {e: {} for e in self.ENGS}
        self.W = {}
        self.R = {}
        self.awaited = {}

    def op(self, eng, fn, reads=(), writes=(), dma=None):
        assert fn is not None or not writes
        own = "eng:" + eng
        waits = {}

        def merge(d, skip_own):
            if d:
                for s, c in d.items():
                    if skip_own and s == own and not dma:
                        continue
                    if waits.get(s, 0) < c:
                        waits[s] = c

        for k in reads:
            merge(self.W.get(k), False)
        for k in writes:
            merge(self.W.get(k), True)
            merge(self.R.get(k), True)
        sem = ("dma:" + dma) if dma else own
        for s, c in waits.items():
            if self.seen[eng].get(s, 0) >= c:
                continue
            self.seen[eng][s] = c
            self.ops[eng].append(("wait", s, c))
            self.awaited.setdefault(s, set()).add(c)
        n = self.cnt.get(sem, 0) + 1
        self.cnt[sem] = n
        self.ops[eng].append(("op", fn, sem, n))
        for k in reads:
            self.R.setdefault(k, {})[sem] = n
        for k in writes:
            self.W[k] = {sem: n}
            self.R[k] = {}

    def dma(self, eng, out, in_, reads, writes, group):
        self.op(eng, lambda e: e.dma_start(out=out, in_=in_), reads=reads,
                writes=writes, dma=group)

    def dma_multi(self, eng, pairs, keys, group):
        for (out, in_), k in zip(pairs, keys):
            self.dma(eng, out, in_, [], [k], group)
        sem = "dma:" + group
        for k in keys:
            self.W[k] = {sem: self.cnt[sem]}

    def emit(self, name):
        nc = self.nc
        cmap = {}
        for s, cs in self.awaited.items():
            for i, c in enumerate(sorted(cs)):
                cmap[(s, c)] = i + 1
        allsems = set(self.awaited)
        for s, n in self.cnt.items():
            if s.startswith("dma:"):
                allsems.add(s)
                for c in range(1, n + 1):
                    cmap[(s, c)] = c
        with ExitStack() as st:
            st.enter_context(nc.cleanup_on_exit())
            sems = {}
            for s in sorted(allsems):
                _UID[0] += 1
                sems[s] = nc.alloc_semaphore(name=f"{name}_{_UID[0]}_" + s.replace(":", "_"))
            block = st.enter_context(nc.Block())

            def make(ename):
                def body(e):
                    for o in self.ops[ename]:
                        if o[0] == "wait":
                            _, s, c = o
                            mult = 16 if s.startswith("dma:") else 1
                            e.wait_ge(sems[s], cmap[(s, c)] * mult)
                        else:
                            _, fn, s, n = o
                            if fn is None:
                                continue
                            ins = fn(e)
                            if (s, n) in cmap:
                                ins.then_inc(
                                    sems[s], 16 if s.startswith("dma:") else 1)
                return body

            block.tensor(make("pe"))
            block.scalar(make("act"))
            block.vector(make("dve"))
            block.gpsimd(make("pool"))
            block.sync(make("sp"))


_UID = [0]


def sb(st, nc, name, shape, dt):
    _UID[0] += 1
    return st.enter_context(nc.sbuf_tensor(f"{name}_u{_UID[0]}", list(shape), dt))


def pst(st, nc, name, shape, dt):
    _UID[0] += 1
    return st.enter_context(nc.psum_tensor(f"{name}_u{_UID[0]}", list(shape), dt))


def phase1(nc, io, layer0, xname="x"):
    x_d = io[xname]
    with ExitStack() as st:
        P = Prog(nc)
        win = sb(st, nc, "p1_win", [128, 8, DIN], BF16)
        ident = sb(st, nc, "p1_ident", [128, 128], F32)
        lng = sb(st, nc, "p1_lng", [128, D], F32)
        lnb = sb(st, nc, "p1_lnb", [128, D], F32)
        gpar = sb(st, nc, "p1_gpar", [128, 16], F32)
        nea = sb(st, nc, "p1_nea", [128, 8], F32)
        zpad = sb(st, nc, "p1_zpad", [128, 2], F32)
        xt = [sb(st, nc, f"p1_xt{i}", [128, D], F32) for i in range(3)]
        xn = [sb(st, nc, f"p1_xn{i}", [128, D], F32) for i in range(2)]
        stats = sb(st, nc, "p1_stats", [128, 2, 6], F32)
        mv = sb(st, nc, "p1_mv", [128, 2], F32)
        rstd = sb(st, nc, "p1_rstd", [128, 1], F32)
        xT = [sb(st, nc, f"p1_xT{i}", [128, 8, 512], BF16) for i in range(2)]
        fo = [sb(st, nc, f"p1_fo{i}", [128, 512], F32) for i in range(3)]
        fob = [sb(st, nc, f"p1_fob{i}", [128, 512], BF16) for i in range(3)]
        to = [sb(st, nc, f"p1_to{i}", [128, 512], F32) for i in range(2)]
        tob = [sb(st, nc, f"p1_tob{i}", [128, 512], BF16) for i in range(2)]
        ab = [sb(st, nc, f"p1_ab{i}", [128, 16], F32) for i in range(2)]
        abt = [sb(st, nc, f"p1_abt{i}", [128, 16], F32) for i in range(2)]
        psT = [pst(st, nc, f"p1_psT{i}", [128, 512], F32) for i in range(2)]
        psF = [pst(st, nc, f"p1_psF{i}", [128, 512], F32) for i in range(3)]
        psK = [pst(st, nc, f"p1_psK{i}", [128, 512], F32) for i in range(2)]
        psA = pst(st, nc, "p1_psA", [128, 16], F32)

        for k in range(8):
            P.dma("pool", win[:, k, :], io["w_in"][k * 128:(k + 1) * 128, :],
                  [], [("win", k)], f"win{k}")
        P.dma("sp", ident[:], io["ident"][:, :], [], ["ident"], "c0")
        if layer0:
            P.dma("sp", lng[:], io["ln_in_g"].partition_broadcast(128), [], ["lng"], "c2")
            P.dma("sp", lnb[:], io["ln_in_b"].partition_broadcast(128), [], ["lnb"], "c3")
        P.dma("sp", gpar[:], io["gpar"].partition_broadcast(128), [], ["gpar"], "c4")
        P.op("act", lambda e: e.activation(out=nea[:], in_=gpar[:, 0:8], func=AF.Exp),
             ["gpar"], ["nea0"])
        P.op("dve", lambda e: e.tensor_scalar(out=nea[:], in0=nea[:], scalar1=-1.0,
                                              scalar2=None, op0=ALU.mult),
             ["nea0"], ["nea"])
        P.op("pool", lambda e: e.memset(zpad[:], 0.0), [], ["zpad"])
        for c in range(12):
            P.dma("sp", io["hq"][c * 128:(c + 1) * 128, 0:2], zpad[:], ["zpad"],
                  [("hq", c, -1)], "zp")

        nmac = 9
        cnt = {"xt": 0, "xn": 0, "fo": 0, "to": 0, "psF": 0, "psK": 0, "ab": 0}
        for mt in range(nmac):
            halo = mt == 8
            nsub = 2 if halo else 4
            ntok = nsub * 128
            tok0 = mt * 512
            xTm = xT[mt % 2]
            kxT = ("xT", mt % 2)
            for sub in range(nsub):
                t0 = tok0 + sub * 128
                xi = cnt["xt"] % 3
                cnt["xt"] += 1
                xtile = xt[xi]
                P.dma("sp", xtile[:], x_d[t0:t0 + 128, :], [], [("xt", xi)], f"xt{xi}")
                src = xtile
                ksrc = ("xt", xi)
                if layer0:
                    ni = cnt["xn"] % 2
                    cnt["xn"] += 1
                    xnt = xn[ni]
                    for hh in range(2):
                        P.op("dve", lambda e, hh=hh, xtile=xtile: e.bn_stats(
                            out=stats[:, hh, :], in_=xtile[:, hh * 512:(hh + 1) * 512]),
                            [("xt", xi)], [("stats", hh)])
                    P.op("dve", lambda e: e.bn_aggr(out=mv[:], in_=stats[:].rearrange("p a b -> p (a b)")),
                         [("stats", 0), ("stats", 1)], ["mv"])
                    P.op("act", lambda e: e.activation(
                        out=rstd[:], in_=mv[:, 1:2], func=AF.Ln, bias=LN_EPS), ["mv"], ["rstd"])
                    P.op("act", lambda e: e.activation(
                        out=rstd[:], in_=rstd[:], func=AF.Exp, scale=-0.5), ["rstd"], ["rstd"])
                    P.op("dve", lambda e, xtile=xtile, xnt=xnt: e.tensor_scalar(
                        out=xnt[:], in0=xtile[:], scalar1=mv[:, 0:1], scalar2=rstd[:, 0:1],
                        op0=ALU.subtract, op1=ALU.mult),
                        [("xt", xi), "mv", "rstd"], [("xn", ni)])
                    P.op("pool", lambda e, xnt=xnt: e.tensor_tensor(
                        out=xnt[:], in0=xnt[:], in1=lng[:], op=ALU.mult),
                        [("xn", ni), "lng"], [("xn", ni)])
                    P.op("pool", lambda e, xnt=xnt: e.tensor_tensor(
                        out=xnt[:], in0=xnt[:], in1=lnb[:], op=ALU.add),
                        [("xn", ni), "lnb"], [("xn", ni)])
                    if not halo:
                        P.dma("sp", io["xn"][t0:t0 + 128, :], xnt[:], [("xn", ni)],
                              [("xn_d", t0 // 128)], f"xns{ni}")
                    src = xnt
                    ksrc = ("xn", ni)
                idm, kid = ident, "ident"
                for hb in range(2):
                    pT = psT[hb]

                    def tr(e, hb=hb, pT=pT, src=src, idm=idm):
                        ins = None
                        for kk in range(4):
                            k = hb * 4 + kk
                            ins = e.transpose(out=pT[:, kk * 128:(kk + 1) * 128],
                                              in_=src[:, k * 128:(k + 1) * 128],
                                              identity=idm[:])
                        return ins
                    P.op("pe", tr, [ksrc, kid], [("psT", hb)])
                    outap = xTm[:, hb * 4:(hb + 1) * 4, sub * 128:(sub + 1) * 128]
                    inap = pT[:].rearrange("p (k t) -> p k t", k=4)
                    if hb == 0:
                        P.op("act", lambda e, o=outap, i=inap: e.copy(out=o, in_=i),
                             [("psT", hb)], [(kxT, sub, hb)])
                    else:
                        P.op("dve", lambda e, o=outap, i=inap: e.tensor_copy(out=o, in_=i),
                             [("psT", hb)], [(kxT, sub, hb)])
            xkeys = [(kxT, s_, h_) for s_ in range(nsub) for h_ in range(2)]
            wkeys = [("win", k) for k in range(8)]

            fm = [(c * 128, "hq", c) for c in range(12)]
            if not halo:
                fm += [(2064 + c * 128, "q", c) for c in range(4)]
            fm += [(2576 + c * 128, "k", c) for c in range(4)]
            for (col, kind, c) in fm:
                n = ntok
                if halo and kind == "hq":
                    n = 2
                pi = cnt["psF"] % 3
                cnt["psF"] += 1
                pF = psF[pi]

                def mm(e, col=col, n=n, pF=pF, xTm=xTm):
                    ins = None
                    for k in range(8):
                        ins = e.matmul(pF[:, 0:n], lhsT=win[:, k, col:col + 128],
                                       rhs=xTm[:, k, 0:n], start=(k == 0), stop=(k == 7))
                    return ins
                P.op("pe", mm, xkeys + wkeys, [("psF", pi)])
                fi = cnt["fo"] % 3
                cnt["fo"] += 1
                if kind == "hq":
                    fbuf = fo[fi]
                    P.op("act", lambda e, fbuf=fbuf, pF=pF, n=n: e.copy(out=fbuf[:, 0:n], in_=pF[:, 0:n]),
                         [("psF", pi)], [("fo", fi)])
                    P.dma("sp", io["hq"][c * 128:(c + 1) * 128, 2 + tok0:2 + tok0 + n],
                          fbuf[:, 0:n], [("fo", fi)], [("hq", c, mt)], f"fo{fi}")
                else:
                    fbuf = fob[fi]
                    sc = 0.125 if kind == "q" else 1.0
                    P.op("act", lambda e, fbuf=fbuf, pF=pF, n=n, sc=sc: e.activation(
                        out=fbuf[:, 0:n], in_=pF[:, 0:n], func=AF.Copy, scale=sc),
                        [("psF", pi)], [("fob", fi)])
                    dst = io["qn"] if kind == "q" else io["kn"]
                    P.dma("sp", dst[c * 128:(c + 1) * 128, tok0:tok0 + n], fbuf[:, 0:n],
                          [("fob", fi)], [(kind + "n", c, mt)], f"fob{fi}")

            for sub in range(nsub):
                t0 = tok0 + sub * 128
                tk = [("v", 3088)]
                if not halo:
                    tk = [("z", 1536), ("v", 3088), ("ab", 2048)]
                for (kind, col) in tk:
                    if kind == "ab":
                        def mm(e, sub=sub, xTm=xTm):
                            ins = None
                            for k in range(8):
                                ins = e.matmul(psA[:, 0:16], lhsT=xTm[:, k, sub * 128:(sub + 1) * 128],
                                               rhs=win[:, k, 2048:2064], start=(k == 0), stop=(k == 7))
                            return ins
                        P.op("pe", mm, xkeys + wkeys, ["psA"])
                        ai = cnt["ab"] % 2
                        cnt["ab"] += 1
                        a_, at_ = ab[ai], abt[ai]
                        P.op("dve", lambda e, at_=at_: e.tensor_copy(out=at_[:, 8:16], in_=psA[:, 8:16]),
                             ["psA"], [("abt2", ai)])
                        P.op("dve", lambda e, at_=at_: e.tensor_tensor(
                            out=at_[:, 0:8], in0=psA[:, 0:8], in1=gpar[:, 8:16], op=ALU.add),
                            ["psA", "gpar"], [("abt", ai)])
                        P.op("act", lambda e, at_=at_: e.activation(
                            out=at_[:, 8:16], in_=at_[:, 8:16], func=AF.Exp, scale=-1.0),
                            [("abt2", ai)], [("abt2", ai)])
                        P.op("act", lambda e, at_=at_: e.activation(
                            out=at_[:, 0:8], in_=at_[:, 0:8], func=AF.Exp),
                            [("abt", ai)], [("abt", ai)])
                        P.op("act", lambda e, at_=at_: e.activation(
                            out=at_[:, 0:8], in_=at_[:, 0:8], func=AF.Ln, bias=1.0),
                            [("abt", ai)], [("abt", ai)])
                        P.op("dve", lambda e, at_=at_, a_=a_: e.tensor_tensor(
                            out=a_[:, 0:8], in0=at_[:, 0:8], in1=nea[:], op=ALU.mult),
                            [("abt", ai), "nea"], [("ab", ai)])
                        P.op("dve", lambda e, at_=at_: e.tensor_scalar(
                            out=at_[:, 8:16], in0=at_[:, 8:16], scalar1=1.0, scalar2=None,
                            op0=ALU.add), [("abt2", ai)], [("abt2", ai)])
                        P.op("dve", lambda e, at_=at_, a_=a_: e.reciprocal(
                            out=a_[:, 8:16], in_=at_[:, 8:16]),
                            [("abt2", ai), ("ab", ai)], [("ab", ai)])
                        P.dma("sp", io["gb"][t0:t0 + 128, :], a_[:], [("ab", ai)],
                              [("gb", t0 // 128)], f"ab{ai}")
                        continue
                    pi = cnt["psK"] % 2
                    cnt["psK"] += 1
                    pK = psK[pi]

                    def mm(e, sub=sub, xTm=xTm, pK=pK, col=col):
                        ins = None
                        for k in range(8):
                            ins = e.matmul(pK[:, :], lhsT=xTm[:, k, sub * 128:(sub + 1) * 128],
                                           rhs=win[:, k, col:col + 512], start=(k == 0), stop=(k == 7))
                        return ins
                    P.op("pe", mm, xkeys + wkeys, [("psK", pi)])
                    ti = cnt["to"] % 2
                    cnt["to"] += 1
                    if kind == "z":
                        tb = to[ti]
                        P.op("act", lambda e, tb=tb, pK=pK: e.copy(out=tb[:], in_=pK[:]),
                             [("psK", pi)], [("to", ti)])
                        P.dma("sp", io["zs"][t0:t0 + 128, :], tb[:], [("to", ti)],
                              [("zs", t0 // 128)], f"to{ti}")
                    else:
                        tb = tob[ti]
                        P.op("dve", lambda e, tb=tb, pK=pK: e.tensor_copy(out=tb[:], in_=pK[:]),
                             [("psK", pi)], [("tob", ti)])
                        P.dma("sp", io["vn"][t0:t0 + 128, :], tb[:], [("tob", ti)],
                              [("vn", t0 // 128)], f"tob{ti}")
        allk = [k for k in P.W if isinstance(k, tuple) and k[0] in
                ("hq", "qn", "kn", "vn", "zs", "gb", "xn_d")]
        P.op("sp", None, reads=allk)
        P.emit("p1")


def phase2(nc, io, mts=range(8), tag="p2"):
    with ExitStack() as st:
        P = Prog(nc)
        cw = sb(st, nc, "p2_cw", [128, 12, 5], F32)
        onesb = sb(st, nc, "p2_ones", [128, 128], BF16)
        identb = sb(st, nc, "p2_identb", [128, 128], BF16)
        hw = [sb(st, nc, f"p2_hw{i}", [128, 516], F32) for i in range(3)]
        acc = [sb(st, nc, f"p2_acc{i}", [128, 512], F32) for i in range(2)]
        ptmp = sb(st, nc, "p2_ptmp", [128, 512], F32)
        sil = [sb(st, nc, f"p2_sil{i}", [128, 512], F32) for i in range(12)]
        sqb = [sb(st, nc, f"p2_sq{i}", [128, 512], BF16) for i in range(2)]
        lnv = [sb(st, nc, f"p2_ln{i}", [128, 512], F32) for i in range(8)]
        nb = [sb(st, nc, f"p2_nb{i}", [128, 512], BF16) for i in range(12)]
        tk = [sb(st, nc, f"p2_tk{i}", [128, 512], BF16) for i in range(2)]
        psN = [pst(st, nc, f"p2_psN{i}", [128, 512], F32) for i in range(4)]
        psT = [pst(st, nc, f"p2_psT{i}", [128, 512], BF16) for i in range(2)]
        P.dma_multi("sp", [(cw[:, c, :], io["convw"][c * 128:(c + 1) * 128, :]) for c in range(12)],
                    [("cw", c) for c in range(12)], "c0")
        P.op("pool", lambda e: e.memset(onesb[:], 1.0), [], ["ones"])
        P.dma("pool", identb[:], io["ident"][:, :], [], ["identb"], "c1")
        cnt = {"hw": 0, "acc": 0, "sq": 0, "psN": 0, "psT": 0, "tk": 0}
        stop = ""
        for mt in mts:
            tok0 = mt * 512
            for c in range(12):
                hi = cnt["hw"] % 3
                cnt["hw"] += 1
                h_ = hw[hi]
                P.dma("sp", h_[:], io["hq"][c * 128:(c + 1) * 128, tok0:tok0 + 516], [], [("hw", hi)], f"hw{hi}")
                ai = cnt["acc"] % 2
                cnt["acc"] += 1
                a_ = acc[ai]
                eng = "pool" if c % 3 == 2 else "dve"
                P.op(eng, lambda e, a_=a_, h_=h_, c=c: e.tensor_scalar(
                    out=a_[:], in0=h_[:, 0:512], scalar1=cw[:, c, 0:1], scalar2=None, op0=ALU.mult),
                    [("hw", hi), ("cw", c)], [("acc", ai)])
                for i in range(1, 5):
                    if eng == "dve":
                        P.op(eng, lambda e, a_=a_, h_=h_, c=c, i=i: e.scalar_tensor_tensor(
                            out=a_[:], in0=h_[:, i:i + 512], scalar=cw[:, c, i:i + 1], in1=a_[:],
                            op0=ALU.mult, op1=ALU.add), [("hw", hi), ("cw", c), ("acc", ai)], [("acc", ai)])
                    else:
                        P.op(eng, lambda e, h_=h_, c=c, i=i: e.tensor_scalar(
                            out=ptmp[:], in0=h_[:, i:i + 512], scalar1=cw[:, c, i:i + 1], scalar2=None,
                            op0=ALU.mult), [("hw", hi), ("cw", c)], ["ptmp"])
                        P.op(eng, lambda e, a_=a_: e.tensor_tensor(out=a_[:], in0=a_[:], in1=ptmp[:], op=ALU.add),
                             ["ptmp", ("acc", ai)], [("acc", ai)])
                P.op("act", lambda e, a_=a_, c=c: e.activation(out=sil[c][:], in_=a_[:], func=AF.Silu),
                     [("acc", ai)], [("sil", c)])
            if stop == "s1":
                break
            for c in range(8):
                si = cnt["sq"] % 2
                cnt["sq"] += 1
                P.op("pool", lambda e, c=c, si=si: e.tensor_tensor(
                    out=sqb[si][:], in0=sil[c][:], in1=sil[c][:], op=ALU.mult), [("sil", c)], [("sq", si)])
                pi = cnt["psN"] % 4
                cnt["psN"] += 1
                pN = psN[pi]
                P.op("pe", lambda e, pN=pN, si=si: e.matmul(pN[:, :], lhsT=onesb[:, :], rhs=sqb[si][:, :],
                                                             start=True, stop=True),
                     [("sq", si), "ones"], [("psN", pi)])
                P.op("act", lambda e, pN=pN, c=c: e.activation(out=lnv[c][:], in_=pN[:], func=AF.Ln, bias=RMS_EPS),
                     [("psN", pi)], [("ln", c)])
            if stop == "s2":
                break
            for c in range(12):
                if c < 8:
                    P.op("act", lambda e, c=c: e.activation(out=lnv[c][:], in_=lnv[c][:], func=AF.Exp, scale=-0.5),
                         [("ln", c)], [("ln", c)])
                    sc = 128.0 ** -0.5 if c < 4 else 1.0
                    P.op("dve", lambda e, c=c, sc=sc: e.scalar_tensor_tensor(
                        out=nb[c][:], in0=sil[c][:], scalar=sc, in1=lnv[c][:], op0=ALU.mult, op1=ALU.mult),
                        [("sil", c), ("ln", c)], [("nb", c)])
                    dst = io["qg"] if c < 4 else io["kg"]
                    cc = c % 4
                    P.dma("sp", dst[cc * 128:(cc + 1) * 128, tok0:tok0 + 512], nb[c][:], [("nb", c)],
                          [("qkg", c, mt)], f"nb{c}")
                else:
                    P.op("pool", lambda e, c=c: e.tensor_copy(out=nb[c][:], in_=sil[c][:]),
                         [("sil", c)], [("nb", c)])
            if stop == "s3":
                break
            for kind, base, dname in (("k", 4, "ktok"), ("v", 8, "vtok")):
                for sub in range(4):
                    pi = cnt["psT"] % 2
                    cnt["psT"] += 1
                    pT = psT[pi]

                    def tr(e, pT=pT, base=base, sub=sub):
                        ins = None
                        for h in range(4):
                            ins = e.transpose(out=pT[:, h * 128:(h + 1) * 128],
                                              in_=nb[base + h][:, sub * 128:(sub + 1) * 128],
                                              identity=identb[:])
                        return ins
                    P.op("pe", tr, [("nb", base + h) for h in range(4)] + ["identb"], [("psT", pi)])
                    ti = cnt["tk"] % 2
                    cnt["tk"] += 1
                    if ti == 0:
                        P.op("act", lambda e, pT=pT, ti=ti: e.copy(out=tk[ti][:], in_=pT[:]),
                             [("psT", pi)], [("tk", ti)])
                    else:
                        P.op("dve", lambda e, pT=pT, ti=ti: e.tensor_copy(out=tk[ti][:], in_=pT[:]),
                             [("psT", pi)], [("tk", ti)])
                    t0 = tok0 + sub * 128
                    P.dma("sp", io[dname][t0:t0 + 128, :], tk[ti][:], [("tk", ti)],
                          [(dname, t0 // 128)], f"tk{ti}")
        allk = [k for k in P.W if isinstance(k, tuple) and k[0] in ("qkg", "ktok", "vtok")]
        P.op("sp", None, reads=allk)
        P.emit(tag)


def phase3(nc, io, dirn, tag, fused=False):
    sfx = "A" if dirn == 0 else "B"
    with ExitStack() as st:
        P = Prog(nc)
        LT = sb(st, nc, "p3_LT", [128, 128], F32)
        XS = sb(st, nc, "p3_XS", [128, 128], F32)
        NTI = sb(st, nc, "p3_NTI", [128, 512], F32)
        NS = sb(st, nc, "p3_NS", [128, 512], F32)
        ident = sb(st, nc, "p3_ident", [128, 128], F32)
        identb = sb(st, nc, "p3_identb", [128, 128], BF16)
        I4f = sb(st, nc, "p3_I4f", [128, 512], F32)
        ones = sb(st, nc, "p3_ones", [128, 128], F32)
        S4 = sb(st, nc, "p3_S4", [128, 512], F32)
        Sbf = sb(st, nc, "p3_Sbf", [128, 512], BF16)
        NS_ = 2
        qT4 = [sb(st, nc, f"p3_qT{i}", [128, 4, 128], BF16) for i in range(NS_)]
        kT4 = [sb(st, nc, f"p3_kT{i}", [128, 4, 128], BF16) for i in range(NS_)]
        kt4 = [sb(st, nc, f"p3_kt{i}", [128, 4, 128], BF16) for i in range(NS_)]
        vt4 = [sb(st, nc, f"p3_vt{i}", [128, 4, 128], BF16) for i in range(NS_)]
        gb = [sb(st, nc, f"p3_gb{i}", [128, 16], F32) for i in range(NS_)]
        sm = [sb(st, nc, f"p3_sm{i}", [128, 32], F32) for i in range(NS_)]
        Y4 = sb(st, nc, "p3_Y4", [128, 4, 128], F32)
        GTi = [sb(st, nc, f"p3_GTi{i}", [128, 512], F32) for i in range(NS_)]
        Gs = sb(st, nc, "p3_Gs", [128, 4, 128], F32)
        CH = F32
        Mb = [sb(st, nc, f"p3_M{i}", [128, 4, 128], CH) for i in range(2)]
        MTb = [sb(st, nc, f"p3_MT{i}", [128, 4, 128], CH) for i in range(2)]
        Xb = [sb(st, nc, f"p3_X{i}", [128, 4, 128], CH) for i in range(3)]
        Dg4 = sb(st, nc, "p3_Dg4", [128, 4, 128], F32)
        kbgT = [sb(st, nc, f"p3_kbgT{i}", [128, 4, 128], F32) for i in range(NS_)]
        vb = [sb(st, nc, f"p3_vb{i}", [128, 4, 128], F32) for i in range(NS_)]
        r4 = sb(st, nc, "p3_r4", [128, 4, 128], F32)
        kdec = [sb(st, nc, f"p3_kdec{i}", [128, 4, 128], BF16) for i in range(NS_)]
        nwT = [sb(st, nc, f"p3_nwT{i}", [128, 4, 128], BF16) for i in range(NS_)]
        qkT = [sb(st, nc, f"p3_qkT{i}", [128, 4, 128], BF16) for i in range(NS_)]
        qdT = [sb(st, nc, f"p3_qdT{i}", [128, 4, 128], F32) for i in range(NS_)]
        RT = [sb(st, nc, f"p3_RT{i}", [128, 4, 128], F32) for i in range(NS_)]
        vnew = sb(st, nc, "p3_vnew", [128, 4, 128], BF16)
        osb = [sb(st, nc, f"p3_o{i}", [128, 512], F32) for i in range(2)]
        oin = [sb(st, nc, f"p3_oin{i}", [128, 512], F32) for i in range(2)]
        b0 = pst(st, nc, "p3_b0", [128, 512], F32)
        b1 = pst(st, nc, "p3_b1", [128, 512], F32)
        b2 = pst(st, nc, "p3_b2", [128, 512], F32)
        b3 = pst(st, nc, "p3_b3", [128, 512], F32)
        pV = pst(st, nc, "p3_pV", [128, 512], F32)
        pO = pst(st, nc, "p3_pO", [128, 512], F32)
        pS = pst(st, nc, "p3_pS", [128, 512], F32)
        tb = pst(st, nc, "p3_tb", [128, 512], F32)

        P.dma("sp", LT[:], io["LT" + sfx][:, :], [], ["LT"], "c0")
        P.dma("sp", XS[:], io["XS" + sfx][:, :], [], ["XS"], "c1")
        P.dma("sp", NTI[:], io["NTI" + sfx][:, :], [], ["NTI"], "c2")
        P.dma("sp", NS[:], io["NS" + sfx][:, :], [], ["NS"], "c3")
        P.dma("sp", ident[:], io["ident"][:, :], [], ["ident"], "c4")
        P.dma("pool", identb[:], io["ident"][:, :], [], ["identb"], "c5")
        P.dma("sp", I4f[:], io["I4f"][:, :], [], ["I4f"], "c6")
        P.op("pool", lambda e: e.memset(ones[:], 1.0), [], ["ones"])
        if not fused:
            P.dma("sp", S4[:], io["st_in"][:, :], [], ["S4"], "c7")
        elif dirn == 0:
            P.op("pool", lambda e: e.memset(S4[:], 0.0), [], ["S4"])
        else:
            sga = sb(st, nc, "p3_sga", [128, 512], F32)
            sgb = sb(st, nc, "p3_sgb", [128, 512], F32)
            selt = sb(st, nc, "p3_sel", [128, 2], F32)
            P.dma("sp", sga[:], io["sg"][0:128, :], [], ["sga"], "c7")
            P.dma("sp", sgb[:], io["sg"][128:256, :], [], ["sgb"], "c9")
            P.dma("sp", selt[:], io["sel"].partition_broadcast(128), [], ["selt"], "c10")
            P.op("dve", lambda e: e.tensor_scalar(out=S4[:], in0=sga[:], scalar1=selt[:, 0:1], scalar2=None,
                                                  op0=ALU.mult), ["sga", "selt"], ["S4"])
            P.op("dve", lambda e: e.scalar_tensor_tensor(out=S4[:], in0=sgb[:], scalar=selt[:, 1:2], in1=S4[:],
                                                         op0=ALU.mult, op1=ALU.add), ["sgb", "selt", "S4"], ["S4"])

        def v3(t):
            return t[:].rearrange("p (h x) -> p h x", h=4)

        def bc(ap4):
            return ap4.unsqueeze(2).to_broadcast([128, 4, 128])

        def perhead(ps, lhs, rhs, twice=None):
            def f(e):
                ins = None
                for h in range(4):
                    ins = e.matmul(ps[:, h * 128:(h + 1) * 128], lhsT=lhs[:, h, :], rhs=rhs[:, h, :],
                                   start=True, stop=True)
                return ins
            return f

        order = list(range(NT)) if dirn == 0 else list(range(NT - 1, -1, -1))
        xcnt = [0]

        import os
        cut = int(os.environ.get("P3_CUT", "99"))

        def prep(t, si):
            K = lambda n: (n, si)
            c0 = t * 128
            P.dma("sp", qT4[si][:], io["qg"][:, c0:c0 + 128].rearrange("(h d) t -> d h t", h=4), [], [K("qT")], f"qT{si}")
            P.dma("sp", kT4[si][:], io["kg"][:, c0:c0 + 128].rearrange("(h d) t -> d h t", h=4), [], [K("kT")], f"kT{si}")
            P.dma("sp", kt4[si][:].rearrange("p h d -> p (h d)"), io["ktok"][c0:c0 + 128, :], [], [K("kt")], f"kt{si}")
            P.dma("sp", vt4[si][:].rearrange("p h d -> p (h d)"), io["vtok"][c0:c0 + 128, :], [], [K("vt")], f"vt{si}")
            P.dma("sp", gb[si][:], io["gb"][c0:c0 + 128, :], [], [K("gb")], f"gb{si}")
            g4 = gb[si][:, 4 * dirn:4 * dirn + 4]
            be4 = gb[si][:, 8 + 4 * dirn:12 + 4 * dirn]
            s_ = sm[si]
            eg, dk, gl, bg, nbeta, tmp4 = (s_[:, 0:4], s_[:, 4:8], s_[:, 8:12], s_[:, 12:16],
                                           s_[:, 16:20], s_[:, 20:24])
            def mmc(e):
                e.matmul(b0[:, 0:4], lhsT=LT[:, :], rhs=g4, start=True, stop=True)
                return e.matmul(b0[:, 4:8], lhsT=ones[:, :], rhs=g4, start=True, stop=True)
            P.op("pe", mmc, [K("gb"), "LT", "ones"], ["b0"])
            gcs = s_[:, 24:32]
            P.op("dve", lambda e: e.tensor_copy(out=gcs, in_=b0[:, 0:8]), ["b0"], [K("gcs")])
            P.op("act", lambda e: e.activation(out=eg, in_=gcs[:, 0:4], func=AF.Exp), [K("gcs")], [K("eg")])
            P.op("act", lambda e: e.activation(out=gl, in_=gcs[:, 4:8], func=AF.Exp), [K("gcs")], [K("gl")])
            P.op("dve", lambda e: e.tensor_tensor(out=tmp4, in0=gcs[:, 4:8], in1=gcs[:, 0:4], op=ALU.subtract),
                 [K("gcs")], [K("tmp4")])
            P.op("act", lambda e: e.activation(out=dk, in_=tmp4, func=AF.Exp), [K("tmp4")], [K("dk")])
            P.op("dve", lambda e: e.tensor_tensor(out=bg, in0=be4, in1=eg, op=ALU.mult), [K("gb"), K("eg")], [K("bg")])
            P.op("dve", lambda e: e.tensor_scalar(out=nbeta, in0=be4, scalar1=-1.0, scalar2=None, op0=ALU.mult),
                 [K("gb")], [K("nbeta")])
            if cut <= 1:
                return
            P.op("pool", lambda e: e.tensor_tensor(out=Y4[:], in0=LT[:].unsqueeze(1).to_broadcast([128, 4, 128]),
                                                   in1=bc(g4), op=ALU.mult), [K("gb"), "LT"], ["Y4"])
            def mm1(e):
                e.matmul(b1[:, :], lhsT=ident[:, :], rhs=NTI[:, :], start=True, stop=False)
                return e.matmul(b1[:, :], lhsT=XS[:, :], rhs=Y4[:].rearrange("p h c -> p (h c)"), start=False, stop=True)
            P.op("pe", mm1, ["Y4", "XS", "NTI", "ident"], ["b1"])
            P.op("act", lambda e: e.activation(out=GTi[si][:], in_=b1[:], func=AF.Exp), ["b1"], [K("GTi")])
            if cut <= 2:
                return
            def mm2(e):
                ins = e.matmul(b2[:, :], lhsT=ident[:, :], rhs=NS[:, :], start=True, stop=False)
                for h in range(4):
                    ins = e.matmul(b2[:, h * 128:(h + 1) * 128], lhsT=Y4[:, h, :], rhs=XS[:, :],
                                   start=False, stop=(h == 3))
                return ins
            P.op("pe", mm2, ["Y4", "XS", "NS", "ident"], ["b2"])
            P.op("act", lambda e: e.activation(out=Gs[:].rearrange("p h c -> p (h c)"), in_=b2[:], func=AF.Exp),
                 ["b2"], ["Gs"])
            P.op("pool", lambda e: e.tensor_tensor(out=Gs[:], in0=Gs[:], in1=bc(nbeta), op=ALU.mult),
                 ["Gs", K("nbeta")], ["Gs"])
            if cut <= 3:
                return
            P.op("pe", perhead(b3, kT4[si], kT4[si]), [K("kT")], ["b3"])
            P.op("dve", lambda e: e.tensor_tensor(out=Mb[0][:], in0=v3(b3), in1=Gs[:], op=ALU.mult),
                 ["b3", "Gs"], ["M0"])

            if cut <= 4:
                return

            def trM(e):
                ins = None
                for h in range(4):
                    ins = e.transpose(out=tb[:, h * 128:(h + 1) * 128], in_=Mb[0][:, h, :], identity=ident[:])
                return ins
            var = os.environ.get("P3_VAR", "")
            P.op("pe", trM, ["M0", "ident"], ["tb"])
            if var != "noact":
                P.op("act", lambda e: e.copy(out=MTb[0][:], in_=v3(tb)), ["tb"], ["MT0"])
            xi = xcnt[0] % 3
            if var != "nodve":
                P.op("dve", lambda e, xi=xi: e.tensor_tensor(out=Xb[xi][:], in0=MTb[0][:], in1=v3(I4f), op=ALU.add),
                     ["MT0", "I4f"], [("X", xi)])
            if cut <= 5:
                return
            cur = 0
            for lvl in range(1, 7):
                nxt = 1 - cur
                P.op("pe", perhead(b1, MTb[cur], Mb[cur]), [f"M{cur}", f"MT{cur}"], ["b1"])
                P.op("act", lambda e, nxt=nxt: e.copy(out=Mb[nxt][:], in_=v3(b1)), ["b1"], [f"M{nxt}"])
                if lvl < 6:
                    P.op("pe", perhead(b2, Mb[cur], MTb[cur]), [f"M{cur}", f"MT{cur}"], ["b2"])
                    P.op("dve", lambda e, nxt=nxt: e.tensor_copy(out=MTb[nxt][:], in_=v3(b2)), ["b2"], [f"MT{nxt}"])
                P.op("pe", perhead(b3, Mb[nxt], Xb[xi]), [f"M{nxt}", ("X", xi)], ["b3"])
                xn_ = (xi + 1) % 3
                last = lvl == 6
                dst = RT[si] if last else Xb[xn_]
                kd = K("RT") if last else ("X", xn_)
                P.op("dve", lambda e, dst=dst, xi=xi: e.tensor_tensor(out=dst[:], in0=v3(b3), in1=Xb[xi][:], op=ALU.add),
                     ["b3", ("X", xi)], [kd])
                xi = xn_
                cur = nxt
            xcnt[0] = xi + 1
            if cut <= 6:
                return
            P.op("pool", lambda e: e.tensor_tensor(out=vb[si][:], in0=vt4[si][:], in1=bc(be4), op=ALU.mult),
                 [K("vt"), K("gb")], [K("vb")])
            P.op("pool", lambda e: e.tensor_tensor(out=kdec[si][:], in0=kt4[si][:], in1=bc(dk), op=ALU.mult),
                 [K("kt"), K("dk")], [K("kdec")])
            P.op("pe", perhead(b2, kT4[si], qT4[si]), [K("kT"), K("qT")], ["b2"])
            P.op("dve", lambda e: e.tensor_tensor(out=qkT[si][:], in0=v3(b2), in1=v3(GTi[si]), op=ALU.mult),
                 ["b2", K("GTi")], [K("qkT")])
            P.op("pool", lambda e: e.tensor_tensor(out=Dg4[:], in0=v3(I4f), in1=bc(eg), op=ALU.mult),
                 ["I4f", K("eg")], ["Dg4"])
            P.op("pe", lambda e: e.matmul(b3[:, :], lhsT=ones[:, :], rhs=Dg4[:].rearrange("p h c -> p (h c)"),
                                          start=True, stop=True), ["Dg4", "ones"], ["b3"])
            P.op("dve", lambda e: e.tensor_tensor(out=qdT[si][:], in0=v3(b3), in1=qT4[si][:], op=ALU.mult),
                 ["b3", K("qT")], [K("qdT")])
            P.op("pool", lambda e: e.tensor_tensor(out=Dg4[:], in0=v3(I4f), in1=bc(bg), op=ALU.mult),
                 ["I4f", K("bg")], ["Dg4"])
            P.op("pe", lambda e: e.matmul(b1[:, :], lhsT=ones[:, :], rhs=Dg4[:].rearrange("p h c -> p (h c)"),
                                          start=True, stop=True), ["Dg4", "ones"], ["b1"])
            P.op("dve", lambda e: e.tensor_tensor(out=kbgT[si][:], in0=v3(b1), in1=kT4[si][:], op=ALU.mult),
                 ["b1", K("kT")], [K("kbgT")])
            if dirn == 1:
                P.dma("sp", oin[si][:], io["oa"][c0:c0 + 128, :], [], [K("oin")], f"oin{si}")

        def step(t, si):
            K = lambda n: (n, si)
            c0 = t * 128
            gl = sm[si][:, 8:12]

            def mmR(e):
                ins = None
                for h in range(4):
                    ins = e.matmul(pV[:, h * 128:(h + 1) * 128], lhsT=kbgT[si][:, h, :],
                                   rhs=S4[:, h * 128:(h + 1) * 128], start=True, stop=True)
                return ins
            P.op("pe", mmR, [K("kbgT"), "S4"], ["pV"])
            P.op("dve", lambda e: e.tensor_tensor(out=r4[:], in0=vb[si][:], in1=v3(pV), op=ALU.subtract),
                 ["pV", K("vb")], ["r4"])
            P.op("pe", perhead(pV, RT[si], r4), [K("RT"), "r4"], ["pV"])
            P.op("act", lambda e: e.copy(out=vnew[:], in_=v3(pV)), ["pV"], ["vnew"])

            def mmO(e):
                ins = None
                for h in range(4):
                    e.matmul(pO[:, h * 128:(h + 1) * 128], lhsT=qdT[si][:, h, :],
                             rhs=S4[:, h * 128:(h + 1) * 128], start=True, stop=False)
                    ins = e.matmul(pO[:, h * 128:(h + 1) * 128], lhsT=qkT[si][:, h, :], rhs=vnew[:, h, :],
                                   start=False, stop=True)
                return ins
            P.op("pe", mmO, [K("qdT"), K("qkT"), "vnew", "S4"], ["pO"])

            def mmS(e):
                ins = None
                for h in range(4):
                    ins = e.matmul(pS[:, h * 128:(h + 1) * 128], lhsT=kdec[si][:, h, :], rhs=vnew[:, h, :],
                                   start=True, stop=True)
                return ins
            P.op("pe", mmS, [K("kdec"), "vnew"], ["pS"])
            P.op("pool", lambda e: e.tensor_tensor(out=v3(S4), in0=v3(S4), in1=bc(gl), op=ALU.mult),
                 ["S4", K("gl")], ["S4"])
            P.op("dve", lambda e: e.tensor_tensor(out=S4[:], in0=S4[:], in1=pS[:], op=ALU.add),
                 ["S4", "pS"], ["S4"])
            oi = t % 2
            if dirn == 0:
                P.op("act", lambda e: e.copy(out=osb[oi][:], in_=pO[:]), ["pO"], [("osb", oi)])
                P.dma("sp", io["oa"][c0:c0 + 128, :], osb[oi][:], [("osb", oi)], [("oa", t)], f"osb{oi}")
            else:
                P.op("dve", lambda e: e.tensor_tensor(out=osb[oi][:], in0=pO[:], in1=oin[si][:], op=ALU.add),
                     ["pO", K("oin")], [("osb", oi)])
                P.dma("sp", io["ob"][c0:c0 + 128, :], osb[oi][:], [("osb", oi)], [("ob", t)], f"osb{oi}")

        import os
        ntl = int(os.environ.get("P3_NT", NT))
        nostep = bool(int(os.environ.get("P3_NOSTEP", "0")))
        order = order[:ntl]
        prep(order[0], 0)
        for i, t in enumerate(order):
            if i + 1 < len(order):
                prep(order[i + 1], (i + 1) % 2)
            if not nostep:
                step(t, i % 2)
        P.dma("sp", io["st_out"][:, :], S4[:], ["S4"], ["st_out"], "c8")
        if fused and dirn == 0:
            P.op("pool", lambda e: e.collective_compute("AllGather", ALU.bypass, replica_groups=PAIRS,
                                                        ins=[io["st_out"][:, :]], outs=[io["sg"][:, :]]),
                 ["st_out"], ["sg"])
            P.op("sp", None, reads=["sg"])
        P.op("sp", None, reads=["st_out"] + [(("oa" if dirn == 0 else "ob"), t) for t in order if not nostep])
        P.emit(tag)


def phase3c(nc, io):
    with ExitStack() as st:
        P = Prog(nc)
        gg = sb(st, nc, "p3c_gg", [128, 128], F32)
        ob = [sb(st, nc, f"p3c_ob{i}", [128, 4, 128], F32) for i in range(2)]
        zz = [sb(st, nc, f"p3c_zz{i}", [128, 4, 128], F32) for i in range(2)]
        sq = sb(st, nc, "p3c_sq", [128, 4, 128], F32)
        ss = sb(st, nc, "p3c_ss", [128, 4], F32)
        P.dma("sp", gg[:], io["gdn_g"].partition_broadcast(128), [], ["gg"], "c0")
        for t in range(NT):
            i = t % 2
            c0 = t * 128
            P.dma("sp", ob[i][:].rearrange("p h d -> p (h d)"), io["ob"][c0:c0 + 128, :], [], [("ob", i)], f"ob{i}")
            P.dma("sp", zz[i][:].rearrange("p h d -> p (h d)"), io["zs"][c0:c0 + 128, :], [], [("zz", i)], f"zz{i}")
            P.op("pool", lambda e, i=i: e.tensor_tensor(out=sq[:], in0=ob[i][:], in1=ob[i][:], op=ALU.mult),
                 [("ob", i)], ["sq"])
            P.op("dve", lambda e: e.reduce_sum(out=ss[:], in_=sq[:], axis=AX.X), ["sq"], ["ss"])
            P.op("act", lambda e: e.activation(out=ss[:], in_=ss[:], func=AF.Ln, bias=RMS_EPS, scale=1.0 / 128),
                 ["ss"], ["ss1"])
            P.op("act", lambda e: e.activation(out=ss[:], in_=ss[:], func=AF.Exp, scale=-0.5), ["ss1"], ["ss2"])
            P.op("act", lambda e, i=i: e.activation(out=zz[i][:], in_=zz[i][:], func=AF.Silu), [("zz", i)], [("zz", i)])
            P.op("dve", lambda e, i=i: e.tensor_tensor(
                out=ob[i][:], in0=ob[i][:], in1=ss[:].unsqueeze(2).to_broadcast([128, 4, 128]), op=ALU.mult),
                [("ob", i), "ss2"], [("ob", i)])
            P.op("pool", lambda e, i=i: e.tensor_tensor(
                out=ob[i][:], in0=ob[i][:], in1=gg[:].unsqueeze(1).to_broadcast([128, 4, 128]), op=ALU.mult),
                [("ob", i), "gg"], [("ob", i)])
            P.op("dve", lambda e, i=i: e.tensor_tensor(out=ob[i][:], in0=ob[i][:], in1=zz[i][:], op=ALU.mult),
                 [("ob", i), ("zz", i)], [("ob", i)])
            P.dma("sp", io["og"][c0:c0 + 128, :], ob[i][:].rearrange("p h d -> p (h d)"), [("ob", i)],
                  [("og", t)], f"ob{i}")
        P.op("sp", None, reads=[("og", t) for t in range(NT)])
        P.emit("p3c")


def phase4(nc, io):
    with ExitStack() as st:
        P = Prog(nc)
        bias = [sb(st, nc, f"p4_bias{i}", [128, 8, 5, 128], F32) for i in range(2)]
        kwin = [sb(st, nc, f"p4_kwin{i}", [128, 4, 640], BF16) for i in range(2)]
        qw = [sb(st, nc, f"p4_qw{i}", [128, 4, 128], BF16) for i in range(2)]
        vwin = [sb(st, nc, f"p4_vwin{i}", [128, 5, 8, 65], BF16) for i in range(2)]
        tmp = [sb(st, nc, f"p4_tmp{i}", [128, 512], F32) for i in range(2)]
        ET = [sb(st, nc, f"p4_ET{i}", [128, 5, 8, 128], BF16) for i in range(2)]
        gna = sb(st, nc, "p4_gna", [128, 64], F32)
        rec = sb(st, nc, "p4_rec", [128, 8], F32)
        on = [sb(st, nc, f"p4_on{i}", [128, 8, 64], F32) for i in range(2)]
        sq = sb(st, nc, "p4_sq", [128, 8, 64], F32)
        ss = sb(st, nc, "p4_ss", [128, 8], F32)
        psS = [pst(st, nc, f"p4_psS{i}", [128, 512], F32) for i in range(4)]
        psO = [pst(st, nc, f"p4_psO{i}", [128, 512], F32) for i in range(4)]

        P.dma("sp", gna[:], io["na_g"].partition_broadcast(128), [], ["gna"], "c0")
        for i in range(2):
            P.op("pool", lambda e, i=i: e.memset(vwin[i][:], 1.0), [], [("vw", i, j) for j in range(5)])
        cnt = {"psS": 0, "tmp": 0}
        for m in range(NT):
            bi = m % 2
            ks = min(max(m - 2, 0), 29)
            var = m if m < 2 else 2
            bsl = 0 if var == 0 else (1 if var == 1 else 0)
            if m <= 2:
                P.dma("sp", bias[bsl][:].rearrange("p h j q -> p (h j q)"), io["nab"][var, :, :],
                      [], [("bias", bsl)], f"bias{bsl}")
            kw, qq, vw, et = kwin[bi], qw[bi], vwin[bi], ET[bi]
            P.dma("sp", kw[:], io["kn"][:, ks * 128:ks * 128 + 640].rearrange("(c p) t -> p c t", p=128),
                  [], [("kw", bi)], f"kw{bi}")
            P.dma("sp", qq[:], io["qn"][:, m * 128:(m + 1) * 128].rearrange("(c p) t -> p c t", p=128),
                  [], [("qw", bi)], f"qw{bi}")
            for j in range(5):
                P.dma("sp", vw[:, j, :, 0:64],
                      io["vn"][(ks + j) * 128:(ks + j + 1) * 128, :].rearrange("t (h d) -> t h d", h=8),
                      [], [("vw", bi, j)], f"vw{bi}_{j}")
            for hg in range(2):
                for j in range(5):
                    pi = cnt["psS"] % 4
                    cnt["psS"] += 1
                    pS = psS[pi]

                    def mm(e, hg=hg, j=j, pS=pS, kw=kw, qq=qq):
                        ins = None
                        for hh in range(4):
                            h = 2 * hh + hg
                            p0 = (h % 2) * 64
                            ins = e.matmul(pS[:, hh * 128:(hh + 1) * 128],
                                           lhsT=kw[p0:p0 + 64, h // 2, j * 128:(j + 1) * 128],
                                           rhs=qq[p0:p0 + 64, h // 2, :], start=True, stop=True)
                        return ins
                    P.op("pe", mm, [("kw", bi), ("qw", bi)], [("psS", pi)])
                    ti = cnt["tmp"] % 2
                    cnt["tmp"] += 1
                    t_ = tmp[ti]
                    P.op("dve", lambda e, t_=t_, pS=pS, hg=hg, j=j, bsl=bsl: e.tensor_tensor(
                        out=t_[:].rearrange("p (h q) -> p h q", h=4),
                        in0=pS[:].rearrange("p (h q) -> p h q", h=4),
                        in1=bias[bsl][:, hg * 4:(hg + 1) * 4, j, :], op=ALU.add),
                        [("psS", pi), ("bias", bsl)], [("tmp", ti)])
                    P.op("act", lambda e, t_=t_, et=et, hg=hg, j=j: e.activation(
                        out=et[:, j, hg * 4:(hg + 1) * 4, :],
                        in_=t_[:].rearrange("p (h q) -> p h q", h=4), func=AF.Exp),
                        [("tmp", ti)], [("ET", bi, hg, j)])
            o_ = on[bi]
            for hg in range(2):
                pO = psO[(m % 2) * 2 + hg]
                kO = ("psO", (m % 2) * 2 + hg)

                def pv(e, hg=hg, pO=pO, et=et, vw=vw):
                    ins = None
                    for hh in range(4):
                        h = hg * 4 + hh
                        slot = (h % 2) * 4 + h // 2
                        for j in range(5):
                            ins = e.matmul(pO[:, hh * 65:(hh + 1) * 65], lhsT=et[:, j, slot, :],
                                           rhs=vw[:, j, h, :], start=(j == 0), stop=(j == 4))
                    return ins
                P.op("pe", pv, [("ET", bi, g_, j) for g_ in range(2) for j in range(5)] +
                     [("vw", bi, j) for j in range(5)], [kO])
                pv3 = pO[:, 0:260].rearrange("p (h d) -> p h d", h=4)
                P.op("dve", lambda e, pv3=pv3, hg=hg: e.reciprocal(
                    out=rec[:, hg * 4:(hg + 1) * 4], in_=pv3[:, :, 64]), [kO], [("rec", hg)])
                P.op("dve", lambda e, pv3=pv3, hg=hg, o_=o_: e.tensor_tensor(
                    out=o_[:, hg * 4:(hg + 1) * 4, :], in0=pv3[:, :, 0:64],
                    in1=rec[:, hg * 4:(hg + 1) * 4].unsqueeze(2).to_broadcast([128, 4, 64]), op=ALU.mult),
                    [kO, ("rec", hg)], [("on", bi, hg)])
            kon = [("on", bi, 0), ("on", bi, 1)]
            P.op("pool", lambda e, o_=o_: e.tensor_tensor(out=sq[:], in0=o_[:], in1=o_[:], op=ALU.mult),
                 kon, ["sq"])
            P.op("dve", lambda e: e.reduce_sum(out=ss[:], in_=sq[:], axis=AX.X), ["sq"], ["ss"])
            P.op("act", lambda e: e.activation(out=ss[:], in_=ss[:], func=AF.Ln, bias=RMS_EPS, scale=1.0 / 64),
                 ["ss"], ["ss1"])
            P.op("act", lambda e: e.activation(out=ss[:], in_=ss[:], func=AF.Exp, scale=-0.5),
                 ["ss1"], ["ss2"])
            P.op("dve", lambda e, o_=o_: e.tensor_tensor(
                out=o_[:], in0=o_[:], in1=ss[:].unsqueeze(2).to_broadcast([128, 8, 64]), op=ALU.mult),
                kon + ["ss2"], kon)
            P.op("pool", lambda e, o_=o_: e.tensor_tensor(
                out=o_[:], in0=o_[:], in1=gna[:].unsqueeze(1).to_broadcast([128, 8, 64]), op=ALU.mult),
                kon + ["gna"], kon)
            P.dma("sp", io["ona"][m * 128:(m + 1) * 128, :], o_[:].rearrange("p h d -> p (h d)"),
                  kon, [("ona", m)], f"on{bi}")
        P.op("sp", None, reads=[("ona", m) for m in range(NT)])
        P.emit("p4")


def ln_rows(P, eng_aff, src, dst, stats, mv, rstd, gvec, bvec, ksrc, kdst, tag):
    for hh in range(2):
        P.op("dve", lambda e, hh=hh: e.bn_stats(out=stats[:, hh, :],
                                                 in_=src[:, hh * 512:(hh + 1) * 512]),
             [ksrc], [(tag, "st", hh)])
    P.op("dve", lambda e: e.bn_aggr(out=mv[:], in_=stats[:].rearrange("p a b -> p (a b)")),
         [(tag, "st", 0), (tag, "st", 1)], [(tag, "mv")])
    P.op("act", lambda e: e.activation(out=rstd[:], in_=mv[:, 1:2], func=AF.Ln, bias=LN_EPS),
         [(tag, "mv")], [(tag, "rs0")])
    P.op("act", lambda e: e.activation(out=rstd[:], in_=rstd[:], func=AF.Exp, scale=-0.5),
         [(tag, "rs0")], [(tag, "rs")])
    P.op("dve", lambda e: e.tensor_scalar(out=dst[:], in0=src[:], scalar1=mv[:, 0:1],
                                          scalar2=rstd[:, 0:1], op0=ALU.subtract, op1=ALU.mult),
         [ksrc, (tag, "mv"), (tag, "rs")], [kdst])
    P.op(eng_aff, lambda e: e.tensor_tensor(out=dst[:], in0=dst[:], in1=gvec[:], op=ALU.mult),
         [kdst, "v0", "v1", "v2", "v3"], [kdst])
    P.op(eng_aff, lambda e: e.tensor_tensor(out=dst[:], in0=dst[:], in1=bvec[:], op=ALU.add),
         [kdst, "v0", "v1", "v2", "v3"], [kdst])


def phase5(nc, io, resname="xn", outname="xo"):
    MT = 256
    NMT = T // MT
    with ExitStack() as st:
        P = Prog(nc)
        w1 = sb(st, nc, "p5_w1", [128, 8, DFF], BF16)
        w2 = sb(st, nc, "p5_w2", [128, 32, D], BF16)
        wo = sb(st, nc, "p5_wo", [128, 8, D], BF16)
        g1 = sb(st, nc, "p5_g1", [128, D], F32)
        bb1 = sb(st, nc, "p5_bb1", [128, D], F32)
        g2 = sb(st, nc, "p5_g2", [128, D], F32)
        bb2 = sb(st, nc, "p5_bb2", [128, D], F32)
        b1t = sb(st, nc, "p5_b1t", [128, 32], F32)
        b2r = sb(st, nc, "p5_b2r", [1, D], BF16)
        ones1 = sb(st, nc, "p5_ones1", [1, 128], BF16)
        ident = sb(st, nc, "p5_ident", [128, 128], F32)
        hT = sb(st, nc, "p5_hT", [128, 32, MT], BF16)
        xs = [sb(st, nc, f"p5_xs{i}", [128, D], F32) for i in range(4)]
        mixin = [sb(st, nc, f"p5_mi{i}", [128, D], F32) for i in range(1)]
        mixT = [sb(st, nc, f"p5_mT{i}", [128, 8, 128], BF16) for i in range(1)]
        x1T = sb(st, nc, "p5_x1T", [128, 8, MT], BF16)
        rl = [sb(st, nc, f"p5_rl{i}", [128, MT], F32) for i in range(2)]
        stats = [sb(st, nc, f"p5_stats{i}", [128, 2, 6], F32) for i in range(2)]
        mv = [sb(st, nc, f"p5_mv{i}", [128, 2], F32) for i in range(2)]
        rstd = [sb(st, nc, f"p5_rstd{i}", [128, 1], F32) for i in range(2)]
        psT = [pst(st, nc, f"p5_psT{i}", [128, 512], F32) for i in range(2)]
        psM = [pst(st, nc, f"p5_psM{i}", [128, 512], F32) for i in range(2)]
        psF = [pst(st, nc, f"p5_psF{i}", [128, 512], F32) for i in range(4)]

        P.dma("sp", ident[:], io["ident"][:, :], [], ["ident"], "c0")
        P.dma_multi("pool", [(wo[:, k, :], io["w_out"][k * 128:(k + 1) * 128, :]) for k in range(8)],
                    [("wo", k) for k in range(8)], "wo")
        P.dma("sp", g1[:], io["ln1_g"].partition_broadcast(128), [], ["v0"], "c1")
        P.dma("sp", bb1[:], io["ln1_b"].partition_broadcast(128), [], ["v1"], "c2")
        P.dma("sp", g2[:], io["ln2_g"].partition_broadcast(128), [], ["v2"], "c3")
        P.dma("sp", bb2[:], io["ln2_b"].partition_broadcast(128), [], ["v3"], "c4")
        P.dma("sp", b1t[:], io["b1t"][:, :], [], ["b1t"], "c5")
        P.dma("pool", b2r[:], io["b2"].rearrange("(o d) -> o d", o=1), [], ["b2r"], "c6")
        P.op("pool", lambda e: e.memset(ones1[:], 1.0), [], ["ones1"])
        P.dma_multi("pool", [(w1[:, k, :], io["w1"][k * 128:(k + 1) * 128, :]) for k in range(8)],
                    [("w1", k) for k in range(8)], "w1")
        for q4 in range(4):
            P.dma_multi("pool", [(w2[:, k, :], io["w2"][k * 128:(k + 1) * 128, :]) for k in range(q4 * 8, q4 * 8 + 8)],
                        [("w2", k) for k in range(q4 * 8, q4 * 8 + 8)], f"w2{q4}")
        wok = [("wo", k) for k in range(8)]
        w1k = [("w1", k) for k in range(8)]
        w2k = [("w2", k) for k in range(32)]
        cnt = {"psF": 0, "rl": 0}

        def transposes(src, ksrc, dstT, kdst, col0):
            for hb in range(2):
                pT = psT[hb]

                def tr(e, hb=hb, pT=pT):
                    ins = None
                    for kk in range(4):
                        k = hb * 4 + kk
                        ins = e.transpose(out=pT[:, kk * 128:(kk + 1) * 128],
                                          in_=src[:, k * 128:(k + 1) * 128], identity=ident[:])
                    return ins
                P.op("pe", tr, (ksrc if isinstance(ksrc, list) else [ksrc]) + ["ident"], [("psT", hb)])
                o = dstT[:, hb * 4:(hb + 1) * 4, col0:col0 + 128]
                i = pT[:].rearrange("p (k t) -> p k t", k=4)
                if hb == 0:
                    P.op("act", lambda e, o=o, i=i: e.copy(out=o, in_=i), [("psT", hb)], [(kdst, hb)])
                else:
                    P.op("dve", lambda e, o=o, i=i: e.tensor_copy(out=o, in_=i), [("psT", hb)], [(kdst, hb)])

        def stage_a1(mt):
            for sub in range(2):
                t0 = mt * MT + sub * 128
                si = (mt % 2) * 2 + sub
                mi = 0
                x_ = xs[si]
                m_ = mixin[mi]
                P.dma("sp", x_[:], io[resname][t0:t0 + 128, :], [], [("xs", si)], f"xs{si}")
                P.dma("sp", m_[:, 0:512], io["og"][t0:t0 + 128, :], [], [("mi", mi, 0)], f"mi{mi}a")
                P.dma("sp", m_[:, 512:1024], io["ona"][t0:t0 + 128, :], [], [("mi", mi, 1)], f"mi{mi}b")
                transposes(m_, [("mi", mi, 0), ("mi", mi, 1)], mixT[mi], ("mT", mi), 0)
                for half in range(2):
                    pM = psM[half]

                    def mm(e, half=half, pM=pM, mi=mi):
                        ins = None
                        for k in range(8):
                            ins = e.matmul(pM[:, :], lhsT=mixT[mi][:, k, :],
                                           rhs=wo[:, k, half * 512:(half + 1) * 512],
                                           start=(k == 0), stop=(k == 7))
                        return ins
                    P.op("pe", mm, [(("mT", mi), 0), (("mT", mi), 1)] + wok, [("psM", half)])
                    P.op("dve", lambda e, half=half, pM=pM, x_=x_: e.scalar_tensor_tensor(
                        out=x_[:, half * 512:(half + 1) * 512], in0=x_[:, half * 512:(half + 1) * 512],
                        scalar=ALPHA, in1=pM[:, :], op0=ALU.mult, op1=ALU.add),
                        [("xs", si), ("psM", half)], [("xs", si)])
                ln_rows(P, "pool", x_, x_, stats[0], mv[0], rstd[0], g1, bb1, ("xs", si), ("xs", si), "ln1")

        def stage_a2(mt):
            for sub in range(2):
                si = (mt % 2) * 2 + sub
                transposes(xs[si], ("xs", si), x1T, ("x1T", sub), sub * 128)

        def stage_b(mt):
            xk = [(("x1T", s_), h_) for s_ in range(2) for h_ in range(2)]
            for fc in range(32):
                pi = cnt["psF"] % 4
                cnt["psF"] += 1
                pF = psF[pi]

                def mm(e, fc=fc, pF=pF):
                    ins = None
                    for k in range(8):
                        ins = e.matmul(pF[:, 0:MT], lhsT=w1[:, k, fc * 128:(fc + 1) * 128],
                                       rhs=x1T[:, k, :], start=(k == 0), stop=(k == 7))
                    return ins
                P.op("pe", mm, xk + w1k, [("psF", pi)])
                ri = cnt["rl"] % 2
                cnt["rl"] += 1
                r_ = rl[ri]
                P.op("act", lambda e, r_=r_, pF=pF, fc=fc: e.activation(
                    out=r_[:], in_=pF[:, 0:MT], func=AF.Relu, bias=b1t[:, fc:fc + 1]),
                    [("psF", pi), "b1t"], [("rl", ri)])
                eng = "pool" if fc % 2 == 0 else "dve"
                P.op(eng, lambda e, r_=r_, fc=fc: e.tensor_tensor(
                    out=hT[:, fc, :], in0=r_[:], in1=r_[:], op=ALU.mult),
                    [("rl", ri)], [("hT", fc)])

        def stage_c(mt):
            hk = [("hT", fc) for fc in range(32)]
            for sub in range(2):
                t0 = mt * MT + sub * 128
                si = (mt % 2) * 2 + sub
                x_ = xs[si]
                for half in range(2):
                    pM = psM[half]

                    def mm(e, half=half, pM=pM, sub=sub):
                        ins = None
                        for fc in range(32):
                            ins = e.matmul(pM[:, :], lhsT=hT[:, fc, sub * 128:(sub + 1) * 128],
                                           rhs=w2[:, fc, half * 512:(half + 1) * 512],
                                           start=(fc == 0), stop=False)
                        ins = e.matmul(pM[:, :], lhsT=ones1[:, :], rhs=b2r[:, half * 512:(half + 1) * 512],
                                       start=False, stop=True)
                        return ins
                    P.op("pe", mm, hk + w2k + ["ones1", "b2r"], [("psM", half)])
                    P.op("dve", lambda e, half=half, pM=pM, x_=x_: e.scalar_tensor_tensor(
                        out=x_[:, half * 512:(half + 1) * 512], in0=x_[:, half * 512:(half + 1) * 512],
                        scalar=ALPHA, in1=pM[:, :], op0=ALU.mult, op1=ALU.add),
                        [("xs", si), ("psM", half)], [("xs", si)])
                ln_rows(P, "pool", x_, x_, stats[1], mv[1], rstd[1], g2, bb2, ("xs", si), ("xs", si), "ln2")
                P.dma("sp", io[outname][t0:t0 + 128, :], x_[:], [("xs", si)], [("xo", t0 // 128)], f"xs{si}")

        stage_a1(0)
        stage_a2(0)
        for mt in range(NMT):
            stage_b(mt)
            if mt + 1 < NMT:
                stage_a1(mt + 1)
            stage_c(mt)
            if mt + 1 < NMT:
                stage_a2(mt + 1)
        P.op("sp", None, reads=[("xo", i) for i in range(NT)])
        P.emit("p5")


PAIRS = [[0, 1], [2, 3], [4, 5], [6, 7]]


def phase_halo(nc, io):
    with ExitStack() as st:
        P = Prog(nc)
        jm = sb(st, nc, "px_j", [128, 128], F32)
        selt = sb(st, nc, "px_sel", [128, 2], F32)
        pa = [sb(st, nc, f"px_a{i}", [128, D], F32) for i in range(2)]
        pb = [sb(st, nc, f"px_b{i}", [128, D], F32) for i in range(2)]
        ro = [sb(st, nc, f"px_r{i}", [128, D], F32) for i in range(2)]
        ps = [pst(st, nc, f"px_ps{i}", [128, 512], F32) for i in range(4)]
        P.dma("sp", jm[:], io["jmat"][:, :], [], ["jm"], "c0")
        P.dma("sp", selt[:], io["sel"].partition_broadcast(128), [], ["selt"], "c1")
        P.op("pool", lambda e: e.collective_compute("AllGather", ALU.bypass, replica_groups=PAIRS,
                                                    ins=[io["x1"][T - 256:T, :]], outs=[io["hg"][:, :]]),
             [], ["hg"])
        for i in range(2):
            P.dma("sp", pa[i][:], io["hg"][i * 128:(i + 1) * 128, :], ["hg"], [("pa", i)], f"pa{i}")
            P.dma("sp", pb[i][:], io["hg"][256 + i * 128:256 + (i + 1) * 128, :], ["hg"], [("pb", i)], f"pb{i}")
            P.op("dve", lambda e, i=i: e.tensor_scalar(out=pa[i][:], in0=pa[i][:], scalar1=selt[:, 0:1],
                                                       scalar2=None, op0=ALU.mult), [("pa", i), "selt"], [("pa", i)])
            P.op("dve", lambda e, i=i: e.scalar_tensor_tensor(out=pa[i][:], in0=pb[i][:], scalar=selt[:, 1:2],
                                                              in1=pa[i][:], op0=ALU.mult, op1=ALU.add),
                 [("pa", i), ("pb", i), "selt"], [("pa", i)])
            for hh in range(2):
                p_ = ps[i * 2 + hh]
                P.op("pe", lambda e, i=i, hh=hh, p_=p_: e.matmul(p_[:, :], lhsT=jm[:, :],
                                                                  rhs=pa[i][:, hh * 512:(hh + 1) * 512],
                                                                  start=True, stop=True),
                     [("pa", i), "jm"], [("ps", i, hh)])
                P.op("act", lambda e, i=i, hh=hh, p_=p_: e.copy(out=ro[i][:, hh * 512:(hh + 1) * 512], in_=p_[:, :]),
                     [("ps", i, hh)], [("ro", i, hh)])
            dst0 = T + (1 - i) * 128
            P.dma("sp", io["x1"][dst0:dst0 + 128, :], ro[i][:], [("ro", i, 0), ("ro", i, 1)], [("halo", i)], f"ro{i}")
        P.op("sp", None, reads=[("halo", 0), ("halo", 1)])
        P.emit("px")


SCRATCH = {
    "xn": ([T, D], F32), "hq": ([1536, 4100], F32), "qn": ([512, T], BF16),
    "kn": ([512, TH], BF16), "vn": ([TH, 512], BF16), "zs": ([T, 512], F32),
    "gb": ([T, 16], F32), "og": ([T, 512], F32), "ona": ([T, 512], F32),
    "xo": ([T, D], F32), "qg": ([512, T], BF16), "kg": ([512, T], BF16),
    "ktok": ([T, 512], BF16), "vtok": ([T, 512], BF16), "oa": ([T, 512], F32),
    "ob": ([T, 512], F32), "st_out": ([128, 512], F32), "sg": ([256, 512], F32),
    "x1": ([TH, D], F32), "hg": ([512, D], F32),
}
NPDT = {F32: np.float32}


def build(phases, in_specs, out_names, layer0=True):
    nc = bass.Bass("TRN2", target_bir_lowering=False)
    io = {}
    for name, (shape, dt) in in_specs.items():
        io[name] = nc.dram_tensor(name, list(shape), dt, kind="ExternalInput").ap()
    for name, (shape, dt) in SCRATCH.items():
        if name in io:
            continue
        kind = "ExternalOutput" if name in out_names else "Internal"
        io[name] = nc.dram_tensor(name, list(shape), dt, kind=kind).ap()
    for ph in phases:
        if ph == "p1":
            phase1(nc, io, layer0)
        if ph == "p5":
            phase5(nc, io)
        if ph == "p4":
            phase4(nc, io)
        if ph == "p3a":
            phase3(nc, io, 0, "p3a")
        if ph == "p3b":
            phase3(nc, io, 1, "p3b")
        if ph == "p3c":
            phase3c(nc, io)
        if ph == "p2":
            phase2(nc, io, range(0, 4), "p2a")
            phase2(nc, io, range(4, 8), "p2b")
    return nc


def core_bs(c):
    return c // 2, c % 2


def host_layer_inputs(inp, l, c, x_full_loc):
    b, s = core_bs(c)
    w_in = np.array(inp["w_in"][l], dtype=np.float32, copy=True)
    a_log = np.asarray(inp["a_log"][l], np.float32)
    dt_b = np.asarray(inp["dt_bias"][l], np.float32)
    convw = np.asarray(inp["conv_w"][l], np.float32)
    if s == 1:
        a = w_in[:, 2048:2056].copy()
        bb = w_in[:, 2056:2064].copy()
        w_in[:, 2048:2052] = a[:, 4:8]
        w_in[:, 2052:2056] = a[:, 0:4]
        w_in[:, 2056:2060] = bb[:, 4:8]
        w_in[:, 2060:2064] = bb[:, 0:4]
        a_log = a_log[::-1]
        dt_b = dt_b[::-1]
        convw = convw[::-1]
    d = {
        "x": None if x_full_loc is None else np.ascontiguousarray(x_full_loc),
        "w_in": w_in,
        "gpar": np.ascontiguousarray(np.concatenate([a_log.reshape(-1), dt_b.reshape(-1)])),
        "convw": np.ascontiguousarray(convw.T),
        "ident": np.eye(128, dtype=np.float32),
    }
    return d


def na_bias_tables(rpb_l, s):
    out = np.empty((3, 128, 8, 5, 128), np.float32)
    for vi, m in enumerate((0, 1, 5)):
        ks = min(max(m - 2, 0), 29)
        qi = m * 128 + np.arange(128)
        ki = ks * 128 + np.arange(640)
        tq = qi if s == 0 else 8191 - qi
        tk = ki if s == 0 else 8191 - ki
        rq, cq = tq // 64, tq % 64
        rk, ck = tk // 64, tk % 64
        r0 = np.clip(rq - 4, 0, 120)
        w0 = np.clip(cq - 8, 0, 48)
        valid = ((rk[:, None] >= r0[None, :]) & (rk[:, None] < r0[None, :] + 8) &
                 (ck[:, None] >= w0[None, :]) & (ck[:, None] < w0[None, :] + 16))
        dr = np.clip(rk[:, None] - rq[None, :] + 7, 0, 14)
        dc = np.clip(ck[:, None] - cq[None, :] + 15, 0, 30)
        b = rpb_l[:, dr, dc]
        b = np.where(valid[None], b, np.float32(NEG)).astype(np.float32)
        b = b[[0, 2, 4, 6, 1, 3, 5, 7]]
        out[vi] = b.reshape(8, 5, 128, 128).transpose(2, 0, 1, 3)
    return np.ascontiguousarray(out.reshape(3, 128, 8 * 5 * 128))


def gdn_consts():
    i = np.arange(128)
    lt = (i[:, None] <= i[None, :]).astype(np.float32)
    xs = (i[:, None] > i[None, :]).astype(np.float32)
    nti = np.where(i[None, :] < i[:, None], np.float32(NEG), np.float32(0))
    ns = np.where(i[:, None] <= i[None, :], np.float32(NEG), np.float32(0))
    d = {"LTA": lt, "XSA": xs, "NTIA": np.tile(nti, (1, 4)), "NSA": np.tile(ns, (1, 4)),
         "LTB": lt.T, "XSB": xs.T, "NTIB": np.tile(nti.T, (1, 4)), "NSB": np.tile(ns.T, (1, 4)),
         "I4f": np.tile(np.eye(128, dtype=np.float32), (1, 4))}
    return {k: np.ascontiguousarray(v, dtype=np.float32) for k, v in d.items()}


_BF = ml_dtypes.bfloat16

L1_OUT = ["xn", "zs", "gb", "qg", "kg", "ktok", "vtok", "oa", "ona", "st_out"]


def _specs(maps):
    sp = {}
    for k, v in maps[0].items():
        sp[k] = (list(v.shape), BF16 if v.dtype == _BF else F32)
    return sp


def _run(phases, maps, outs, layer0=True):
    nc = build(phases, _specs(maps), outs, layer0=layer0)
    res = run_bass_kernel_spmd(nc, maps, core_ids=list(range(8)))
    return res.results


def kernel_unfused(x, ln_in_g, ln_in_b, w_in, conv_w, a_log, dt_bias, gdn_norm_g, rpb, na_norm_g,
                   w_out, ln1_g, ln1_b, w1, b1, w2, b2, ln2_g, ln2_b):
    inp = dict(x=x, w_in=w_in, conv_w=conv_w, a_log=a_log, dt_bias=dt_bias)
    x = np.asarray(x, np.float32)
    consts = gdn_consts()
    ident = np.eye(128, dtype=np.float32)
    zeros_st = np.zeros((128, 512), np.float32)
    xloc = []
    for c in range(8):
        b, s = core_bs(c)
        xs = x[b] if s == 0 else x[b][::-1]
        xloc.append(np.ascontiguousarray(xs[:TH]))
    for l in range(2):
        maps = []
        for c in range(8):
            b, s = core_bs(c)
            m = host_layer_inputs(inp, l, c, xloc[c])
            if l == 0:
                m["ln_in_g"] = np.asarray(ln_in_g, np.float32)
                m["ln_in_b"] = np.asarray(ln_in_b, np.float32)
            m["nab"] = na_bias_tables(np.asarray(rpb[l], np.float32), s)
            m["na_g"] = np.asarray(na_norm_g[l], np.float32)
            for k in ("LTA", "XSA", "NTIA", "NSA", "I4f"):
                m[k] = consts[k]
            m["st_in"] = zeros_st
            maps.append(m)
        outs1 = [o for o in L1_OUT if not (o == "xn" and l > 0)]
        r1 = _run(["p1", "p2", "p4", "p3a"], maps, outs1, layer0=(l == 0))
        maps2 = []
        for c in range(8):
            r = r1[c]
            m = {k: np.ascontiguousarray(r[k]) for k in ("qg", "kg", "ktok", "vtok", "gb", "oa", "zs", "ona")}
            m["xn"] = np.ascontiguousarray(r["xn"]) if l == 0 else np.ascontiguousarray(xloc[c][:T])
            m["st_in"] = np.ascontiguousarray(r1[c ^ 1]["st_out"])
            for k in ("LTB", "XSB", "NTIB", "NSB", "I4f"):
                m[k] = consts[k]
            m["ident"] = ident
            m["gdn_g"] = np.asarray(gdn_norm_g[l], np.float32)
            m["w_out"] = np.asarray(w_out[l], np.float32)
            m["w1"] = np.asarray(w1[l], np.float32)
            m["w2"] = np.asarray(w2[l], np.float32)
            m["b1t"] = np.ascontiguousarray(np.asarray(b1[l], np.float32).reshape(32, 128).T)
            m["b2"] = np.asarray(b2[l], np.float32)
            m["ln1_g"] = np.asarray(ln1_g[l], np.float32)
            m["ln1_b"] = np.asarray(ln1_b[l], np.float32)
            m["ln2_g"] = np.asarray(ln2_g[l], np.float32)
            m["ln2_b"] = np.asarray(ln2_b[l], np.float32)
            maps2.append(m)
        r2 = _run(["p3b", "p3c", "p5"], maps2, ["xo"])
        xo = [np.asarray(r2[c]["xo"], np.float32) for c in range(8)]
        xloc = [np.ascontiguousarray(np.concatenate([xo[c], xo[c ^ 1][::-1][:TH - T]], 0)) for c in range(8)]
    out = np.empty((4, 8192, D), np.float32)
    for c in range(8):
        b, s = core_bs(c)
        if s == 0:
            out[b, :T] = xloc[c][:T]
        else:
            out[b, T:] = xloc[c][:T][::-1]
    return out


LAYER_IN = ["w_in", "gpar", "convw", "nab", "na_g", "gdn_g", "w_out", "w1", "w2", "b1t", "b2",
            "ln1_g", "ln1_b", "ln2_g", "ln2_b"]


def build_fused(in_specs):
    nc = bass.Bass("TRN2", target_bir_lowering=False)
    io = {}
    for name, (shape, dt) in in_specs.items():
        io[name] = nc.dram_tensor(name, list(shape), dt, kind="ExternalInput").ap()
    for name, (shape, dt) in SCRATCH.items():
        io[name] = nc.dram_tensor(name, list(shape), dt, kind="Internal").ap()
    io["out"] = nc.dram_tensor("out", [T, D], F32, kind="ExternalOutput").ap()
    for l in range(2):
        iol = dict(io)
        for k in LAYER_IN:
            iol[k] = io[f"{k}_l{l}"]
        phase1(nc, iol, l == 0, "x" if l == 0 else "x1")
        phase2(nc, iol, range(0, 4), f"p2a{l}")
        phase2(nc, iol, range(4, 8), f"p2b{l}")
        phase4(nc, iol)
        phase3(nc, iol, 0, f"p3a{l}", fused=True)
        phase3(nc, iol, 1, f"p3b{l}", fused=True)
        phase3c(nc, iol)
        if l == 0:
            phase5(nc, iol, "xn", "x1")
            phase_halo(nc, iol)
        else:
            phase5(nc, iol, "x1", "out")
    return nc


def kernel(x, ln_in_g, ln_in_b, w_in, conv_w, a_log, dt_bias, gdn_norm_g, rpb, na_norm_g,
           w_out, ln1_g, ln1_b, w1, b1, w2, b2, ln2_g, ln2_b):
    inp = dict(w_in=w_in, conv_w=conv_w, a_log=a_log, dt_bias=dt_bias)
    x = np.asarray(x, np.float32)
    consts = gdn_consts()
    f = lambda a: np.ascontiguousarray(np.asarray(a, np.float32))
    maps = []
    for c in range(8):
        b, s = core_bs(c)
        xs = x[b] if s == 0 else x[b][::-1]
        m = {"x": np.ascontiguousarray(xs[:TH]), "ln_in_g": f(ln_in_g), "ln_in_b": f(ln_in_b),
             "ident": np.eye(128, dtype=np.float32), "jmat": np.ascontiguousarray(np.eye(128, dtype=np.float32)[::-1]),
             "sel": np.array([1.0, 0.0] if s == 1 else [0.0, 1.0], np.float32)}
        m.update(consts)
        for l in range(2):
            hl = host_layer_inputs(inp, l, c, None)
            m[f"w_in_l{l}"] = hl["w_in"]
            m[f"gpar_l{l}"] = hl["gpar"]
            m[f"convw_l{l}"] = hl["convw"]
            m[f"nab_l{l}"] = na_bias_tables(f(rpb[l]), s)
            m[f"na_g_l{l}"] = f(na_norm_g[l])
            m[f"gdn_g_l{l}"] = f(gdn_norm_g[l])
            m[f"w_out_l{l}"] = f(w_out[l])
            m[f"w1_l{l}"] = f(w1[l])
            m[f"w2_l{l}"] = f(w2[l])
            m[f"b1t_l{l}"] = np.ascontiguousarray(f(b1[l]).reshape(32, 128).T)
            m[f"b2_l{l}"] = f(b2[l])
            m[f"ln1_g_l{l}"] = f(ln1_g[l])
            m[f"ln1_b_l{l}"] = f(ln1_b[l])
            m[f"ln2_g_l{l}"] = f(ln2_g[l])
            m[f"ln2_b_l{l}"] = f(ln2_b[l])
        maps.append(m)
    nc = build_fused(_specs(maps))
    res = run_bass_kernel_spmd(nc, maps, core_ids=list(range(8)))
    out = np.empty((4, 8192, D), np.float32)
    for c in range(8):
        b, s = core_bs(c)
        o = np.asarray(res.results[c]["out"], np.float32)
        if s == 0:
            out[b, :T] = o
        else:
            out[b, T:] = o[::-1]
    return out
```

```python
import numpy as np
import ml_dtypes
from contextlib import ExitStack
import concourse.bass as bass
import concourse.mybir as mybir
from concourse.bass_utils import run_bass_kernel_spmd

F32 = mybir.dt.float32
BF16 = mybir.dt.bfloat16
ALU = mybir.AluOpType
AF = mybir.ActivationFunctionType
AX = mybir.AxisListType

D = 1024
T = 4096
TH = 4352
NT = 32
DIN = 3600
DFF = 4096
ALPHA = 4.0 ** 0.25
LN_EPS = 1e-5
RMS_EPS = 1e-6
NEG = -30000.0


class Prog:
    ENGS = ("pe", "act", "dve", "pool", "sp")

    def __init__(self, nc):
        self.nc = nc
        self.ops = {e: [] for e in self.ENGS}
        self.cnt = {}
        self.seen = {e: {} for e in self.ENGS}
        self.W = {}
        self.R = {}
        self.awaited = {}

    def op(self, eng, fn, reads=(), writes=(), dma=None):
        assert fn is not None or not writes
        own = "eng:" + eng
        waits = {}

        def merge(d, skip_own):
            if d:
                for s, c in d.items():
                    if skip_own and s == own and not dma:
                        continue
                    if waits.get(s, 0) < c:
                        waits[s] = c

        for k in reads:
            merge(self.W.get(k), False)
        for k in writes:
            merge(self.W.get(k), True)
            merge(self.R.get(k), True)
        sem = ("dma:" + dma) if dma else own
        for s, c in waits.items():
            if self.seen[eng].get(s, 0) >= c:
                continue
            self.seen[eng][s] = c
            self.ops[eng].append(("wait", s, c))
            self.awaited.setdefault(s, set()).add(c)
        n = self.cnt.get(sem, 0) + 1
        self.cnt[sem] = n
        self.ops[eng].append(("op", fn, sem, n))
        for k in reads:
            self.R.setdefault(k, {})[sem] = n
        for k in writes:
            self.W[k] = {sem: n}
            self.R[k] = {}

    def dma(self, eng, out, in_, reads, writes, group):
        self.op(eng, lambda e: e.dma_start(out=out, in_=in_), reads=reads,
                writes=writes, dma=group)

    def dma_multi(self, eng, pairs, keys, group):
        for (out, in_), k in zip(pairs, keys):
            self.dma(eng, out, in_, [], [k], group)
        sem = "dma:" + group
        for k in keys:
            self.W[k] = {sem: self.cnt[sem]}

    def emit(self, name):
        nc = self.nc
        cmap = {}
        for s, cs in self.awaited.items():
            for i, c in enumerate(sorted(cs)):
                cmap[(s, c)] = i + 1
        allsems = set(self.awaited)
        for s, n in self.cnt.items():
            if s.startswith("dma:"):
                allsems.add(s)
                for c in range(1, n + 1):
                    cmap[(s, c)] = c
        with ExitStack() as st:
            st.enter_context(nc.cleanup_on_exit())
            sems = {}
            for s in sorted(allsems):
                _UID[0] += 1
                sems[s] = nc.alloc_semaphore(name=f"{name}_{_UID[0]}_" + s.replace(":", "_"))
            block = st.enter_context(nc.Block())

            def make(ename):
                def body(e):
                    for o in self.ops[ename]:
                        if o[0] == "wait":
                            _, s, c = o
                            mult = 16 if s.startswith("dma:") else 1
                            e.wait_ge(sems[s], cmap[(s, c)] * mult)
                        else:
                            _, fn, s, n = o
                            if fn is None:
                                continue
                            ins = fn(e)
                            if (s, n) in cmap:
                                ins.then_inc(
                                    sems[s], 16 if s.startswith("dma:") else 1)
                return body

            block.tensor(make("pe"))
            block.scalar(make("act"))
            block.vector(make("dve"))
            block.gpsimd(make("pool"))
            block.sync(make("sp"))


_UID = [0]


def sb(st, nc, name, shape, dt):
    _UID[0] += 1
    return st.enter_context(nc.sbuf_tensor(f"{name}_u{_UID[0]}", list(shape), dt))


def pst(st, nc, name, shape, dt):
    _UID[0] += 1
    return st.enter_context(nc.psum_tensor(f"{name}_u{_UID[0]}", list(shape), dt))


def phase1(nc, io, layer0, xname="x"):
    x_d = io[xname]
    with ExitStack() as st:
        P = Prog(nc)
        win = sb(st, nc, "p1_win", [128, 8, DIN], BF16)
        ident = sb(st, nc, "p1_ident", [128, 128], F32)
        lng = sb(st, nc, "p1_lng", [128, D], F32)
        lnb = sb(st, nc, "p1_lnb", [128, D], F32)
        gpar = sb(st, nc, "p1_gpar", [128, 16], F32)
        nea = sb(st, nc, "p1_nea", [128, 8], F32)
        zpad = sb(st, nc, "p1_zpad", [128, 2], F32)
        xt = [sb(st, nc, f"p1_xt{i}", [128, D], F32) for i in range(3)]
        xn = [sb(st, nc, f"p1_xn{i}", [128, D], F32) for i in range(2)]
        stats = sb(st, nc, "p1_stats", [128, 2, 6], F32)
        mv = sb(st, nc, "p1_mv", [128, 2], F32)
        rstd = sb(st, nc, "p1_rstd", [128, 1], F32)
        xT = [sb(st, nc, f"p1_xT{i}", [128, 8, 512], BF16) for i in range(2)]
        fo = [sb(st, nc, f"p1_fo{i}", [128, 512], F32) for i in range(3)]
        fob = [sb(st, nc, f"p1_fob{i}", [128, 512], BF16) for i in range(3)]
        to = [sb(st, nc, f"p1_to{i}", [128, 512], F32) for i in range(2)]
        tob = [sb(st, nc, f"p1_tob{i}", [128, 512], BF16) for i in range(2)]
        ab = [sb(st, nc, f"p1_ab{i}", [128, 16], F32) for i in range(2)]
        abt = [sb(st, nc, f"p1_abt{i}", [128, 16], F32) for i in range(2)]
        psT = [pst(st, nc, f"p1_psT{i}", [128, 512], F32) for i in range(2)]
        psF = [pst(st, nc, f"p1_psF{i}", [128, 512], F32) for i in range(3)]
        psK = [pst(st, nc, f"p1_psK{i}", [128, 512], F32) for i in range(2)]
        psA = pst(st, nc, "p1_psA", [128, 16], F32)

        for k in range(8):
            P.dma("pool", win[:, k, :], io["w_in"][k * 128:(k + 1) * 128, :],
                  [], [("win", k)], f"win{k}")
        P.dma("sp", ident[:], io["ident"][:, :], [], ["ident"], "c0")
        if layer0:
            P.dma("sp", lng[:], io["ln_in_g"].partition_broadcast(128), [], ["lng"], "c2")
            P.dma("sp", lnb[:], io["ln_in_b"].partition_broadcast(128), [], ["lnb"], "c3")
        P.dma("sp", gpar[:], io["gpar"].partition_broadcast(128), [], ["gpar"], "c4")
        P.op("act", lambda e: e.activation(out=nea[:], in_=gpar[:, 0:8], func=AF.Exp),
             ["gpar"], ["nea0"])
        P.op("dve", lambda e: e.tensor_scalar(out=nea[:], in0=nea[:], scalar1=-1.0,
                                              scalar2=None, op0=ALU.mult),
             ["nea0"], ["nea"])
        P.op("pool", lambda e: e.memset(zpad[:], 0.0), [], ["zpad"])
        for c in range(12):
            P.dma("sp", io["hq"][c * 128:(c + 1) * 128, 0:2], zpad[:], ["zpad"],
                  [("hq", c, -1)], "zp")

        nmac = 9
        cnt = {"xt": 0, "xn": 0, "fo": 0, "to": 0, "psF": 0, "psK": 0, "ab": 0}
        for mt in range(nmac):
            halo = mt == 8
            nsub = 2 if halo else 4
            ntok = nsub * 128
            tok0 = mt * 512
            xTm = xT[mt % 2]
            kxT = ("xT", mt % 2)
            for sub in range(nsub):
                t0 = tok0 + sub * 128
                xi = cnt["xt"] % 3
                cnt["xt"] += 1
                xtile = xt[xi]
                P.dma("sp", xtile[:], x_d[t0:t0 + 128, :], [], [("xt", xi)], f"xt{xi}")
                src = xtile
                ksrc = ("xt", xi)
                if layer0:
                    ni = cnt["xn"] % 2
                    cnt["xn"] += 1
                    xnt = xn[ni]
                    for hh in range(2):
                        P.op("dve", lambda e, hh=hh, xtile=xtile: e.bn_stats(
                            out=stats[:, hh, :], in_=xtile[:, hh * 512:(hh + 1) * 512]),
                            [("xt", xi)], [("stats", hh)])
                    P.op("dve", lambda e: e.bn_aggr(out=mv[:], in_=stats[:].rearrange("p a b -> p (a b)")),
                         [("stats", 0), ("stats", 1)], ["mv"])
                    P.op("act", lambda e: e.activation(
                        out=rstd[:], in_=mv[:, 1:2], func=AF.Ln, bias=LN_EPS), ["mv"], ["rstd"])
                    P.op("act", lambda e: e.activation(
                        out=rstd[:], in_=rstd[:], func=AF.Exp, scale=-0.5), ["rstd"], ["rstd"])
                    P.op("dve", lambda e, xtile=xtile, xnt=xnt: e.tensor_scalar(
                        out=xnt[:], in0=xtile[:], scalar1=mv[:, 0:1], scalar2=rstd[:, 0:1],
                        op0=ALU.subtract, op1=ALU.mult),
                        [("xt", xi), "mv", "rstd"], [("xn", ni)])
                    P.op("dve", lambda e, xnt=xnt: e.tensor_tensor(
                        out=xnt[:], in0=xnt[:], in1=lng[:], op=ALU.mult),
                        [("xn", ni), "lng"], [("xn", ni)])
                    P.op("dve", lambda e, xnt=xnt: e.tensor_tensor(
                        out=xnt[:], in0=xnt[:], in1=lnb[:], op=ALU.add),
                        [("xn", ni), "lnb"], [("xn", ni)])
                    if not halo:
                        P.dma("sp", io["xn"][t0:t0 + 128, :], xnt[:], [("xn", ni)],
                              [("xn_d", t0 // 128)], f"xns{ni}")
                    src = xnt
                    ksrc = ("xn", ni)
                idm, kid = ident, "ident"
                for hb in range(2):
                    pT = psT[hb]

                    def tr(e, hb=hb, pT=pT, src=src, idm=idm):
                        ins = None
                        for kk in range(4):
                            k = hb * 4 + kk
                            ins = e.transpose(out=pT[:, kk * 128:(kk + 1) * 128],
                                              in_=src[:, k * 128:(k + 1) * 128],
                                              identity=idm[:])
                        return ins
                    P.op("pe", tr, [ksrc, kid], [("psT", hb)])
                    outap = xTm[:, hb * 4:(hb + 1) * 4, sub * 128:(sub + 1) * 128]
                    inap = pT[:].rearrange("p (k t) -> p k t", k=4)
                    if hb == 0:
                        P.op("act", lambda e, o=outap, i=inap: e.copy(out=o, in_=i),
                             [("psT", hb)], [(kxT, sub, hb)])
                    else:
                        P.op("dve", lambda e, o=outap, i=inap: e.tensor_copy(out=o, in_=i),
                             [("psT", hb)], [(kxT, sub, hb)])
            xkeys = [(kxT, s_, h_) for s_ in range(nsub) for h_ in range(2)]
            wkeys = [("win", k) for k in range(8)]

            fm = [(c * 128, "hq", c) for c in range(12)]
            if not halo:
                fm += [(2064 + c * 128, "q", c) for c in range(4)]
            fm += [(2576 + c * 128, "k", c) for c in range(4)]
            for (col, kind, c) in fm:
                n = ntok
                if halo and kind == "hq":
                    n = 2
                pi = cnt["psF"] % 3
                cnt["psF"] += 1
                pF = psF[pi]

                def mm(e, col=col, n=n, pF=pF, xTm=xTm):
                    ins = None
                    for k in range(8):
                        ins = e.matmul(pF[:, 0:n], lhsT=win[:, k, col:col + 128],
                                       rhs=xTm[:, k, 0:n], start=(k == 0), stop=(k == 7))
                    return ins
                P.op("pe", mm, xkeys + wkeys, [("psF", pi)])
                fi = cnt["fo"] % 3
                cnt["fo"] += 1
                if kind == "hq":
                    fbuf = fo[fi]
                    P.op("act", lambda e, fbuf=fbuf, pF=pF, n=n: e.copy(out=fbuf[:, 0:n], in_=pF[:, 0:n]),
                         [("psF", pi)], [("fo", fi)])
                    P.dma("sp", io["hq"][c * 128:(c + 1) * 128, 2 + tok0:2 + tok0 + n],
                          fbuf[:, 0:n], [("fo", fi)], [("hq", c, mt)], f"fo{fi}")
                else:
                    fbuf = fob[fi]
                    sc = 0.125 if kind == "q" else 1.0
                    P.op("act", lambda e, fbuf=fbuf, pF=pF, n=n, sc=sc: e.activation(
                        out=fbuf[:, 0:n], in_=pF[:, 0:n], func=AF.Copy, scale=sc),
                        [("psF", pi)], [("fob", fi)])
                    dst = io["qn"] if kind == "q" else io["kn"]
                    P.dma("sp", dst[c * 128:(c + 1) * 128, tok0:tok0 + n], fbuf[:, 0:n],
                          [("fob", fi)], [(kind + "n", c, mt)], f"fob{fi}")

            for sub in range(nsub):
                t0 = tok0 + sub * 128
                tk = [("v", 3088)]
                if not halo:
                    tk = [("z", 1536), ("v", 3088), ("ab", 2048)]
                for (kind, col) in tk:
                    if kind == "ab":
                        def mm(e, sub=sub, xTm=xTm):
                            ins = None
                            for k in range(8):
                                ins = e.matmul(psA[:, 0:16], lhsT=xTm[:, k, sub * 128:(sub + 1) * 128],
                                               rhs=win[:, k, 2048:2064], start=(k == 0), stop=(k == 7))
                            return ins
                        P.op("pe", mm, xkeys + wkeys, ["psA"])
                        ai = cnt["ab"] % 2
                        cnt["ab"] += 1
                        a_, at_ = ab[ai], abt[ai]
                        P.op("dve", lambda e, at_=at_: e.tensor_copy(out=at_[:, 8:16], in_=psA[:, 8:16]),
                             ["psA"], [("abt2", ai)])
                        P.op("dve", lambda e, at_=at_: e.tensor_tensor(
                            out=at_[:, 0:8], in0=psA[:, 0:8], in1=gpar[:, 8:16], op=ALU.add),
                            ["psA", "gpar"], [("abt", ai)])
                        P.op("act", lambda e, at_=at_: e.activation(
                            out=at_[:, 8:16], in_=at_[:, 8:16], func=AF.Exp, scale=-1.0),
                            [("abt2", ai)], [("abt2", ai)])
                        P.op("act", lambda e, at_=at_: e.activation(
                            out=at_[:, 0:8], in_=at_[:, 0:8], func=AF.Exp),
                            [("abt", ai)], [("abt", ai)])
                        P.op("act", lambda e, at_=at_: e.activation(
                            out=at_[:, 0:8], in_=at_[:, 0:8], func=AF.Ln, bias=1.0),
                            [("abt", ai)], [("abt", ai)])
                        P.op("dve", lambda e, at_=at_, a_=a_: e.tensor_tensor(
                            out=a_[:, 0:8], in0=at_[:, 0:8], in1=nea[:], op=ALU.mult),
                            [("abt", ai), "nea"], [("ab", ai)])
                        P.op("dve", lambda e, at_=at_: e.tensor_scalar(
                            out=at_[:, 8:16], in0=at_[:, 8:16], scalar1=1.0, scalar2=None,
                            op0=ALU.add), [("abt2", ai)], [("abt2", ai)])
                        P.op("dve", lambda e, at_=at_, a_=a_: e.reciprocal(
                            out=a_[:, 8:16], in_=at_[:, 8:16]),
                            [("abt2", ai), ("ab", ai)], [("ab", ai)])
                        P.dma("sp", io["gb"][t0:t0 + 128, :], a_[:], [("ab", ai)],
                              [("gb", t0 // 128)], f"ab{ai}")
                        continue
                    pi = cnt["psK"] % 2
                    cnt["psK"] += 1
                    pK = psK[pi]

                    def mm(e, sub=sub, xTm=xTm, pK=pK, col=col):
                        ins = None
                        for k in range(8):
                            ins = e.matmul(pK[:, :], lhsT=xTm[:, k, sub * 128:(sub + 1) * 128],
                                           rhs=win[:, k, col:col + 512], start=(k == 0), stop=(k == 7))
                        return ins
                    P.op("pe", mm, xkeys + wkeys, [("psK", pi)])
                    ti = cnt["to"] % 2
                    cnt["to"] += 1
                    if kind == "z":
                        tb = to[ti]
                        P.op("act", lambda e, tb=tb, pK=pK: e.copy(out=tb[:], in_=pK[:]),
                             [("psK", pi)], [("to", ti)])
                        P.dma("sp", io["zs"][t0:t0 + 128, :], tb[:], [("to", ti)],
                              [("zs", t0 // 128)], f"to{ti}")
                    else:
                        tb = tob[ti]
                        P.op("dve", lambda e, tb=tb, pK=pK: e.tensor_copy(out=tb[:], in_=pK[:]),
                             [("psK", pi)], [("tob", ti)])
                        P.dma("sp", io["vn"][t0:t0 + 128, :], tb[:], [("tob", ti)],
                              [("vn", t0 // 128)], f"tob{ti}")
        allk = [k for k in P.W if isinstance(k, tuple) and k[0] in
                ("hq", "qn", "kn", "vn", "zs", "gb", "xn_d")]
        P.op("sp", None, reads=allk)
        P.emit("p1")


def phase2(nc, io, mts=range(8), tag="p2"):
    with ExitStack() as st:
        P = Prog(nc)
        cw = sb(st, nc, "p2_cw", [128, 12, 5], F32)
        onesb = sb(st, nc, "p2_ones", [128, 128], BF16)
        identb = sb(st, nc, "p2_identb", [128, 128], BF16)
        hw = [sb(st, nc, f"p2_hw{i}", [128, 516], F32) for i in range(3)]
        acc = [sb(st, nc, f"p2_acc{i}", [128, 512], F32) for i in range(2)]
        ptmp = sb(st, nc, "p2_ptmp", [128, 512], F32)
        sil = [sb(st, nc, f"p2_sil{i}", [128, 512], F32) for i in range(12)]
        sqb = [sb(st, nc, f"p2_sq{i}", [128, 512], BF16) for i in range(8)]
        lnv = [sb(st, nc, f"p2_ln{i}", [128, 512], F32) for i in range(8)]
        nb = [sb(st, nc, f"p2_nb{i}", [128, 512], BF16) for i in range(12)]
        tk = [sb(st, nc, f"p2_tk{i}", [128, 512], BF16) for i in range(2)]
        psN = [pst(st, nc, f"p2_psN{i}", [128, 512], F32) for i in range(4)]
        psT = [pst(st, nc, f"p2_psT{i}", [128, 512], BF16) for i in range(2)]
        P.dma_multi("sp", [(cw[:, c, :], io["convw"][c * 128:(c + 1) * 128, :]) for c in range(12)],
                    [("cw", c) for c in range(12)], "c0")
        P.op("pool", lambda e: e.memset(onesb[:], 1.0), [], ["ones"])
        P.dma("pool", identb[:], io["ident"][:, :], [], ["identb"], "c1")
        cnt = {"hw": 0, "acc": 0, "sq": 0, "psN": 0, "psT": 0, "tk": 0}
        stop = ""
        for mt in mts:
            tok0 = mt * 512
            for c in range(12):
                hi = cnt["hw"] % 3
                cnt["hw"] += 1
                h_ = hw[hi]
                P.dma("sp", h_[:], io["hq"][c * 128:(c + 1) * 128, tok0:tok0 + 516], [], [("hw", hi)], f"hw{hi}")
                ai = cnt["acc"] % 2
                cnt["acc"] += 1
                a_ = acc[ai]
                eng = "dve"
                P.op(eng, lambda e, a_=a_, h_=h_, c=c: e.tensor_scalar(
                    out=a_[:], in0=h_[:, 0:512], scalar1=cw[:, c, 0:1], scalar2=None, op0=ALU.mult),
                    [("hw", hi), ("cw", c)], [("acc", ai)])
                for i in range(1, 5):
                    if eng == "dve":
                        P.op(eng, lambda e, a_=a_, h_=h_, c=c, i=i: e.scalar_tensor_tensor(
                            out=a_[:], in0=h_[:, i:i + 512], scalar=cw[:, c, i:i + 1], in1=a_[:],
                            op0=ALU.mult, op1=ALU.add), [("hw", hi), ("cw", c), ("acc", ai)], [("acc", ai)])
                    else:
                        P.op(eng, lambda e, h_=h_, c=c, i=i: e.tensor_scalar(
                            out=ptmp[:], in0=h_[:, i:i + 512], scalar1=cw[:, c, i:i + 1], scalar2=None,
                            op0=ALU.mult), [("hw", hi), ("cw", c)], ["ptmp"])
                        P.op(eng, lambda e, a_=a_: e.tensor_tensor(out=a_[:], in0=a_[:], in1=ptmp[:], op=ALU.add),
                             ["ptmp", ("acc", ai)], [("acc", ai)])
                if c < 8:
                    P.op("act", lambda e, a_=a_, c=c: e.activation(out=sil[c][:], in_=a_[:], func=AF.Silu),
                         [("acc", ai)], [("sil", c)])
                else:
                    P.op("act", lambda e, a_=a_, c=c: e.activation(out=nb[c][:], in_=a_[:], func=AF.Silu),
                         [("acc", ai)], [("nb", c)])
            if stop == "s1":
                break
            for c in range(8):
                P.op("act", lambda e, c=c: e.activation(out=sqb[c][:], in_=sil[c][:], func=AF.Square),
                     [("sil", c)], [("sq", c)])
            for c in range(8):
                si = c
                pi = cnt["psN"] % 4
                cnt["psN"] += 1
                pN = psN[pi]
                P.op("pe", lambda e, pN=pN, si=si: e.matmul(pN[:, :], lhsT=onesb[:, :], rhs=sqb[si][:, :],
                                                             start=True, stop=True),
                     [("sq", si), "ones"], [("psN", pi)])
                P.op("act", lambda e, pN=pN, c=c: e.activation(out=lnv[c][:], in_=pN[:], func=AF.Ln, bias=RMS_EPS),
                     [("psN", pi)], [("ln", c)])
            if stop == "s2":
                break
            for c in range(12):
                if c < 8:
                    P.op("act", lambda e, c=c: e.activation(out=lnv[c][:], in_=lnv[c][:], func=AF.Exp, scale=-0.5),
                         [("ln", c)], [("ln", c)])
                    sc = 128.0 ** -0.5 if c < 4 else 1.0
                    P.op("dve", lambda e, c=c, sc=sc: e.scalar_tensor_tensor(
                        out=nb[c][:], in0=sil[c][:], scalar=sc, in1=lnv[c][:], op0=ALU.mult, op1=ALU.mult),
                        [("sil", c), ("ln", c)], [("nb", c)])
                    dst = io["qg"] if c < 4 else io["kg"]
                    cc = c % 4
                    P.dma("sp", dst[cc * 128:(cc + 1) * 128, tok0:tok0 + 512], nb[c][:], [("nb", c)],
                          [("qkg", c, mt)], f"nb{c}")
            if stop == "s3":
                break
            for kind, base, dname in (("k", 4, "ktok"), ("v", 8, "vtok")):
                for sub in range(4):
                    pi = cnt["psT"] % 2
                    cnt["psT"] += 1
                    pT = psT[pi]

                    def tr(e, pT=pT, base=base, sub=sub):
                        ins = None
                        for h in range(4):
                            ins = e.transpose(out=pT[:, h * 128:(h + 1) * 128],
                                              in_=nb[base + h][:, sub * 128:(sub + 1) * 128],
                                              identity=identb[:])
                        return ins
                    P.op("pe", tr, [("nb", base + h) for h in range(4)] + ["identb"], [("psT", pi)])
                    ti = cnt["tk"] % 2
                    cnt["tk"] += 1
                    if ti == 0:
                        P.op("act", lambda e, pT=pT, ti=ti: e.copy(out=tk[ti][:], in_=pT[:]),
                             [("psT", pi)], [("tk", ti)])
                    else:
                        P.op("dve", lambda e, pT=pT, ti=ti: e.tensor_copy(out=tk[ti][:], in_=pT[:]),
                             [("psT", pi)], [("tk", ti)])
                    t0 = tok0 + sub * 128
                    P.dma("sp", io[dname][t0:t0 + 128, :], tk[ti][:], [("tk", ti)],
                          [(dname, t0 // 128)], f"tk{ti}")
        allk = [k for k in P.W if isinstance(k, tuple) and k[0] in ("qkg", "ktok", "vtok")]
        P.op("sp", None, reads=allk)
        P.emit(tag)


def phase3(nc, io, dirn, tag, fused=False):
    sfx = "A" if dirn == 0 else "B"
    with ExitStack() as st:
        P = Prog(nc)
        LT = sb(st, nc, "p3_LT", [128, 128], F32)
        XS = sb(st, nc, "p3_XS", [128, 128], F32)
        NTI = sb(st, nc, "p3_NTI", [128, 512], F32)
        NS = sb(st, nc, "p3_NS", [128, 512], F32)
        ident = sb(st, nc, "p3_ident", [128, 128], F32)
        identb = sb(st, nc, "p3_identb", [128, 128], BF16)
        I4f = sb(st, nc, "p3_I4f", [128, 512], F32)
        ones = sb(st, nc, "p3_ones", [128, 128], F32)
        S4 = sb(st, nc, "p3_S4", [128, 512], F32)
        Sbf = sb(st, nc, "p3_Sbf", [128, 512], BF16)
        NS_ = 2
        qT4 = [sb(st, nc, f"p3_qT{i}", [128, 4, 128], BF16) for i in range(NS_)]
        kT4 = [sb(st, nc, f"p3_kT{i}", [128, 4, 128], BF16) for i in range(NS_)]
        kt4 = [sb(st, nc, f"p3_kt{i}", [128, 4, 128], BF16) for i in range(NS_)]
        vt4 = [sb(st, nc, f"p3_vt{i}", [128, 4, 128], BF16) for i in range(NS_)]
        gb = [sb(st, nc, f"p3_gb{i}", [128, 16], F32) for i in range(NS_)]
        sm = [sb(st, nc, f"p3_sm{i}", [128, 32], F32) for i in range(NS_)]
        Y4 = sb(st, nc, "p3_Y4", [128, 4, 128], F32)
        GTi = [sb(st, nc, f"p3_GTi{i}", [128, 512], F32) for i in range(NS_)]
        Gs = sb(st, nc, "p3_Gs", [128, 4, 128], F32)
        CH = F32
        Mb = [sb(st, nc, f"p3_M{i}", [128, 4, 128], CH) for i in range(2)]
        MTb = [sb(st, nc, f"p3_MT{i}", [128, 4, 128], CH) for i in range(2)]
        Xb = [sb(st, nc, f"p3_X{i}", [128, 4, 128], CH) for i in range(3)]
        Dg4 = sb(st, nc, "p3_Dg4", [128, 4, 128], F32)
        kbgT = [sb(st, nc, f"p3_kbgT{i}", [128, 4, 128], F32) for i in range(NS_)]
        vb = [sb(st, nc, f"p3_vb{i}", [128, 4, 128], F32) for i in range(NS_)]
        r4 = sb(st, nc, "p3_r4", [128, 4, 128], F32)
        kdec = [sb(st, nc, f"p3_kdec{i}", [128, 4, 128], BF16) for i in range(NS_)]
        nwT = [sb(st, nc, f"p3_nwT{i}", [128, 4, 128], BF16) for i in range(NS_)]
        qkT = [sb(st, nc, f"p3_qkT{i}", [128, 4, 128], BF16) for i in range(NS_)]
        qdT = [sb(st, nc, f"p3_qdT{i}", [128, 4, 128], F32) for i in range(NS_)]
        RT = [sb(st, nc, f"p3_RT{i}", [128, 4, 128], F32) for i in range(NS_)]
        vnew = sb(st, nc, "p3_vnew", [128, 4, 128], BF16)
        osb = [sb(st, nc, f"p3_o{i}", [128, 512], F32) for i in range(2)]
        oin = [sb(st, nc, f"p3_oin{i}", [128, 512], F32) for i in range(2)]
        b0 = pst(st, nc, "p3_b0", [128, 512], F32)
        b1 = pst(st, nc, "p3_b1", [128, 512], F32)
        b2 = pst(st, nc, "p3_b2", [128, 512], F32)
        b3 = pst(st, nc, "p3_b3", [128, 512], F32)
        pV = pst(st, nc, "p3_pV", [128, 512], F32)
        pO = pst(st, nc, "p3_pO", [128, 512], F32)
        pS = pst(st, nc, "p3_pS", [128, 512], F32)
        tb = pst(st, nc, "p3_tb", [128, 512], F32)

        P.dma("sp", LT[:], io["LT" + sfx][:, :], [], ["LT"], "c0")
        P.dma("sp", XS[:], io["XS" + sfx][:, :], [], ["XS"], "c1")
        P.dma("sp", NTI[:], io["NTI" + sfx][:, :], [], ["NTI"], "c2")
        P.dma("sp", NS[:], io["NS" + sfx][:, :], [], ["NS"], "c3")
        P.dma("sp", ident[:], io["ident"][:, :], [], ["ident"], "c4")
        P.dma("pool", identb[:], io["ident"][:, :], [], ["identb"], "c5")
        P.dma("sp", I4f[:], io["I4f"][:, :], [], ["I4f"], "c6")
        P.op("pool", lambda e: e.memset(ones[:], 1.0), [], ["ones"])
        if not fused:
            P.dma("sp", S4[:], io["st_in"][:, :], [], ["S4"], "c7")
        elif dirn == 0:
            P.op("pool", lambda e: e.memset(S4[:], 0.0), [], ["S4"])
        else:
            sga = sb(st, nc, "p3_sga", [128, 512], F32)
            sgb = sb(st, nc, "p3_sgb", [128, 512], F32)
            selt = sb(st, nc, "p3_sel", [128, 2], F32)
            P.dma("sp", sga[:], io["sg"][0:128, :], [], ["sga"], "c7")
            P.dma("sp", sgb[:], io["sg"][128:256, :], [], ["sgb"], "c9")
            P.dma("sp", selt[:], io["sel"].partition_broadcast(128), [], ["selt"], "c10")
            P.op("dve", lambda e: e.tensor_scalar(out=S4[:], in0=sga[:], scalar1=selt[:, 0:1], scalar2=None,
                                                  op0=ALU.mult), ["sga", "selt"], ["S4"])
            P.op("dve", lambda e: e.scalar_tensor_tensor(out=S4[:], in0=sgb[:], scalar=selt[:, 1:2], in1=S4[:],
                                                         op0=ALU.mult, op1=ALU.add), ["sgb", "selt", "S4"], ["S4"])

        def v3(t):
            return t[:].rearrange("p (h x) -> p h x", h=4)

        def bc(ap4):
            return ap4.unsqueeze(2).to_broadcast([128, 4, 128])

        def perhead(ps, lhs, rhs, twice=None):
            def f(e):
                ins = None
                for h in range(4):
                    ins = e.matmul(ps[:, h * 128:(h + 1) * 128], lhsT=lhs[:, h, :], rhs=rhs[:, h, :],
                                   start=True, stop=True)
                return ins
            return f

        order = list(range(NT)) if dirn == 0 else list(range(NT - 1, -1, -1))
        xcnt = [0]

        import os
        cut = int(os.environ.get("P3_CUT", "99"))

        def prep(t, si):
            K = lambda n: (n, si)
            c0 = t * 128
            P.dma("sp", qT4[si][:], io["qg"][:, c0:c0 + 128].rearrange("(h d) t -> d h t", h=4), [], [K("qT")], f"qT{si}")
            P.dma("sp", kT4[si][:], io["kg"][:, c0:c0 + 128].rearrange("(h d) t -> d h t", h=4), [], [K("kT")], f"kT{si}")
            P.dma("sp", kt4[si][:].rearrange("p h d -> p (h d)"), io["ktok"][c0:c0 + 128, :], [], [K("kt")], f"kt{si}")
            P.dma("sp", vt4[si][:].rearrange("p h d -> p (h d)"), io["vtok"][c0:c0 + 128, :], [], [K("vt")], f"vt{si}")
            P.dma("sp", gb[si][:], io["gb"][c0:c0 + 128, :], [], [K("gb")], f"gb{si}")
            g4 = gb[si][:, 4 * dirn:4 * dirn + 4]
            be4 = gb[si][:, 8 + 4 * dirn:12 + 4 * dirn]
            s_ = sm[si]
            eg, dk, gl, bg, nbeta, tmp4 = (s_[:, 0:4], s_[:, 4:8], s_[:, 8:12], s_[:, 12:16],
                                           s_[:, 16:20], s_[:, 20:24])
            def mmc(e):
                e.matmul(b0[:, 0:4], lhsT=LT[:, :], rhs=g4, start=True, stop=True)
                return e.matmul(b0[:, 4:8], lhsT=ones[:, :], rhs=g4, start=True, stop=True)
            P.op("pe", mmc, [K("gb"), "LT", "ones"], ["b0"])
            gcs = s_[:, 24:32]
            P.op("dve", lambda e: e.tensor_copy(out=gcs, in_=b0[:, 0:8]), ["b0"], [K("gcs")])
            P.op("act", lambda e: e.activation(out=eg, in_=gcs[:, 0:4], func=AF.Exp), [K("gcs")], [K("eg")])
            P.op("act", lambda e: e.activation(out=gl, in_=gcs[:, 4:8], func=AF.Exp), [K("gcs")], [K("gl")])
            P.op("dve", lambda e: e.tensor_tensor(out=tmp4, in0=gcs[:, 4:8], in1=gcs[:, 0:4], op=ALU.subtract),
                 [K("gcs")], [K("tmp4")])
            P.op("act", lambda e: e.activation(out=dk, in_=tmp4, func=AF.Exp), [K("tmp4")], [K("dk")])
            P.op("dve", lambda e: e.tensor_tensor(out=bg, in0=be4, in1=eg, op=ALU.mult), [K("gb"), K("eg")], [K("bg")])
            P.op("dve", lambda e: e.tensor_scalar(out=nbeta, in0=be4, scalar1=-1.0, scalar2=None, op0=ALU.mult),
                 [K("gb")], [K("nbeta")])
            if cut <= 1:
                return
            P.op("dve", lambda e: e.tensor_tensor(out=Y4[:], in0=LT[:].unsqueeze(1).to_broadcast([128, 4, 128]),
                                                   in1=bc(g4), op=ALU.mult), [K("gb"), "LT"], ["Y4"])
            def mm1(e):
                e.matmul(b1[:, :], lhsT=ident[:, :], rhs=NTI[:, :], start=True, stop=False)
                return e.matmul(b1[:, :], lhsT=XS[:, :], rhs=Y4[:].rearrange("p h c -> p (h c)"), start=False, stop=True)
            P.op("pe", mm1, ["Y4", "XS", "NTI", "ident"], ["b1"])
            P.op("act", lambda e: e.activation(out=GTi[si][:], in_=b1[:], func=AF.Exp), ["b1"], [K("GTi")])
            if cut <= 2:
                return
            def mm2(e):
                ins = e.matmul(b2[:, :], lhsT=ident[:, :], rhs=NS[:, :], start=True, stop=False)
                for h in range(4):
                    ins = e.matmul(b2[:, h * 128:(h + 1) * 128], lhsT=Y4[:, h, :], rhs=XS[:, :],
                                   start=False, stop=(h == 3))
                return ins
            P.op("pe", mm2, ["Y4", "XS", "NS", "ident"], ["b2"])
            P.op("act", lambda e: e.activation(out=Gs[:].rearrange("p h c -> p (h c)"), in_=b2[:], func=AF.Exp),
                 ["b2"], ["Gs"])
            P.op("dve", lambda e: e.tensor_tensor(out=Gs[:], in0=Gs[:], in1=bc(nbeta), op=ALU.mult),
                 ["Gs", K("nbeta")], ["Gs"])
            if cut <= 3:
                return
            P.op("pe", perhead(b3, kT4[si], kT4[si]), [K("kT")], ["b3"])
            P.op("dve", lambda e: e.tensor_tensor(out=Mb[0][:], in0=v3(b3), in1=Gs[:], op=ALU.mult),
                 ["b3", "Gs"], ["M0"])

            if cut <= 4:
                return

            def trM(e):
                ins = None
                for h in range(4):
                    ins = e.transpose(out=tb[:, h * 128:(h + 1) * 128], in_=Mb[0][:, h, :], identity=ident[:])
                return ins
            var = os.environ.get("P3_VAR", "")
            P.op("pe", trM, ["M0", "ident"], ["tb"])
            if var != "noact":
                P.op("act", lambda e: e.copy(out=MTb[0][:], in_=v3(tb)), ["tb"], ["MT0"])
            xi = xcnt[0] % 3
            if var != "nodve":
                P.op("dve", lambda e, xi=xi: e.tensor_tensor(out=Xb[xi][:], in0=MTb[0][:], in1=v3(I4f), op=ALU.add),
                     ["MT0", "I4f"], [("X", xi)])
            if cut <= 5:
                return
            cur = 0
            for lvl in range(1, 7):
                nxt = 1 - cur
                P.op("pe", perhead(b1, MTb[cur], Mb[cur]), [f"M{cur}", f"MT{cur}"], ["b1"])
                P.op("act", lambda e, nxt=nxt: e.copy(out=Mb[nxt][:], in_=v3(b1)), ["b1"], [f"M{nxt}"])
                if lvl < 6:
                    P.op("pe", perhead(b2, Mb[cur], MTb[cur]), [f"M{cur}", f"MT{cur}"], ["b2"])
                    P.op("dve", lambda e, nxt=nxt: e.tensor_copy(out=MTb[nxt][:], in_=v3(b2)), ["b2"], [f"MT{nxt}"])
                P.op("pe", perhead(b3, Mb[nxt], Xb[xi]), [f"M{nxt}", ("X", xi)], ["b3"])
                xn_ = (xi + 1) % 3
                last = lvl == 6
                dst = RT[si] if last else Xb[xn_]
                kd = K("RT") if last else ("X", xn_)
                P.op("dve", lambda e, dst=dst, xi=xi: e.tensor_tensor(out=dst[:], in0=v3(b3), in1=Xb[xi][:], op=ALU.add),
                     ["b3", ("X", xi)], [kd])
                xi = xn_
                cur = nxt
            xcnt[0] = xi + 1
            if cut <= 6:
                return
            def _f(e):
                ins = None
                for h in range(4):
                    ins = e.activation(out=vb[si][:, h, :], in_=vt4[si][:, h, :], func=AF.Copy, scale=be4[:, h:h + 1])
                return ins
            P.op("act", _f, [K("vt"), K("gb")], [K("vb")])
            def _f(e):
                ins = None
                for h in range(4):
                    ins = e.activation(out=kdec[si][:, h, :], in_=kt4[si][:, h, :], func=AF.Copy, scale=dk[:, h:h + 1])
                return ins
            P.op("act", _f, [K("kt"), K("dk")], [K("kdec")])
            P.op("pe", perhead(b2, kT4[si], qT4[si]), [K("kT"), K("qT")], ["b2"])
            P.op("dve", lambda e: e.tensor_tensor(out=qkT[si][:], in0=v3(b2), in1=v3(GTi[si]), op=ALU.mult),
                 ["b2", K("GTi")], [K("qkT")])
            def _f(e):
                ins = None
                for h in range(4):
                    ins = e.activation(out=Dg4[:, h, :], in_=ident[:, :], func=AF.Copy, scale=eg[:, h:h + 1])
                return ins
            P.op("act", _f, ["ident", K("eg")], ["Dg4"])
            P.op("pe", lambda e: e.matmul(b3[:, :], lhsT=ones[:, :], rhs=Dg4[:].rearrange("p h c -> p (h c)"),
                                          start=True, stop=True), ["Dg4", "ones"], ["b3"])
            P.op("dve", lambda e: e.tensor_tensor(out=qdT[si][:], in0=v3(b3), in1=qT4[si][:], op=ALU.mult),
                 ["b3", K("qT")], [K("qdT")])
            def _f(e):
                ins = None
                for h in range(4):
                    ins = e.activation(out=Dg4[:, h, :], in_=ident[:, :], func=AF.Copy, scale=bg[:, h:h + 1])
                return ins
            P.op("act", _f, ["ident", K("bg")], ["Dg4"])
            P.op("pe", lambda e: e.matmul(b1[:, :], lhsT=ones[:, :], rhs=Dg4[:].rearrange("p h c -> p (h c)"),
                                          start=True, stop=True), ["Dg4", "ones"], ["b1"])
            P.op("dve", lambda e: e.tensor_tensor(out=kbgT[si][:], in0=v3(b1), in1=kT4[si][:], op=ALU.mult),
                 ["b1", K("kT")], [K("kbgT")])
            if dirn == 1:
                P.dma("sp", oin[si][:], io["oa"][c0:c0 + 128, :], [], [K("oin")], f"oin{si}")

        def step(t, si):
            K = lambda n: (n, si)
            c0 = t * 128
            gl = sm[si][:, 8:12]

            def mmR(e):
                ins = None
                for h in range(4):
                    ins = e.matmul(pV[:, h * 128:(h + 1) * 128], lhsT=kbgT[si][:, h, :],
                                   rhs=S4[:, h * 128:(h + 1) * 128], start=True, stop=True)
                return ins
            P.op("pe", mmR, [K("kbgT"), "S4"], ["pV"])
            P.op("dve", lambda e: e.tensor_tensor(out=r4[:], in0=vb[si][:], in1=v3(pV), op=ALU.subtract),
                 ["pV", K("vb")], ["r4"])
            P.op("pe", perhead(pV, RT[si], r4), [K("RT"), "r4"], ["pV"])
            P.op("act", lambda e: e.copy(out=vnew[:], in_=v3(pV)), ["pV"], ["vnew"])

            def mmO(e):
                ins = None
                for h in range(4):
                    e.matmul(pO[:, h * 128:(h + 1) * 128], lhsT=qdT[si][:, h, :],
                             rhs=S4[:, h * 128:(h + 1) * 128], start=True, stop=False)
                    ins = e.matmul(pO[:, h * 128:(h + 1) * 128], lhsT=qkT[si][:, h, :], rhs=vnew[:, h, :],
                                   start=False, stop=True)
                return ins
            P.op("pe", mmO, [K("qdT"), K("qkT"), "vnew", "S4"], ["pO"])

            def mmS(e):
                ins = None
                for h in range(4):
                    ins = e.matmul(pS[:, h * 128:(h + 1) * 128], lhsT=kdec[si][:, h, :], rhs=vnew[:, h, :],
                                   start=True, stop=True)
                return ins
            P.op("pe", mmS, [K("kdec"), "vnew"], ["pS"])
            P.op("dve", lambda e: e.tensor_tensor(out=v3(S4), in0=v3(S4), in1=bc(gl), op=ALU.mult),
                 ["S4", K("gl")], ["S4"])
            P.op("dve", lambda e: e.tensor_tensor(out=S4[:], in0=S4[:], in1=pS[:], op=ALU.add),
                 ["S4", "pS"], ["S4"])
            oi = t % 2
            if dirn == 0:
                P.op("act", lambda e: e.copy(out=osb[oi][:], in_=pO[:]), ["pO"], [("osb", oi)])
                P.dma("sp", io["oa"][c0:c0 + 128, :], osb[oi][:], [("osb", oi)], [("oa", t)], f"osb{oi}")
            else:
                P.op("dve", lambda e: e.tensor_tensor(out=osb[oi][:], in0=pO[:], in1=oin[si][:], op=ALU.add),
                     ["pO", K("oin")], [("osb", oi)])
                P.dma("sp", io["ob"][c0:c0 + 128, :], osb[oi][:], [("osb", oi)], [("ob", t)], f"osb{oi}")

        import os
        ntl = int(os.environ.get("P3_NT", NT))
        nostep = bool(int(os.environ.get("P3_NOSTEP", "0")))
        order = order[:ntl]
        prep(order[0], 0)
        for i, t in enumerate(order):
            if i + 1 < len(order):
                prep(order[i + 1], (i + 1) % 2)
            if not nostep:
                step(t, i % 2)
        P.dma("sp", io["st_out"][:, :], S4[:], ["S4"], ["st_out"], "c8")
        if fused and dirn == 0:
            P.op("pool", lambda e: e.collective_compute("AllGather", ALU.bypass, replica_groups=PAIRS,
                                                        ins=[io["st_out"][:, :]], outs=[io["sg"][:, :]]),
                 ["st_out"], ["sg"])
            P.op("sp", None, reads=["sg"])
        P.op("sp", None, reads=["st_out"] + [(("oa" if dirn == 0 else "ob"), t) for t in order if not nostep])
        P.emit(tag)


def phase3c(nc, io):
    with ExitStack() as st:
        P = Prog(nc)
        gg = sb(st, nc, "p3c_gg", [128, 128], F32)
        ob = [sb(st, nc, f"p3c_ob{i}", [128, 4, 128], F32) for i in range(2)]
        zz = [sb(st, nc, f"p3c_zz{i}", [128, 4, 128], F32) for i in range(2)]
        sq = sb(st, nc, "p3c_sq", [128, 4, 128], F32)
        ss = sb(st, nc, "p3c_ss", [128, 4], F32)
        P.dma("sp", gg[:], io["gdn_g"].partition_broadcast(128), [], ["gg"], "c0")
        for t in range(NT):
            i = t % 2
            c0 = t * 128
            P.dma("sp", ob[i][:].rearrange("p h d -> p (h d)"), io["ob"][c0:c0 + 128, :], [], [("ob", i)], f"ob{i}")
            P.dma("sp", zz[i][:].rearrange("p h d -> p (h d)"), io["zs"][c0:c0 + 128, :], [], [("zz", i)], f"zz{i}")
            P.op("dve", lambda e, i=i: e.tensor_tensor(out=sq[:], in0=ob[i][:], in1=ob[i][:], op=ALU.mult),
                 [("ob", i)], ["sq"])
            P.op("dve", lambda e: e.reduce_sum(out=ss[:], in_=sq[:], axis=AX.X), ["sq"], ["ss"])
            P.op("act", lambda e: e.activation(out=ss[:], in_=ss[:], func=AF.Ln, bias=RMS_EPS, scale=1.0 / 128),
                 ["ss"], ["ss1"])
            P.op("act", lambda e: e.activation(out=ss[:], in_=ss[:], func=AF.Exp, scale=-0.5), ["ss1"], ["ss2"])
            P.op("act", lambda e, i=i: e.activation(out=zz[i][:], in_=zz[i][:], func=AF.Silu), [("zz", i)], [("zz", i)])
            P.op("dve", lambda e, i=i: e.tensor_tensor(
                out=ob[i][:], in0=ob[i][:], in1=ss[:].unsqueeze(2).to_broadcast([128, 4, 128]), op=ALU.mult),
                [("ob", i), "ss2"], [("ob", i)])
            P.op("dve", lambda e, i=i: e.tensor_tensor(
                out=ob[i][:], in0=ob[i][:], in1=gg[:].unsqueeze(1).to_broadcast([128, 4, 128]), op=ALU.mult),
                [("ob", i), "gg"], [("ob", i)])
            P.op("dve", lambda e, i=i: e.tensor_tensor(out=ob[i][:], in0=ob[i][:], in1=zz[i][:], op=ALU.mult),
                 [("ob", i), ("zz", i)], [("ob", i)])
            P.dma("sp", io["og"][c0:c0 + 128, :], ob[i][:].rearrange("p h d -> p (h d)"), [("ob", i)],
                  [("og", t)], f"ob{i}")
        P.op("sp", None, reads=[("og", t) for t in range(NT)])
        P.emit("p3c")


def phase4(nc, io):
    with ExitStack() as st:
        P = Prog(nc)
        bias = [sb(st, nc, f"p4_bias{i}", [128, 8, 5, 128], F32) for i in range(2)]
        kwin = [sb(st, nc, f"p4_kwin{i}", [128, 4, 640], BF16) for i in range(2)]
        qw = [sb(st, nc, f"p4_qw{i}", [128, 4, 128], BF16) for i in range(2)]
        vwin = [sb(st, nc, f"p4_vwin{i}", [128, 5, 8, 65], BF16) for i in range(2)]
        tmp = [sb(st, nc, f"p4_tmp{i}", [128, 512], F32) for i in range(2)]
        ET = [sb(st, nc, f"p4_ET{i}", [128, 5, 8, 128], BF16) for i in range(2)]
        gna = sb(st, nc, "p4_gna", [128, 64], F32)
        rec = sb(st, nc, "p4_rec", [128, 8], F32)
        on = [sb(st, nc, f"p4_on{i}", [128, 8, 64], F32) for i in range(2)]
        sq = sb(st, nc, "p4_sq", [128, 8, 64], F32)
        ss = sb(st, nc, "p4_ss", [128, 8], F32)
        psS = [pst(st, nc, f"p4_psS{i}", [128, 512], F32) for i in range(4)]
        psO = [pst(st, nc, f"p4_psO{i}", [128, 512], F32) for i in range(4)]

        P.dma("sp", gna[:], io["na_g"].partition_broadcast(128), [], ["gna"], "c0")
        for i in range(2):
            P.op("pool", lambda e, i=i: e.memset(vwin[i][:], 1.0), [], [("vw", i, j) for j in range(5)])
        cnt = {"psS": 0, "tmp": 0}
        for m in range(NT):
            bi = m % 2
            ks = min(max(m - 2, 0), 29)
            var = m if m < 2 else 2
            bsl = 0 if var == 0 else (1 if var == 1 else 0)
            if m <= 2:
                P.dma("sp", bias[bsl][:].rearrange("p h j q -> p (h j q)"), io["nab"][var, :, :],
                      [], [("bias", bsl)], f"bias{bsl}")
            kw, qq, vw, et = kwin[bi], qw[bi], vwin[bi], ET[bi]
            P.dma("sp", kw[:], io["kn"][:, ks * 128:ks * 128 + 640].rearrange("(c p) t -> p c t", p=128),
                  [], [("kw", bi)], f"kw{bi}")
            P.dma("sp", qq[:], io["qn"][:, m * 128:(m + 1) * 128].rearrange("(c p) t -> p c t", p=128),
                  [], [("qw", bi)], f"qw{bi}")
            for j in range(5):
                P.dma("sp", vw[:, j, :, 0:64],
                      io["vn"][(ks + j) * 128:(ks + j + 1) * 128, :].rearrange("t (h d) -> t h d", h=8),
                      [], [("vw", bi, j)], f"vw{bi}_{j}")
            for hg in range(2):
                for j in range(5):
                    pi = cnt["psS"] % 4
                    cnt["psS"] += 1
                    pS = psS[pi]

                    def mm(e, hg=hg, j=j, pS=pS, kw=kw, qq=qq):
                        ins = None
                        for hh in range(4):
                            h = 2 * hh + hg
                            p0 = (h % 2) * 64
                            ins = e.matmul(pS[:, hh * 128:(hh + 1) * 128],
                                           lhsT=kw[p0:p0 + 64, h // 2, j * 128:(j + 1) * 128],
                                           rhs=qq[p0:p0 + 64, h // 2, :], start=True, stop=True)
                        return ins
                    P.op("pe", mm, [("kw", bi), ("qw", bi)], [("psS", pi)])
                    ti = cnt["tmp"] % 2
                    cnt["tmp"] += 1
                    t_ = tmp[ti]
                    P.op("dve", lambda e, t_=t_, pS=pS, hg=hg, j=j, bsl=bsl: e.tensor_tensor(
                        out=t_[:].rearrange("p (h q) -> p h q", h=4),
                        in0=pS[:].rearrange("p (h q) -> p h q", h=4),
                        in1=bias[bsl][:, hg * 4:(hg + 1) * 4, j, :], op=ALU.add),
                        [("psS", pi), ("bias", bsl)], [("tmp", ti)])
                    P.op("act", lambda e, t_=t_, et=et, hg=hg, j=j: e.activation(
                        out=et[:, j, hg * 4:(hg + 1) * 4, :],
                        in_=t_[:].rearrange("p (h q) -> p h q", h=4), func=AF.Exp),
                        [("tmp", ti)], [("ET", bi, hg, j)])
            o_ = on[bi]
            for hg in range(2):
                pO = psO[(m % 2) * 2 + hg]
                kO = ("psO", (m % 2) * 2 + hg)

                def pv(e, hg=hg, pO=pO, et=et, vw=vw):
                    ins = None
                    for hh in range(4):
                        h = hg * 4 + hh
                        slot = (h % 2) * 4 + h // 2
                        for j in range(5):
                            ins = e.matmul(pO[:, hh * 65:(hh + 1) * 65], lhsT=et[:, j, slot, :],
                                           rhs=vw[:, j, h, :], start=(j == 0), stop=(j == 4))
                    return ins
                P.op("pe", pv, [("ET", bi, g_, j) for g_ in range(2) for j in range(5)] +
                     [("vw", bi, j) for j in range(5)], [kO])
                pv3 = pO[:, 0:260].rearrange("p (h d) -> p h d", h=4)
                P.op("dve", lambda e, pv3=pv3, hg=hg: e.reciprocal(
                    out=rec[:, hg * 4:(hg + 1) * 4], in_=pv3[:, :, 64]), [kO], [("rec", hg)])
                P.op("dve", lambda e, pv3=pv3, hg=hg, o_=o_: e.tensor_tensor(
                    out=o_[:, hg * 4:(hg + 1) * 4, :], in0=pv3[:, :, 0:64],
                    in1=rec[:, hg * 4:(hg + 1) * 4].unsqueeze(2).to_broadcast([128, 4, 64]), op=ALU.mult),
                    [kO, ("rec", hg)], [("on", bi, hg)])
            kon = [("on", bi, 0), ("on", bi, 1)]
            P.op("dve", lambda e, o_=o_: e.tensor_tensor(out=sq[:], in0=o_[:], in1=o_[:], op=ALU.mult),
                 kon, ["sq"])
            P.op("dve", lambda e: e.reduce_sum(out=ss[:], in_=sq[:], axis=AX.X), ["sq"], ["ss"])
            P.op("act", lambda e: e.activation(out=ss[:], in_=ss[:], func=AF.Ln, bias=RMS_EPS, scale=1.0 / 64),
                 ["ss"], ["ss1"])
            P.op("act", lambda e: e.activation(out=ss[:], in_=ss[:], func=AF.Exp, scale=-0.5),
                 ["ss1"], ["ss2"])
            P.op("dve", lambda e, o_=o_: e.tensor_tensor(
                out=o_[:], in0=o_[:], in1=ss[:].unsqueeze(2).to_broadcast([128, 8, 64]), op=ALU.mult),
                kon + ["ss2"], kon)
            P.op("dve", lambda e, o_=o_: e.tensor_tensor(
                out=o_[:], in0=o_[:], in1=gna[:].unsqueeze(1).to_broadcast([128, 8, 64]), op=ALU.mult),
                kon + ["gna"], kon)
            P.dma("sp", io["ona"][m * 128:(m + 1) * 128, :], o_[:].rearrange("p h d -> p (h d)"),
                  kon, [("ona", m)], f"on{bi}")
        P.op("sp", None, reads=[("ona", m) for m in range(NT)])
        P.emit("p4")


def ln_rows(P, eng_aff, src, dst, stats, mv, rstd, gvec, bvec, ksrc, kdst, tag):
    for hh in range(2):
        P.op("dve", lambda e, hh=hh: e.bn_stats(out=stats[:, hh, :],
                                                 in_=src[:, hh * 512:(hh + 1) * 512]),
             [ksrc], [(tag, "st", hh)])
    P.op("dve", lambda e: e.bn_aggr(out=mv[:], in_=stats[:].rearrange("p a b -> p (a b)")),
         [(tag, "st", 0), (tag, "st", 1)], [(tag, "mv")])
    P.op("act", lambda e: e.activation(out=rstd[:], in_=mv[:, 1:2], func=AF.Ln, bias=LN_EPS),
         [(tag, "mv")], [(tag, "rs0")])
    P.op("act", lambda e: e.activation(out=rstd[:], in_=rstd[:], func=AF.Exp, scale=-0.5),
         [(tag, "rs0")], [(tag, "rs")])
    P.op("dve", lambda e: e.tensor_scalar(out=dst[:], in0=src[:], scalar1=mv[:, 0:1],
                                          scalar2=rstd[:, 0:1], op0=ALU.subtract, op1=ALU.mult),
         [ksrc, (tag, "mv"), (tag, "rs")], [kdst])
    P.op(eng_aff, lambda e: e.tensor_tensor(out=dst[:], in0=dst[:], in1=gvec[:], op=ALU.mult),
         [kdst, "v0", "v1", "v2", "v3"], [kdst])
    P.op(eng_aff, lambda e: e.tensor_tensor(out=dst[:], in0=dst[:], in1=bvec[:], op=ALU.add),
         [kdst, "v0", "v1", "v2", "v3"], [kdst])


def phase5(nc, io, resname="xn", outname="xo"):
    MT = 256
    NMT = T // MT
    with ExitStack() as st:
        P = Prog(nc)
        w1 = sb(st, nc, "p5_w1", [128, 8, DFF], BF16)
        w2 = sb(st, nc, "p5_w2", [128, 32, D], BF16)
        wo = sb(st, nc, "p5_wo", [128, 8, D], BF16)
        g1 = sb(st, nc, "p5_g1", [128, D], F32)
        bb1 = sb(st, nc, "p5_bb1", [128, D], F32)
        g2 = sb(st, nc, "p5_g2", [128, D], F32)
        bb2 = sb(st, nc, "p5_bb2", [128, D], F32)
        b1t = sb(st, nc, "p5_b1t", [128, 32], F32)
        b2r = sb(st, nc, "p5_b2r", [1, D], BF16)
        ones1 = sb(st, nc, "p5_ones1", [1, 128], BF16)
        ident = sb(st, nc, "p5_ident", [128, 128], F32)
        hT = sb(st, nc, "p5_hT", [128, 32, MT], BF16)
        xs = [sb(st, nc, f"p5_xs{i}", [128, D], F32) for i in range(4)]
        mixin = [sb(st, nc, f"p5_mi{i}", [128, D], F32) for i in range(1)]
        mixT = [sb(st, nc, f"p5_mT{i}", [128, 8, 128], BF16) for i in range(1)]
        x1T = sb(st, nc, "p5_x1T", [128, 8, MT], BF16)
        rl = [sb(st, nc, f"p5_rl{i}", [128, MT], F32) for i in range(2)]
        stats = [sb(st, nc, f"p5_stats{i}", [128, 2, 6], F32) for i in range(2)]
        mv = [sb(st, nc, f"p5_mv{i}", [128, 2], F32) for i in range(2)]
        rstd = [sb(st, nc, f"p5_rstd{i}", [128, 1], F32) for i in range(2)]
        psT = [pst(st, nc, f"p5_psT{i}", [128, 512], F32) for i in range(2)]
        psM = [pst(st, nc, f"p5_psM{i}", [128, 512], F32) for i in range(2)]
        psF = [pst(st, nc, f"p5_psF{i}", [128, 512], F32) for i in range(4)]

        P.dma("sp", ident[:], io["ident"][:, :], [], ["ident"], "c0")
        P.dma_multi("pool", [(wo[:, k, :], io["w_out"][k * 128:(k + 1) * 128, :]) for k in range(8)],
                    [("wo", k) for k in range(8)], "wo")
        P.dma("sp", g1[:], io["ln1_g"].partition_broadcast(128), [], ["v0"], "c1")
        P.dma("sp", bb1[:], io["ln1_b"].partition_broadcast(128), [], ["v1"], "c2")
        P.dma("sp", g2[:], io["ln2_g"].partition_broadcast(128), [], ["v2"], "c3")
        P.dma("sp", bb2[:], io["ln2_b"].partition_broadcast(128), [], ["v3"], "c4")
        P.dma("sp", b1t[:], io["b1t"][:, :], [], ["b1t"], "c5")
        P.dma("pool", b2r[:], io["b2"].rearrange("(o d) -> o d", o=1), [], ["b2r"], "c6")
        P.op("pool", lambda e: e.memset(ones1[:], 1.0), [], ["ones1"])
        P.dma_multi("pool", [(w1[:, k, :], io["w1"][k * 128:(k + 1) * 128, :]) for k in range(8)],
                    [("w1", k) for k in range(8)], "w1")
        for q4 in range(4):
            P.dma_multi("pool", [(w2[:, k, :], io["w2"][k * 128:(k + 1) * 128, :]) for k in range(q4 * 8, q4 * 8 + 8)],
                        [("w2", k) for k in range(q4 * 8, q4 * 8 + 8)], f"w2{q4}")
        wok = [("wo", k) for k in range(8)]
        w1k = [("w1", k) for k in range(8)]
        w2k = [("w2", k) for k in range(32)]
        cnt = {"psF": 0, "rl": 0}

        def transposes(src, ksrc, dstT, kdst, col0):
            for hb in range(2):
                pT = psT[hb]

                def tr(e, hb=hb, pT=pT):
                    ins = None
                    for kk in range(4):
                        k = hb * 4 + kk
                        ins = e.transpose(out=pT[:, kk * 128:(kk + 1) * 128],
                                          in_=src[:, k * 128:(k + 1) * 128], identity=ident[:])
                    return ins
                P.op("pe", tr, (ksrc if isinstance(ksrc, list) else [ksrc]) + ["ident"], [("psT", hb)])
                o = dstT[:, hb * 4:(hb + 1) * 4, col0:col0 + 128]
                i = pT[:].rearrange("p (k t) -> p k t", k=4)
                if hb == 0:
                    P.op("act", lambda e, o=o, i=i: e.copy(out=o, in_=i), [("psT", hb)], [(kdst, hb)])
                else:
                    P.op("dve", lambda e, o=o, i=i: e.tensor_copy(out=o, in_=i), [("psT", hb)], [(kdst, hb)])

        def stage_a1(mt):
            for sub in range(2):
                t0 = mt * MT + sub * 128
                si = (mt % 2) * 2 + sub
                mi = 0
                x_ = xs[si]
                m_ = mixin[mi]
                P.dma("sp", x_[:], io[resname][t0:t0 + 128, :], [], [("xs", si)], f"xs{si}")
                P.dma("sp", m_[:, 0:512], io["og"][t0:t0 + 128, :], [], [("mi", mi, 0)], f"mi{mi}a")
                P.dma("sp", m_[:, 512:1024], io["ona"][t0:t0 + 128, :], [], [("mi", mi, 1)], f"mi{mi}b")
                transposes(m_, [("mi", mi, 0), ("mi", mi, 1)], mixT[mi], ("mT", mi), 0)
                for half in range(2):
                    pM = psM[half]

                    def mm(e, half=half, pM=pM, mi=mi):
                        ins = None
                        for k in range(8):
                            ins = e.matmul(pM[:, :], lhsT=mixT[mi][:, k, :],
                                           rhs=wo[:, k, half * 512:(half + 1) * 512],
                                           start=(k == 0), stop=(k == 7))
                        return ins
                    P.op("pe", mm, [(("mT", mi), 0), (("mT", mi), 1)] + wok, [("psM", half)])
                    P.op("dve", lambda e, half=half, pM=pM, x_=x_: e.scalar_tensor_tensor(
                        out=x_[:, half * 512:(half + 1) * 512], in0=x_[:, half * 512:(half + 1) * 512],
                        scalar=ALPHA, in1=pM[:, :], op0=ALU.mult, op1=ALU.add),
                        [("xs", si), ("psM", half)], [("xs", si)])
                ln_rows(P, "dve", x_, x_, stats[0], mv[0], rstd[0], g1, bb1, ("xs", si), ("xs", si), "ln1")

        def stage_a2(mt):
            for sub in range(2):
                si = (mt % 2) * 2 + sub
                transposes(xs[si], ("xs", si), x1T, ("x1T", sub), sub * 128)

        def stage_b(mt):
            xk = [(("x1T", s_), h_) for s_ in range(2) for h_ in range(2)]
            for fc in range(32):
                pi = cnt["psF"] % 4
                cnt["psF"] += 1
                pF = psF[pi]

                def mm(e, fc=fc, pF=pF):
                    ins = None
                    for k in range(8):
                        ins = e.matmul(pF[:, 0:MT], lhsT=w1[:, k, fc * 128:(fc + 1) * 128],
                                       rhs=x1T[:, k, :], start=(k == 0), stop=(k == 7))
                    return ins
                P.op("pe", mm, xk + w1k, [("psF", pi)])
                ri = cnt["rl"] % 2
                cnt["rl"] += 1
                r_ = rl[ri]
                P.op("act", lambda e, r_=r_, pF=pF, fc=fc: e.activation(
                    out=r_[:], in_=pF[:, 0:MT], func=AF.Relu, bias=b1t[:, fc:fc + 1]),
                    [("psF", pi), "b1t"], [("rl", ri)])
                eng = "dve"
                P.op(eng, lambda e, r_=r_, fc=fc: e.tensor_tensor(
                    out=hT[:, fc, :], in0=r_[:], in1=r_[:], op=ALU.mult),
                    [("rl", ri)], [("hT", fc)])

        def stage_c(mt):
            hk = [("hT", fc) for fc in range(32)]
            for sub in range(2):
                t0 = mt * MT + sub * 128
                si = (mt % 2) * 2 + sub
                x_ = xs[si]
                for half in range(2):
                    pM = psM[half]

                    def mm(e, half=half, pM=pM, sub=sub):
                        ins = None
                        for fc in range(32):
                            ins = e.matmul(pM[:, :], lhsT=hT[:, fc, sub * 128:(sub + 1) * 128],
                                           rhs=w2[:, fc, half * 512:(half + 1) * 512],
                                           start=(fc == 0), stop=False)
                        ins = e.matmul(pM[:, :], lhsT=ones1[:, :], rhs=b2r[:, half * 512:(half + 1) * 512],
                                       start=False, stop=True)
                        return ins
                    P.op("pe", mm, hk + w2k + ["ones1", "b2r"], [("psM", half)])
                    P.op("dve", lambda e, half=half, pM=pM, x_=x_: e.scalar_tensor_tensor(
                        out=x_[:, half * 512:(half + 1) * 512], in0=x_[:, half * 512:(half + 1) * 512],
                        scalar=ALPHA, in1=pM[:, :], op0=ALU.mult, op1=ALU.add),
                        [("xs", si), ("psM", half)], [("xs", si)])
                ln_rows(P, "dve", x_, x_, stats[1], mv[1], rstd[1], g2, bb2, ("xs", si), ("xs", si), "ln2")
                P.dma("sp", io[outname][t0:t0 + 128, :], x_[:], [("xs", si)], [("xo", t0 // 128)], f"xs{si}")

        stage_a1(0)
        stage_a2(0)
        for mt in range(NMT):
            stage_b(mt)
            if mt + 1 < NMT:
                stage_a1(mt + 1)
            stage_c(mt)
            if mt + 1 < NMT:
                stage_a2(mt + 1)
        P.op("sp", None, reads=[("xo", i) for i in range(NT)])
        P.emit("p5")


PAIRS = [[0, 1], [2, 3], [4, 5], [6, 7]]


def phase_halo(nc, io):
    with ExitStack() as st:
        P = Prog(nc)
        jm = sb(st, nc, "px_j", [128, 128], F32)
        selt = sb(st, nc, "px_sel", [128, 2], F32)
        pa = [sb(st, nc, f"px_a{i}", [128, D], F32) for i in range(2)]
        pb = [sb(st, nc, f"px_b{i}", [128, D], F32) for i in range(2)]
        ro = [sb(st, nc, f"px_r{i}", [128, D], F32) for i in range(2)]
        ps = [pst(st, nc, f"px_ps{i}", [128, 512], F32) for i in range(4)]
        P.dma("sp", jm[:], io["jmat"][:, :], [], ["jm"], "c0")
        P.dma("sp", selt[:], io["sel"].partition_broadcast(128), [], ["selt"], "c1")
        P.op("pool", lambda e: e.collective_compute("AllGather", ALU.bypass, replica_groups=PAIRS,
                                                    ins=[io["x1"][T - 256:T, :]], outs=[io["hg"][:, :]]),
             [], ["hg"])
        for i in range(2):
            P.dma("sp", pa[i][:], io["hg"][i * 128:(i + 1) * 128, :], ["hg"], [("pa", i)], f"pa{i}")
            P.dma("sp", pb[i][:], io["hg"][256 + i * 128:256 + (i + 1) * 128, :], ["hg"], [("pb", i)], f"pb{i}")
            P.op("dve", lambda e, i=i: e.tensor_scalar(out=pa[i][:], in0=pa[i][:], scalar1=selt[:, 0:1],
                                                       scalar2=None, op0=ALU.mult), [("pa", i), "selt"], [("pa", i)])
            P.op("dve", lambda e, i=i: e.scalar_tensor_tensor(out=pa[i][:], in0=pb[i][:], scalar=selt[:, 1:2],
                                                              in1=pa[i][:], op0=ALU.mult, op1=ALU.add),
                 [("pa", i), ("pb", i), "selt"], [("pa", i)])
            for hh in range(2):
                p_ = ps[i * 2 + hh]
                P.op("pe", lambda e, i=i, hh=hh, p_=p_: e.matmul(p_[:, :], lhsT=jm[:, :],
                                                                  rhs=pa[i][:, hh * 512:(hh + 1) * 512],
                                                                  start=True, stop=True),
                     [("pa", i), "jm"], [("ps", i, hh)])
                P.op("act", lambda e, i=i, hh=hh, p_=p_: e.copy(out=ro[i][:, hh * 512:(hh + 1) * 512], in_=p_[:, :]),
                     [("ps", i, hh)], [("ro", i, hh)])
            dst0 = T + (1 - i) * 128
            P.dma("sp", io["x1"][dst0:dst0 + 128, :], ro[i][:], [("ro", i, 0), ("ro", i, 1)], [("halo", i)], f"ro{i}")
        P.op("sp", None, reads=[("halo", 0), ("halo", 1)])
        P.emit("px")


SCRATCH = {
    "xn": ([T, D], F32), "hq": ([1536, 4100], F32), "qn": ([512, T], BF16),
    "kn": ([512, TH], BF16), "vn": ([TH, 512], BF16), "zs": ([T, 512], F32),
    "gb": ([T, 16], F32), "og": ([T, 512], F32), "ona": ([T, 512], F32),
    "xo": ([T, D], F32), "qg": ([512, T], BF16), "kg": ([512, T], BF16),
    "ktok": ([T, 512], BF16), "vtok": ([T, 512], BF16), "oa": ([T, 512], F32),
    "ob": ([T, 512], F32), "st_out": ([128, 512], F32), "sg": ([256, 512], F32),
    "x1": ([TH, D], F32), "hg": ([512, D], F32),
}
NPDT = {F32: np.float32}


def build(phases, in_specs, out_names, layer0=True):
    nc = bass.Bass("TRN2", target_bir_lowering=False)
    io = {}
    for name, (shape, dt) in in_specs.items():
        io[name] = nc.dram_tensor(name, list(shape), dt, kind="ExternalInput").ap()
    for name, (shape, dt) in SCRATCH.items():
        if name in io:
            continue
        kind = "ExternalOutput" if name in out_names else "Internal"
        io[name] = nc.dram_tensor(name, list(shape), dt, kind=kind).ap()
    for ph in phases:
        if ph == "p1":
            phase1(nc, io, layer0)
        if ph == "p5":
            phase5(nc, io)
        if ph == "p4":
            phase4(nc, io)
        if ph == "p3a":
            phase3(nc, io, 0, "p3a")
        if ph == "p3b":
            phase3(nc, io, 1, "p3b")
        if ph == "p3c":
            phase3c(nc, io)
        if ph == "p2":
            phase2(nc, io, range(0, 4), "p2a")
            phase2(nc, io, range(4, 8), "p2b")
    return nc


def core_bs(c):
    return c // 2, c % 2


def host_layer_inputs(inp, l, c, x_full_loc):
    b, s = core_bs(c)
    w_in = np.array(inp["w_in"][l], dtype=np.float32, copy=True)
    a_log = np.asarray(inp["a_log"][l], np.float32)
    dt_b = np.asarray(inp["dt_bias"][l], np.float32)
    convw = np.asarray(inp["conv_w"][l], np.float32)
    if s == 1:
        a = w_in[:, 2048:2056].copy()
        bb = w_in[:, 2056:2064].copy()
        w_in[:, 2048:2052] = a[:, 4:8]
        w_in[:, 2052:2056] = a[:, 0:4]
        w_in[:, 2056:2060] = bb[:, 4:8]
        w_in[:, 2060:2064] = bb[:, 0:4]
        a_log = a_log[::-1]
        dt_b = dt_b[::-1]
        convw = convw[::-1]
    d = {
        "x": None if x_full_loc is None else np.ascontiguousarray(x_full_loc),
        "w_in": w_in,
        "gpar": np.ascontiguousarray(np.concatenate([a_log.reshape(-1), dt_b.reshape(-1)])),
        "convw": np.ascontiguousarray(convw.T),
        "ident": np.eye(128, dtype=np.float32),
    }
    return d


def na_bias_tables(rpb_l, s):
    out = np.empty((3, 128, 8, 5, 128), np.float32)
    for vi, m in enumerate((0, 1, 5)):
        ks = min(max(m - 2, 0), 29)
        qi = m * 128 + np.arange(128)
        ki = ks * 128 + np.arange(640)
        tq = qi if s == 0 else 8191 - qi
        tk = ki if s == 0 else 8191 - ki
        rq, cq = tq // 64, tq % 64
        rk, ck = tk // 64, tk % 64
        r0 = np.clip(rq - 4, 0, 120)
        w0 = np.clip(cq - 8, 0, 48)
        valid = ((rk[:, None] >= r0[None, :]) & (rk[:, None] < r0[None, :] + 8) &
                 (ck[:, None] >= w0[None, :]) & (ck[:, None] < w0[None, :] + 16))
        dr = np.clip(rk[:, None] - rq[None, :] + 7, 0, 14)
        dc = np.clip(ck[:, None] - cq[None, :] + 15, 0, 30)
        b = rpb_l[:, dr, dc]
        b = np.where(valid[None], b, np.float32(NEG)).astype(np.float32)
        b = b[[0, 2, 4, 6, 1, 3, 5, 7]]
        out[vi] = b.reshape(8, 5, 128, 128).transpose(2, 0, 1, 3)
    return np.ascontiguousarray(out.reshape(3, 128, 8 * 5 * 128))


def gdn_consts():
    i = np.arange(128)
    lt = (i[:, None] <= i[None, :]).astype(np.float32)
    xs = (i[:, None] > i[None, :]).astype(np.float32)
    nti = np.where(i[None, :] < i[:, None], np.float32(NEG), np.float32(0))
    ns = np.where(i[:, None] <= i[None, :], np.float32(NEG), np.float32(0))
    d = {"LTA": lt, "XSA": xs, "NTIA": np.tile(nti, (1, 4)), "NSA": np.tile(ns, (1, 4)),
         "LTB": lt.T, "XSB": xs.T, "NTIB": np.tile(nti.T, (1, 4)), "NSB": np.tile(ns.T, (1, 4)),
         "I4f": np.tile(np.eye(128, dtype=np.float32), (1, 4))}
    return {k: np.ascontiguousarray(v, dtype=np.float32) for k, v in d.items()}


_BF = ml_dtypes.bfloat16

L1_OUT = ["xn", "zs", "gb", "qg", "kg", "ktok", "vtok", "oa", "ona", "st_out"]


def _specs(maps):
    sp = {}
    for k, v in maps[0].items():
        sp[k] = (list(v.shape), BF16 if v.dtype == _BF else F32)
    return sp


def _run(phases, maps, outs, layer0=True):
    nc = build(phases, _specs(maps), outs, layer0=layer0)
    res = run_bass_kernel_spmd(nc, maps, core_ids=list(range(8)))
    return res.results


def kernel_unfused(x, ln_in_g, ln_in_b, w_in, conv_w, a_log, dt_bias, gdn_norm_g, rpb, na_norm_g,
                   w_out, ln1_g, ln1_b, w1, b1, w2, b2, ln2_g, ln2_b):
    inp = dict(x=x, w_in=w_in, conv_w=conv_w, a_log=a_log, dt_bias=dt_bias)
    x = np.asarray(x, np.float32)
    consts = gdn_consts()
    ident = np.eye(128, dtype=np.float32)
    zeros_st = np.zeros((128, 512), np.float32)
    xloc = []
    for c in range(8):
        b, s = core_bs(c)
        xs = x[b] if s == 0 else x[b][::-1]
        xloc.append(np.ascontiguousarray(xs[:TH]))
    for l in range(2):
        maps = []
        for c in range(8):
            b, s = core_bs(c)
            m = host_layer_inputs(inp, l, c, xloc[c])
            if l == 0:
                m["ln_in_g"] = np.asarray(ln_in_g, np.float32)
                m["ln_in_b"] = np.asarray(ln_in_b, np.float32)
            m["nab"] = na_bias_tables(np.asarray(rpb[l], np.float32), s)
            m["na_g"] = np.asarray(na_norm_g[l], np.float32)
            for k in ("LTA", "XSA", "NTIA", "NSA", "I4f"):
                m[k] = consts[k]
            m["st_in"] = zeros_st
            maps.append(m)
        outs1 = [o for o in L1_OUT if not (o == "xn" and l > 0)]
        r1 = _run(["p1", "p2", "p4", "p3a"], maps, outs1, layer0=(l == 0))
        maps2 = []
        for c in range(8):
            r = r1[c]
            m = {k: np.ascontiguousarray(r[k]) for k in ("qg", "kg", "ktok", "vtok", "gb", "oa", "zs", "ona")}
            m["xn"] = np.ascontiguousarray(r["xn"]) if l == 0 else np.ascontiguousarray(xloc[c][:T])
            m["st_in"] = np.ascontiguousarray(r1[c ^ 1]["st_out"])
            for k in ("LTB", "XSB", "NTIB", "NSB", "I4f"):
                m[k] = consts[k]
            m["ident"] = ident
            m["gdn_g"] = np.asarray(gdn_norm_g[l], np.float32)
            m["w_out"] = np.asarray(w_out[l], np.float32)
            m["w1"] = np.asarray(w1[l], np.float32)
            m["w2"] = np.asarray(w2[l], np.float32)
            m["b1t"] = np.ascontiguousarray(np.asarray(b1[l], np.float32).reshape(32, 128).T)
            m["b2"] = np.asarray(b2[l], np.float32)
            m["ln1_g"] = np.asarray(ln1_g[l], np.float32)
            m["ln1_b"] = np.asarray(ln1_b[l], np.float32)
            m["ln2_g"] = np.asarray(ln2_g[l], np.float32)
            m["ln2_b"] = np.asarray(ln2_b[l], np.float32)
            maps2.append(m)
        r2 = _run(["p3b", "p3c", "p5"], maps2, ["xo"])
        xo = [np.asarray(r2[c]["xo"], np.float32) for c in range(8)]
        xloc = [np.ascontiguousarray(np.concatenate([xo[c], xo[c ^ 1][::-1][:TH - T]], 0)) for c in range(8)]
    out = np.empty((4, 8192, D), np.float32)
    for c in range(8):
        b, s = core_bs(c)
        if s == 0:
            out[b, :T] = xloc[c][:T]
        else:
            out[b, T:] = xloc[c][:T][::-1]
    return out


LAYER_IN = ["w_in", "gpar", "convw", "nab", "na_g", "gdn_g", "w_out", "w1", "w2", "b1t", "b2",
            "ln1_g", "ln1_b", "ln2_g", "ln2_b"]


def build_fused(in_specs):
    nc = bass.Bass("TRN2", target_bir_lowering=False)
    io = {}
    for name, (shape, dt) in in_specs.items():
        io[name] = nc.dram_tensor(name, list(shape), dt, kind="ExternalInput").ap()
    for name, (shape, dt) in SCRATCH.items():
        io[name] = nc.dram_tensor(name, list(shape), dt, kind="Internal").ap()
    io["out"] = nc.dram_tensor("out", [T, D], F32, kind="ExternalOutput").ap()
    for l in range(2):
        iol = dict(io)
        for k in LAYER_IN:
            iol[k] = io[f"{k}_l{l}"]
        phase1(nc, iol, l == 0, "x" if l == 0 else "x1")
        phase2(nc, iol, range(0, 4), f"p2a{l}")
        phase2(nc, iol, range(4, 8), f"p2b{l}")
        phase4(nc, iol)
        phase3(nc, iol, 0, f"p3a{l}", fused=True)
        phase3(nc, iol, 1, f"p3b{l}", fused=True)
        phase3c(nc, iol)
        if l == 0:
            phase5(nc, iol, "xn", "x1")
            phase_halo(nc, iol)
        else:
            phase5(nc, iol, "x1", "out")
    return nc


def kernel(x, ln_in_g, ln_in_b, w_in, conv_w, a_log, dt_bias, gdn_norm_g, rpb, na_norm_g,
           w_out, ln1_g, ln1_b, w1, b1, w2, b2, ln2_g, ln2_b):
    inp = dict(w_in=w_in, conv_w=conv_w, a_log=a_log, dt_bias=dt_bias)
    x = np.asarray(x, np.float32)
    consts = gdn_consts()
    f = lambda a: np.ascontiguousarray(np.asarray(a, np.float32))
    maps = []
    for c in range(8):
        b, s = core_bs(c)
        xs = x[b] if s == 0 else x[b][::-1]
        m = {"x": np.ascontiguousarray(xs[:TH]), "ln_in_g": f(ln_in_g), "ln_in_b": f(ln_in_b),
             "ident": np.eye(128, dtype=np.float32), "jmat": np.ascontiguousarray(np.eye(128, dtype=np.float32)[::-1]),
             "sel": np.array([1.0, 0.0] if s == 1 else [0.0, 1.0], np.float32)}
        m.update(consts)
        for l in range(2):
            hl = host_layer_inputs(inp, l, c, None)
            m[f"w_in_l{l}"] = hl["w_in"]
            m[f"gpar_l{l}"] = hl["gpar"]
            m[f"convw_l{l}"] = hl["convw"]
            m[f"nab_l{l}"] = na_bias_tables(f(rpb[l]), s)
            m[f"na_g_l{l}"] = f(na_norm_g[l])
            m[f"gdn_g_l{l}"] = f(gdn_norm_g[l])
            m[f"w_out_l{l}"] = f(w_out[l])
            m[f"w1_l{l}"] = f(w1[l])
            m[f"w2_l{l}"] = f(w2[l])
            m[f"b1t_l{l}"] = np.ascontiguousarray(f(b1[l]).reshape(32, 128).T)
            m[f"b2_l{l}"] = f(b2[l])
            m[f"ln1_g_l{l}"] = f(ln1_g[l])
            m[f"ln1_b_l{l}"] = f(ln1_b[l])
            m[f"ln2_g_l{l}"] = f(ln2_g[l])
            m[f"ln2_b_l{l}"] = f(ln2_b[l])
        maps.append(m)
    nc = build_fused(_specs(maps))
    res = run_bass_kernel_spmd(nc, maps, core_ids=list(range(8)))
    out = np.empty((4, 8192, D), np.float32)
    for c in range(8):
        b, s = core_bs(c)
        o = np.asarray(res.results[c]["out"], np.float32)
        if s == 0:
            out[b, :T] = o
        else:
            out[b, T:] = o[::-1]
    return out
```

```python
import numpy as np
import ml_dtypes
from contextlib import ExitStack
import concourse.bass as bass
import concourse.mybir as mybir
from concourse.bass_utils import run_bass_kernel_spmd

F32 = mybir.dt.float32
BF16 = mybir.dt.bfloat16
ALU = mybir.AluOpType
AF = mybir.ActivationFunctionType
AX = mybir.AxisListType

D = 1024
T = 4096
TH = 4352
NT = 32
DIN = 3600
DFF = 4096
ALPHA = 4.0 ** 0.25
LN_EPS = 1e-5
RMS_EPS = 1e-6
NEG = -30000.0


class Prog:
    ENGS = ("pe", "act", "dve", "pool", "sp")

    def __init__(self, nc):
        self.nc = nc
        self.ops = {e: [] for e in self.ENGS}
        self.cnt = {}
        self.seen = {e: {} for e in self.ENGS}
        self.W = {}
        self.R = {}
        self.awaited = {}
        self._cap = None

    def capture(self, fn):
        saved, self._cap = self._cap, []
        fn()
        lst, self._cap = self._cap, saved
        return lst

    def replay_interleaved(self, a, b):
        if not b:
            for o in a:
                self.op(*o)
            return
        stride = max(1, len(a) // len(b))
        bi = 0
        for i, o in enumerate(a):
            self.op(*o)
            if (i + 1) % stride == 0 and bi < len(b):
                self.op(*b[bi])
                bi += 1
        for o in b[bi:]:
            self.op(*o)

    def op(self, eng, fn, reads=(), writes=(), dma=None):
        if self._cap is not None:
            self._cap.append((eng, fn, tuple(reads), tuple(writes), dma))
            return
        assert fn is not None or not writes
        own = "eng:" + eng
        waits = {}

        def merge(d, skip_own):
            if d:
                for s, c in d.items():
                    if skip_own and s == own and not dma:
                        continue
                    if waits.get(s, 0) < c:
                        waits[s] = c

        for k in reads:
            merge(self.W.get(k), False)
        for k in writes:
            merge(self.W.get(k), True)
            merge(self.R.get(k), True)
        sem = ("dma:" + dma) if dma else own
        for s, c in waits.items():
            if self.seen[eng].get(s, 0) >= c:
                continue
            self.seen[eng][s] = c
            self.ops[eng].append(("wait", s, c))
            self.awaited.setdefault(s, set()).add(c)
        n = self.cnt.get(sem, 0) + 1
        self.cnt[sem] = n
        self.ops[eng].append(("op", fn, sem, n))
        for k in reads:
            self.R.setdefault(k, {})[sem] = n
        for k in writes:
            self.W[k] = {sem: n}
            self.R[k] = {}

    def dma(self, eng, out, in_, reads, writes, group):
        self.op(eng, lambda e: e.dma_start(out=out, in_=in_), reads=reads,
                writes=writes, dma=group)

    def dma_multi(self, eng, pairs, keys, group):
        for (out, in_), k in zip(pairs, keys):
            self.dma(eng, out, in_, [], [k], group)
        sem = "dma:" + group
        for k in keys:
            self.W[k] = {sem: self.cnt[sem]}

    def emit(self, name):
        nc = self.nc
        cmap = {}
        for s, cs in self.awaited.items():
            for i, c in enumerate(sorted(cs)):
                cmap[(s, c)] = i + 1
        allsems = set(self.awaited)
        for s, n in self.cnt.items():
            if s.startswith("dma:"):
                allsems.add(s)
                for c in range(1, n + 1):
                    cmap[(s, c)] = c
        with ExitStack() as st:
            st.enter_context(nc.cleanup_on_exit())
            sems = {}
            for s in sorted(allsems):
                _UID[0] += 1
                sems[s] = nc.alloc_semaphore(name=f"{name}_{_UID[0]}_" + s.replace(":", "_"))
            block = st.enter_context(nc.Block())

            def make(ename):
                def body(e):
                    for o in self.ops[ename]:
                        if o[0] == "wait":
                            _, s, c = o
                            mult = 16 if s.startswith("dma:") else 1
                            e.wait_ge(sems[s], cmap[(s, c)] * mult)
                        else:
                            _, fn, s, n = o
                            if fn is None:
                                continue
                            ins = fn(e)
                            if (s, n) in cmap:
                                ins.then_inc(
                                    sems[s], 16 if s.startswith("dma:") else 1)
                return body

            block.tensor(make("pe"))
            block.scalar(make("act"))
            block.vector(make("dve"))
            block.gpsimd(make("pool"))
            block.sync(make("sp"))


_UID = [0]


def sb(st, nc, name, shape, dt):
    _UID[0] += 1
    return st.enter_context(nc.sbuf_tensor(f"{name}_u{_UID[0]}", list(shape), dt))


def pst(st, nc, name, shape, dt):
    _UID[0] += 1
    return st.enter_context(nc.psum_tensor(f"{name}_u{_UID[0]}", list(shape), dt))


def phase1(nc, io, layer0, xname="x"):
    x_d = io[xname]
    with ExitStack() as st:
        P = Prog(nc)
        win = sb(st, nc, "p1_win", [128, 8, DIN], BF16)
        ident = sb(st, nc, "p1_ident", [128, 128], F32)
        lng = sb(st, nc, "p1_lng", [128, D], F32)
        lnb = sb(st, nc, "p1_lnb", [128, D], F32)
        gpar = sb(st, nc, "p1_gpar", [128, 16], F32)
        nea = sb(st, nc, "p1_nea", [128, 8], F32)
        zpad = sb(st, nc, "p1_zpad", [128, 2], F32)
        xt = [sb(st, nc, f"p1_xt{i}", [128, D], F32) for i in range(3)]
        xn = [sb(st, nc, f"p1_xn{i}", [128, D], F32) for i in range(2)]
        stats = sb(st, nc, "p1_stats", [128, 2, 6], F32)
        mv = sb(st, nc, "p1_mv", [128, 2], F32)
        rstd = sb(st, nc, "p1_rstd", [128, 1], F32)
        xT = [sb(st, nc, f"p1_xT{i}", [128, 8, 512], BF16) for i in range(2)]
        fo = [sb(st, nc, f"p1_fo{i}", [128, 512], F32) for i in range(3)]
        fob = [sb(st, nc, f"p1_fob{i}", [128, 512], BF16) for i in range(3)]
        to = [sb(st, nc, f"p1_to{i}", [128, 512], F32) for i in range(2)]
        tob = [sb(st, nc, f"p1_tob{i}", [128, 512], BF16) for i in range(2)]
        ab = [sb(st, nc, f"p1_ab{i}", [128, 16], F32) for i in range(2)]
        abt = [sb(st, nc, f"p1_abt{i}", [128, 16], F32) for i in range(2)]
        psT = [pst(st, nc, f"p1_psT{i}", [128, 512], F32) for i in range(2)]
        psF = [pst(st, nc, f"p1_psF{i}", [128, 512], F32) for i in range(3)]
        psK = [pst(st, nc, f"p1_psK{i}", [128, 512], F32) for i in range(2)]
        psA = pst(st, nc, "p1_psA", [128, 16], F32)

        for k in range(8):
            P.dma("pool", win[:, k, :], io["w_in"][k * 128:(k + 1) * 128, :],
                  [], [("win", k)], f"win{k}")
        P.dma("sp", ident[:], io["ident"][:, :], [], ["ident"], "c0")
        if layer0:
            P.dma("sp", lng[:], io["ln_in_g"].partition_broadcast(128), [], ["lng"], "c2")
            P.dma("sp", lnb[:], io["ln_in_b"].partition_broadcast(128), [], ["lnb"], "c3")
        P.dma("sp", gpar[:], io["gpar"].partition_broadcast(128), [], ["gpar"], "c4")
        P.op("act", lambda e: e.activation(out=nea[:], in_=gpar[:, 0:8], func=AF.Exp),
             ["gpar"], ["nea0"])
        P.op("dve", lambda e: e.tensor_scalar(out=nea[:], in0=nea[:], scalar1=-1.0,
                                              scalar2=None, op0=ALU.mult),
             ["nea0"], ["nea"])
        P.op("pool", lambda e: e.memset(zpad[:], 0.0), [], ["zpad"])
        for c in range(12):
            P.dma("sp", io["hq"][c * 128:(c + 1) * 128, 0:2], zpad[:], ["zpad"],
                  [("hq", c, -1)], "zp")

        nmac = 9
        cnt = {"xt": 0, "xn": 0, "fo": 0, "to": 0, "psF": 0, "psK": 0, "ab": 0}
        for mt in range(nmac):
            halo = mt == 8
            nsub = 2 if halo else 4
            ntok = nsub * 128
            tok0 = mt * 512
            xTm = xT[mt % 2]
            kxT = ("xT", mt % 2)
            for sub in range(nsub):
                t0 = tok0 + sub * 128
                xi = cnt["xt"] % 3
                cnt["xt"] += 1
                xtile = xt[xi]
                P.dma("sp", xtile[:], x_d[t0:t0 + 128, :], [], [("xt", xi)], f"xt{xi}")
                src = xtile
                ksrc = ("xt", xi)
                if layer0:
                    ni = cnt["xn"] % 2
                    cnt["xn"] += 1
                    xnt = xn[ni]
                    for hh in range(2):
                        P.op("dve", lambda e, hh=hh, xtile=xtile: e.bn_stats(
                            out=stats[:, hh, :], in_=xtile[:, hh * 512:(hh + 1) * 512]),
                            [("xt", xi)], [("stats", hh)])
                    P.op("dve", lambda e: e.bn_aggr(out=mv[:], in_=stats[:].rearrange("p a b -> p (a b)")),
                         [("stats", 0), ("stats", 1)], ["mv"])
                    P.op("act", lambda e: e.activation(
                        out=rstd[:], in_=mv[:, 1:2], func=AF.Ln, bias=LN_EPS), ["mv"], ["rstd"])
                    P.op("act", lambda e: e.activation(
                        out=rstd[:], in_=rstd[:], func=AF.Exp, scale=-0.5), ["rstd"], ["rstd"])
                    P.op("dve", lambda e, xtile=xtile, xnt=xnt: e.tensor_scalar(
                        out=xnt[:], in0=xtile[:], scalar1=mv[:, 0:1], scalar2=rstd[:, 0:1],
                        op0=ALU.subtract, op1=ALU.mult),
                        [("xt", xi), "mv", "rstd"], [("xn", ni)])
                    P.op("dve", lambda e, xnt=xnt: e.tensor_tensor(
                        out=xnt[:], in0=xnt[:], in1=lng[:], op=ALU.mult),
                        [("xn", ni), "lng"], [("xn", ni)])
                    P.op("dve", lambda e, xnt=xnt: e.tensor_tensor(
                        out=xnt[:], in0=xnt[:], in1=lnb[:], op=ALU.add),
                        [("xn", ni), "lnb"], [("xn", ni)])
                    if not halo:
                        P.dma("sp", io["xn"][t0:t0 + 128, :], xnt[:], [("xn", ni)],
                              [("xn_d", t0 // 128)], f"xns{ni}")
                    src = xnt
                    ksrc = ("xn", ni)
                idm, kid = ident, "ident"
                for hb in range(2):
                    pT = psT[hb]

                    def tr(e, hb=hb, pT=pT, src=src, idm=idm):
                        ins = None
                        for kk in range(4):
                            k = hb * 4 + kk
                            ins = e.transpose(out=pT[:, kk * 128:(kk + 1) * 128],
                                              in_=src[:, k * 128:(k + 1) * 128],
                                              identity=idm[:])
                        return ins
                    P.op("pe", tr, [ksrc, kid], [("psT", hb)])
                    outap = xTm[:, hb * 4:(hb + 1) * 4, sub * 128:(sub + 1) * 128]
                    inap = pT[:].rearrange("p (k t) -> p k t", k=4)
                    if hb == 0:
                        P.op("act", lambda e, o=outap, i=inap: e.copy(out=o, in_=i),
                             [("psT", hb)], [(kxT, sub, hb)])
                    else:
                        P.op("dve", lambda e, o=outap, i=inap: e.tensor_copy(out=o, in_=i),
                             [("psT", hb)], [(kxT, sub, hb)])
            xkeys = [(kxT, s_, h_) for s_ in range(nsub) for h_ in range(2)]
            wkeys = [("win", k) for k in range(8)]

            fm = [(c * 128, "hq", c) for c in range(12)]
            if not halo:
                fm += [(2064 + c * 128, "q", c) for c in range(4)]
            fm += [(2576 + c * 128, "k", c) for c in range(4)]
            for (col, kind, c) in fm:
                n = ntok
                if halo and kind == "hq":
                    n = 2
                pi = cnt["psF"] % 3
                cnt["psF"] += 1
                pF = psF[pi]

                def mm(e, col=col, n=n, pF=pF, xTm=xTm):
                    ins = None
                    for k in range(8):
                        ins = e.matmul(pF[:, 0:n], lhsT=win[:, k, col:col + 128],
                                       rhs=xTm[:, k, 0:n], start=(k == 0), stop=(k == 7))
                    return ins
                P.op("pe", mm, xkeys + wkeys, [("psF", pi)])
                fi = cnt["fo"] % 3
                cnt["fo"] += 1
                if kind == "hq":
                    fbuf = fo[fi]
                    P.op("act", lambda e, fbuf=fbuf, pF=pF, n=n: e.copy(out=fbuf[:, 0:n], in_=pF[:, 0:n]),
                         [("psF", pi)], [("fo", fi)])
                    P.dma("sp", io["hq"][c * 128:(c + 1) * 128, 2 + tok0:2 + tok0 + n],
                          fbuf[:, 0:n], [("fo", fi)], [("hq", c, mt)], f"fo{fi}")
                else:
                    fbuf = fob[fi]
                    sc = 0.125 if kind == "q" else 1.0
                    P.op("act", lambda e, fbuf=fbuf, pF=pF, n=n, sc=sc: e.activation(
                        out=fbuf[:, 0:n], in_=pF[:, 0:n], func=AF.Copy, scale=sc),
                        [("psF", pi)], [("fob", fi)])
                    dst = io["qn"] if kind == "q" else io["kn"]
                    P.dma("sp", dst[c * 128:(c + 1) * 128, tok0:tok0 + n], fbuf[:, 0:n],
                          [("fob", fi)], [(kind + "n", c, mt)], f"fob{fi}")

            for sub in range(nsub):
                t0 = tok0 + sub * 128
                tk = [("v", 3088)]
                if not halo:
                    tk = [("z", 1536), ("v", 3088), ("ab", 2048)]
                for (kind, col) in tk:
                    if kind == "ab":
                        def mm(e, sub=sub, xTm=xTm):
                            ins = None
                            for k in range(8):
                                ins = e.matmul(psA[:, 0:16], lhsT=xTm[:, k, sub * 128:(sub + 1) * 128],
                                               rhs=win[:, k, 2048:2064], start=(k == 0), stop=(k == 7))
                            return ins
                        P.op("pe", mm, xkeys + wkeys, ["psA"])
                        ai = cnt["ab"] % 2
                        cnt["ab"] += 1
                        a_, at_ = ab[ai], abt[ai]
                        P.op("dve", lambda e, at_=at_: e.tensor_copy(out=at_[:, 8:16], in_=psA[:, 8:16]),
                             ["psA"], [("abt2", ai)])
                        P.op("dve", lambda e, at_=at_: e.tensor_tensor(
                            out=at_[:, 0:8], in0=psA[:, 0:8], in1=gpar[:, 8:16], op=ALU.add),
                            ["psA", "gpar"], [("abt", ai)])
                        P.op("act", lambda e, at_=at_: e.activation(
                            out=at_[:, 8:16], in_=at_[:, 8:16], func=AF.Exp, scale=-1.0),
                            [("abt2", ai)], [("abt2", ai)])
                        P.op("act", lambda e, at_=at_: e.activation(
                            out=at_[:, 0:8], in_=at_[:, 0:8], func=AF.Exp),
                            [("abt", ai)], [("abt", ai)])
                        P.op("act", lambda e, at_=at_: e.activation(
                            out=at_[:, 0:8], in_=at_[:, 0:8], func=AF.Ln, bias=1.0),
                            [("abt", ai)], [("abt", ai)])
                        P.op("dve", lambda e, at_=at_, a_=a_: e.tensor_tensor(
                            out=a_[:, 0:8], in0=at_[:, 0:8], in1=nea[:], op=ALU.mult),
                            [("abt", ai), "nea"], [("ab", ai)])
                        P.op("dve", lambda e, at_=at_: e.tensor_scalar(
                            out=at_[:, 8:16], in0=at_[:, 8:16], scalar1=1.0, scalar2=None,
                            op0=ALU.add), [("abt2", ai)], [("abt2", ai)])
                        P.op("dve", lambda e, at_=at_, a_=a_: e.reciprocal(
                            out=a_[:, 8:16], in_=at_[:, 8:16]),
                            [("abt2", ai), ("ab", ai)], [("ab", ai)])
                        P.dma("sp", io["gb"][t0:t0 + 128, :], a_[:], [("ab", ai)],
                              [("gb", t0 // 128)], f"ab{ai}")
                        continue
                    pi = cnt["psK"] % 2
                    cnt["psK"] += 1
                    pK = psK[pi]

                    def mm(e, sub=sub, xTm=xTm, pK=pK, col=col):
                        ins = None
                        for k in range(8):
                            ins = e.matmul(pK[:, :], lhsT=xTm[:, k, sub * 128:(sub + 1) * 128],
                                           rhs=win[:, k, col:col + 512], start=(k == 0), stop=(k == 7))
                        return ins
                    P.op("pe", mm, xkeys + wkeys, [("psK", pi)])
                    ti = cnt["to"] % 2
                    cnt["to"] += 1
                    if kind == "z":
                        tb = to[ti]
                        P.op("act", lambda e, tb=tb, pK=pK: e.copy(out=tb[:], in_=pK[:]),
                             [("psK", pi)], [("to", ti)])
                        P.dma("sp", io["zs"][t0:t0 + 128, :], tb[:], [("to", ti)],
                              [("zs", t0 // 128)], f"to{ti}")
                    else:
                        tb = tob[ti]
                        P.op("dve", lambda e, tb=tb, pK=pK: e.tensor_copy(out=tb[:], in_=pK[:]),
                             [("psK", pi)], [("tob", ti)])
                        P.dma("sp", io["vn"][t0:t0 + 128, :], tb[:], [("tob", ti)],
                              [("vn", t0 // 128)], f"tob{ti}")
        allk = [k for k in P.W if isinstance(k, tuple) and k[0] in
                ("hq", "qn", "kn", "vn", "zs", "gb", "xn_d")]
        P.op("sp", None, reads=allk)
        P.emit("p1")


def phase2(nc, io, mts=range(8), tag="p2"):
    with ExitStack() as st:
        P = Prog(nc)
        cw = sb(st, nc, "p2_cw", [128, 12, 5], F32)
        onesb = sb(st, nc, "p2_ones", [128, 128], BF16)
        identb = sb(st, nc, "p2_identb", [128, 128], BF16)
        hw = [sb(st, nc, f"p2_hw{i}", [128, 516], F32) for i in range(3)]
        acc = [sb(st, nc, f"p2_acc{i}", [128, 512], F32) for i in range(2)]
        ptmp = sb(st, nc, "p2_ptmp", [128, 512], F32)
        sil = [sb(st, nc, f"p2_sil{i}", [128, 512], F32) for i in range(12)]
        sqb = [sb(st, nc, f"p2_sq{i}", [128, 512], BF16) for i in range(8)]
        lnv = [sb(st, nc, f"p2_ln{i}", [128, 512], F32) for i in range(8)]
        nb = [sb(st, nc, f"p2_nb{i}", [128, 512], BF16) for i in range(12)]
        tk = [sb(st, nc, f"p2_tk{i}", [128, 512], BF16) for i in range(2)]
        psN = [pst(st, nc, f"p2_psN{i}", [128, 512], F32) for i in range(4)]
        psT = [pst(st, nc, f"p2_psT{i}", [128, 512], BF16) for i in range(2)]
        P.dma_multi("sp", [(cw[:, c, :], io["convw"][c * 128:(c + 1) * 128, :]) for c in range(12)],
                    [("cw", c) for c in range(12)], "c0")
        P.op("pool", lambda e: e.memset(onesb[:], 1.0), [], ["ones"])
        P.dma("pool", identb[:], io["ident"][:, :], [], ["identb"], "c1")
        cnt = {"hw": 0, "acc": 0, "sq": 0, "psN": 0, "psT": 0, "tk": 0}
        stop = ""
        for mt in mts:
            tok0 = mt * 512
            for c in range(12):
                hi = cnt["hw"] % 3
                cnt["hw"] += 1
                h_ = hw[hi]
                P.dma("sp", h_[:], io["hq"][c * 128:(c + 1) * 128, tok0:tok0 + 516], [], [("hw", hi)], f"hw{hi}")
                ai = cnt["acc"] % 2
                cnt["acc"] += 1
                a_ = acc[ai]
                eng = "dve"
                P.op(eng, lambda e, a_=a_, h_=h_, c=c: e.tensor_scalar(
                    out=a_[:], in0=h_[:, 0:512], scalar1=cw[:, c, 0:1], scalar2=None, op0=ALU.mult),
                    [("hw", hi), ("cw", c)], [("acc", ai)])
                for i in range(1, 5):
                    if eng == "dve":
                        P.op(eng, lambda e, a_=a_, h_=h_, c=c, i=i: e.scalar_tensor_tensor(
                            out=a_[:], in0=h_[:, i:i + 512], scalar=cw[:, c, i:i + 1], in1=a_[:],
                            op0=ALU.mult, op1=ALU.add), [("hw", hi), ("cw", c), ("acc", ai)], [("acc", ai)])
                    else:
                        P.op(eng, lambda e, h_=h_, c=c, i=i: e.tensor_scalar(
                            out=ptmp[:], in0=h_[:, i:i + 512], scalar1=cw[:, c, i:i + 1], scalar2=None,
                            op0=ALU.mult), [("hw", hi), ("cw", c)], ["ptmp"])
                        P.op(eng, lambda e, a_=a_: e.tensor_tensor(out=a_[:], in0=a_[:], in1=ptmp[:], op=ALU.add),
                             ["ptmp", ("acc", ai)], [("acc", ai)])
                if c < 8:
                    P.op("act", lambda e, a_=a_, c=c: e.activation(out=sil[c][:], in_=a_[:], func=AF.Silu),
                         [("acc", ai)], [("sil", c)])
                else:
                    P.op("act", lambda e, a_=a_, c=c: e.activation(out=nb[c][:], in_=a_[:], func=AF.Silu),
                         [("acc", ai)], [("nb", c)])
            if stop == "s1":
                break
            for c in range(8):
                P.op("act", lambda e, c=c: e.activation(out=sqb[c][:], in_=sil[c][:], func=AF.Square),
                     [("sil", c)], [("sq", c)])
            for c in range(8):
                si = c
                pi = cnt["psN"] % 4
                cnt["psN"] += 1
                pN = psN[pi]
                P.op("pe", lambda e, pN=pN, si=si: e.matmul(pN[:, :], lhsT=onesb[:, :], rhs=sqb[si][:, :],
                                                             start=True, stop=True),
                     [("sq", si), "ones"], [("psN", pi)])
                P.op("act", lambda e, pN=pN, c=c: e.activation(out=lnv[c][:], in_=pN[:], func=AF.Ln, bias=RMS_EPS),
                     [("psN", pi)], [("ln", c)])
            if stop == "s2":
                break
            for c in range(12):
                if c < 8:
                    P.op("act", lambda e, c=c: e.activation(out=lnv[c][:], in_=lnv[c][:], func=AF.Exp, scale=-0.5),
                         [("ln", c)], [("ln", c)])
                    sc = 128.0 ** -0.5 if c < 4 else 1.0
                    P.op("dve", lambda e, c=c, sc=sc: e.scalar_tensor_tensor(
                        out=nb[c][:], in0=sil[c][:], scalar=sc, in1=lnv[c][:], op0=ALU.mult, op1=ALU.mult),
                        [("sil", c), ("ln", c)], [("nb", c)])
                    dst = io["qg"] if c < 4 else io["kg"]
                    cc = c % 4
                    P.dma("sp", dst[cc * 128:(cc + 1) * 128, tok0:tok0 + 512], nb[c][:], [("nb", c)],
                          [("qkg", c, mt)], f"nb{c}")
            if stop == "s3":
                break
            for kind, base, dname in (("k", 4, "ktok"), ("v", 8, "vtok")):
                for sub in range(4):
                    pi = cnt["psT"] % 2
                    cnt["psT"] += 1
                    pT = psT[pi]

                    def tr(e, pT=pT, base=base, sub=sub):
                        ins = None
                        for h in range(4):
                            ins = e.transpose(out=pT[:, h * 128:(h + 1) * 128],
                                              in_=nb[base + h][:, sub * 128:(sub + 1) * 128],
                                              identity=identb[:])
                        return ins
                    P.op("pe", tr, [("nb", base + h) for h in range(4)] + ["identb"], [("psT", pi)])
                    ti = cnt["tk"] % 2
                    cnt["tk"] += 1
                    if ti == 0:
                        P.op("act", lambda e, pT=pT, ti=ti: e.copy(out=tk[ti][:], in_=pT[:]),
                             [("psT", pi)], [("tk", ti)])
                    else:
                        P.op("dve", lambda e, pT=pT, ti=ti: e.tensor_copy(out=tk[ti][:], in_=pT[:]),
                             [("psT", pi)], [("tk", ti)])
                    t0 = tok0 + sub * 128
                    P.dma("sp", io[dname][t0:t0 + 128, :], tk[ti][:], [("tk", ti)],
                          [(dname, t0 // 128)], f"tk{ti}")
        allk = [k for k in P.W if isinstance(k, tuple) and k[0] in ("qkg", "ktok", "vtok")]
        P.op("sp", None, reads=allk)
        P.emit(tag)


def phase3(nc, io, dirn, tag, fused=False):
    sfx = "A" if dirn == 0 else "B"
    with ExitStack() as st:
        P = Prog(nc)
        LT = sb(st, nc, "p3_LT", [128, 128], F32)
        XS = sb(st, nc, "p3_XS", [128, 128], F32)
        NTI = sb(st, nc, "p3_NTI", [128, 512], F32)
        NS = sb(st, nc, "p3_NS", [128, 512], F32)
        ident = sb(st, nc, "p3_ident", [128, 128], F32)
        identb = sb(st, nc, "p3_identb", [128, 128], BF16)
        I4f = sb(st, nc, "p3_I4f", [128, 512], F32)
        ones = sb(st, nc, "p3_ones", [128, 128], F32)
        S4 = sb(st, nc, "p3_S4", [128, 512], F32)
        Sbf = sb(st, nc, "p3_Sbf", [128, 512], BF16)
        NS_ = 2
        qT4 = [sb(st, nc, f"p3_qT{i}", [128, 4, 128], BF16) for i in range(NS_)]
        kT4 = [sb(st, nc, f"p3_kT{i}", [128, 4, 128], BF16) for i in range(NS_)]
        kt4 = [sb(st, nc, f"p3_kt{i}", [128, 4, 128], BF16) for i in range(NS_)]
        vt4 = [sb(st, nc, f"p3_vt{i}", [128, 4, 128], BF16) for i in range(NS_)]
        gb = [sb(st, nc, f"p3_gb{i}", [128, 16], F32) for i in range(NS_)]
        sm = [sb(st, nc, f"p3_sm{i}", [128, 32], F32) for i in range(NS_)]
        Y4 = sb(st, nc, "p3_Y4", [128, 4, 128], F32)
        GTi = [sb(st, nc, f"p3_GTi{i}", [128, 512], F32) for i in range(NS_)]
        Gs = sb(st, nc, "p3_Gs", [128, 4, 128], F32)
        CH = F32
        Mb = [sb(st, nc, f"p3_M{i}", [128, 4, 128], CH) for i in range(2)]
        MTb = [sb(st, nc, f"p3_MT{i}", [128, 4, 128], CH) for i in range(2)]
        Xb = [sb(st, nc, f"p3_X{i}", [128, 4, 128], CH) for i in range(3)]
        Dg4 = sb(st, nc, "p3_Dg4", [128, 4, 128], F32)
        kbgT = [sb(st, nc, f"p3_kbgT{i}", [128, 4, 128], F32) for i in range(NS_)]
        vb = [sb(st, nc, f"p3_vb{i}", [128, 4, 128], F32) for i in range(NS_)]
        r4 = sb(st, nc, "p3_r4", [128, 4, 128], F32)
        kdec = [sb(st, nc, f"p3_kdec{i}", [128, 4, 128], BF16) for i in range(NS_)]
        nwT = [sb(st, nc, f"p3_nwT{i}", [128, 4, 128], BF16) for i in range(NS_)]
        qkT = [sb(st, nc, f"p3_qkT{i}", [128, 4, 128], BF16) for i in range(NS_)]
        qdT = [sb(st, nc, f"p3_qdT{i}", [128, 4, 128], F32) for i in range(NS_)]
        RT = [sb(st, nc, f"p3_RT{i}", [128, 4, 128], F32) for i in range(NS_)]
        vnew = sb(st, nc, "p3_vnew", [128, 4, 128], BF16)
        osb = [sb(st, nc, f"p3_o{i}", [128, 512], F32) for i in range(2)]
        oin = [sb(st, nc, f"p3_oin{i}", [128, 512], F32) for i in range(2)]
        b0 = pst(st, nc, "p3_b0", [128, 512], F32)
        b1 = pst(st, nc, "p3_b1", [128, 512], F32)
        b2 = pst(st, nc, "p3_b2", [128, 512], F32)
        b3 = pst(st, nc, "p3_b3", [128, 512], F32)
        pV = pst(st, nc, "p3_pV", [128, 512], F32)
        pO = pst(st, nc, "p3_pO", [128, 512], F32)
        pS = pst(st, nc, "p3_pS", [128, 512], F32)
        tb = pst(st, nc, "p3_tb", [128, 512], F32)

        P.dma("sp", LT[:], io["LT" + sfx][:, :], [], ["LT"], "c0")
        P.dma("sp", XS[:], io["XS" + sfx][:, :], [], ["XS"], "c1")
        P.dma("sp", NTI[:], io["NTI" + sfx][:, :], [], ["NTI"], "c2")
        P.dma("sp", NS[:], io["NS" + sfx][:, :], [], ["NS"], "c3")
        P.dma("sp", ident[:], io["ident"][:, :], [], ["ident"], "c4")
        P.dma("pool", identb[:], io["ident"][:, :], [], ["identb"], "c5")
        P.dma("sp", I4f[:], io["I4f"][:, :], [], ["I4f"], "c6")
        P.op("pool", lambda e: e.memset(ones[:], 1.0), [], ["ones"])
        if not fused:
            P.dma("sp", S4[:], io["st_in"][:, :], [], ["S4"], "c7")
        elif dirn == 0:
            P.op("pool", lambda e: e.memset(S4[:], 0.0), [], ["S4"])
        else:
            sga = sb(st, nc, "p3_sga", [128, 512], F32)
            sgb = sb(st, nc, "p3_sgb", [128, 512], F32)
            selt = sb(st, nc, "p3_sel", [128, 2], F32)
            P.dma("sp", sga[:], io["sg"][0:128, :], [], ["sga"], "c7")
            P.dma("sp", sgb[:], io["sg"][128:256, :], [], ["sgb"], "c9")
            P.dma("sp", selt[:], io["sel"].partition_broadcast(128), [], ["selt"], "c10")
            P.op("dve", lambda e: e.tensor_scalar(out=S4[:], in0=sga[:], scalar1=selt[:, 0:1], scalar2=None,
                                                  op0=ALU.mult), ["sga", "selt"], ["S4"])
            P.op("dve", lambda e: e.scalar_tensor_tensor(out=S4[:], in0=sgb[:], scalar=selt[:, 1:2], in1=S4[:],
                                                         op0=ALU.mult, op1=ALU.add), ["sgb", "selt", "S4"], ["S4"])

        def v3(t):
            return t[:].rearrange("p (h x) -> p h x", h=4)

        def bc(ap4):
            return ap4.unsqueeze(2).to_broadcast([128, 4, 128])

        def perhead(ps, lhs, rhs, twice=None):
            def f(e):
                ins = None
                for h in range(4):
                    ins = e.matmul(ps[:, h * 128:(h + 1) * 128], lhsT=lhs[:, h, :], rhs=rhs[:, h, :],
                                   start=True, stop=True)
                return ins
            return f

        order = list(range(NT)) if dirn == 0 else list(range(NT - 1, -1, -1))
        xcnt = [0]

        import os
        cut = int(os.environ.get("P3_CUT", "99"))

        def prep(t, si):
            K = lambda n: (n, si)
            c0 = t * 128
            P.dma("sp", qT4[si][:], io["qg"][:, c0:c0 + 128].rearrange("(h d) t -> d h t", h=4), [], [K("qT")], f"qT{si}")
            P.dma("sp", kT4[si][:], io["kg"][:, c0:c0 + 128].rearrange("(h d) t -> d h t", h=4), [], [K("kT")], f"kT{si}")
            P.dma("sp", kt4[si][:].rearrange("p h d -> p (h d)"), io["ktok"][c0:c0 + 128, :], [], [K("kt")], f"kt{si}")
            P.dma("sp", vt4[si][:].rearrange("p h d -> p (h d)"), io["vtok"][c0:c0 + 128, :], [], [K("vt")], f"vt{si}")
            P.dma("sp", gb[si][:], io["gb"][c0:c0 + 128, :], [], [K("gb")], f"gb{si}")
            g4 = gb[si][:, 4 * dirn:4 * dirn + 4]
            be4 = gb[si][:, 8 + 4 * dirn:12 + 4 * dirn]
            s_ = sm[si]
            eg, dk, gl, bg, nbeta, tmp4 = (s_[:, 0:4], s_[:, 4:8], s_[:, 8:12], s_[:, 12:16],
                                           s_[:, 16:20], s_[:, 20:24])
            def mmc(e):
                e.matmul(b0[:, 0:4], lhsT=LT[:, :], rhs=g4, start=True, stop=True)
                return e.matmul(b0[:, 4:8], lhsT=ones[:, :], rhs=g4, start=True, stop=True)
            P.op("pe", mmc, [K("gb"), "LT", "ones"], ["b0"])
            gcs = s_[:, 24:32]
            P.op("dve", lambda e: e.tensor_copy(out=gcs, in_=b0[:, 0:8]), ["b0"], [K("gcs")])
            P.op("act", lambda e: e.activation(out=eg, in_=gcs[:, 0:4], func=AF.Exp), [K("gcs")], [K("eg")])
            P.op("act", lambda e: e.activation(out=gl, in_=gcs[:, 4:8], func=AF.Exp), [K("gcs")], [K("gl")])
            P.op("dve", lambda e: e.tensor_tensor(out=tmp4, in0=gcs[:, 4:8], in1=gcs[:, 0:4], op=ALU.subtract),
                 [K("gcs")], [K("tmp4")])
            P.op("act", lambda e: e.activation(out=dk, in_=tmp4, func=AF.Exp), [K("tmp4")], [K("dk")])
            P.op("dve", lambda e: e.tensor_tensor(out=bg, in0=be4, in1=eg, op=ALU.mult), [K("gb"), K("eg")], [K("bg")])
            P.op("dve", lambda e: e.tensor_scalar(out=nbeta, in0=be4, scalar1=-1.0, scalar2=None, op0=ALU.mult),
                 [K("gb")], [K("nbeta")])
            if cut <= 1:
                return
            P.op("dve", lambda e: e.tensor_tensor(out=Y4[:], in0=LT[:].unsqueeze(1).to_broadcast([128, 4, 128]),
                                                   in1=bc(g4), op=ALU.mult), [K("gb"), "LT"], ["Y4"])
            def mm1(e):
                e.matmul(b1[:, :], lhsT=ident[:, :], rhs=NTI[:, :], start=True, stop=False)
                return e.matmul(b1[:, :], lhsT=XS[:, :], rhs=Y4[:].rearrange("p h c -> p (h c)"), start=False, stop=True)
            P.op("pe", mm1, ["Y4", "XS", "NTI", "ident"], ["b1"])
            P.op("act", lambda e: e.activation(out=GTi[si][:], in_=b1[:], func=AF.Exp), ["b1"], [K("GTi")])
            if cut <= 2:
                return
            def mm2(e):
                ins = e.matmul(b2[:, :], lhsT=ident[:, :], rhs=NS[:, :], start=True, stop=False)
                for h in range(4):
                    ins = e.matmul(b2[:, h * 128:(h + 1) * 128], lhsT=Y4[:, h, :], rhs=XS[:, :],
                                   start=False, stop=(h == 3))
                return ins
            P.op("pe", mm2, ["Y4", "XS", "NS", "ident"], ["b2"])
            P.op("act", lambda e: e.activation(out=Gs[:].rearrange("p h c -> p (h c)"), in_=b2[:], func=AF.Exp),
                 ["b2"], ["Gs"])
            P.op("dve", lambda e: e.tensor_tensor(out=Gs[:], in0=Gs[:], in1=bc(nbeta), op=ALU.mult),
                 ["Gs", K("nbeta")], ["Gs"])
            if cut <= 3:
                return
            P.op("pe", perhead(b3, kT4[si], kT4[si]), [K("kT")], ["b3"])
            P.op("dve", lambda e: e.tensor_tensor(out=Mb[0][:], in0=v3(b3), in1=Gs[:], op=ALU.mult),
                 ["b3", "Gs"], ["M0"])

            if cut <= 4:
                return

            def trM(e):
                ins = None
                for h in range(4):
                    ins = e.transpose(out=tb[:, h * 128:(h + 1) * 128], in_=Mb[0][:, h, :], identity=ident[:])
                return ins
            var = os.environ.get("P3_VAR", "")
            P.op("pe", trM, ["M0", "ident"], ["tb"])
            if var != "noact":
                P.op("act", lambda e: e.copy(out=MTb[0][:], in_=v3(tb)), ["tb"], ["MT0"])
            xi = xcnt[0] % 3
            if var != "nodve":
                P.op("dve", lambda e, xi=xi: e.tensor_tensor(out=Xb[xi][:], in0=MTb[0][:], in1=v3(I4f), op=ALU.add),
                     ["MT0", "I4f"], [("X", xi)])
            if cut <= 5:
                return
            cur = 0
            for lvl in range(1, 7):
                nxt = 1 - cur
                P.op("pe", perhead(b1, MTb[cur], Mb[cur]), [f"M{cur}", f"MT{cur}"], ["b1"])
                P.op("act", lambda e, nxt=nxt: e.copy(out=Mb[nxt][:], in_=v3(b1)), ["b1"], [f"M{nxt}"])
                if lvl < 6:
                    P.op("pe", perhead(b2, Mb[cur], MTb[cur]), [f"M{cur}", f"MT{cur}"], ["b2"])
                    P.op("dve", lambda e, nxt=nxt: e.tensor_copy(out=MTb[nxt][:], in_=v3(b2)), ["b2"], [f"MT{nxt}"])
                P.op("pe", perhead(b3, Mb[nxt], Xb[xi]), [f"M{nxt}", ("X", xi)], ["b3"])
                xn_ = (xi + 1) % 3
                last = lvl == 6
                dst = RT[si] if last else Xb[xn_]
                kd = K("RT") if last else ("X", xn_)
                P.op("dve", lambda e, dst=dst, xi=xi: e.tensor_tensor(out=dst[:], in0=v3(b3), in1=Xb[xi][:], op=ALU.add),
                     ["b3", ("X", xi)], [kd])
                xi = xn_
                cur = nxt
            xcnt[0] = xi + 1
            if cut <= 6:
                return
            def _f(e):
                ins = None
                for h in range(4):
                    ins = e.activation(out=vb[si][:, h, :], in_=vt4[si][:, h, :], func=AF.Copy, scale=be4[:, h:h + 1])
                return ins
            P.op("act", _f, [K("vt"), K("gb")], [K("vb")])
            def _f(e):
                ins = None
                for h in range(4):
                    ins = e.activation(out=kdec[si][:, h, :], in_=kt4[si][:, h, :], func=AF.Copy, scale=dk[:, h:h + 1])
                return ins
            P.op("act", _f, [K("kt"), K("dk")], [K("kdec")])
            P.op("pe", perhead(b2, kT4[si], qT4[si]), [K("kT"), K("qT")], ["b2"])
            P.op("dve", lambda e: e.tensor_tensor(out=qkT[si][:], in0=v3(b2), in1=v3(GTi[si]), op=ALU.mult),
                 ["b2", K("GTi")], [K("qkT")])
            def _f(e):
                ins = None
                for h in range(4):
                    ins = e.activation(out=Dg4[:, h, :], in_=ident[:, :], func=AF.Copy, scale=eg[:, h:h + 1])
                return ins
            P.op("act", _f, ["ident", K("eg")], ["Dg4"])
            P.op("pe", lambda e: e.matmul(b3[:, :], lhsT=ones[:, :], rhs=Dg4[:].rearrange("p h c -> p (h c)"),
                                          start=True, stop=True), ["Dg4", "ones"], ["b3"])
            P.op("dve", lambda e: e.tensor_tensor(out=qdT[si][:], in0=v3(b3), in1=qT4[si][:], op=ALU.mult),
                 ["b3", K("qT")], [K("qdT")])
            def _f(e):
                ins = None
                for h in range(4):
                    ins = e.activation(out=Dg4[:, h, :], in_=ident[:, :], func=AF.Copy, scale=bg[:, h:h + 1])
                return ins
            P.op("act", _f, ["ident", K("bg")], ["Dg4"])
            P.op("pe", lambda e: e.matmul(b1[:, :], lhsT=ones[:, :], rhs=Dg4[:].rearrange("p h c -> p (h c)"),
                                          start=True, stop=True), ["Dg4", "ones"], ["b1"])
            P.op("dve", lambda e: e.tensor_tensor(out=kbgT[si][:], in0=v3(b1), in1=kT4[si][:], op=ALU.mult),
                 ["b1", K("kT")], [K("kbgT")])
            if dirn == 1:
                P.dma("sp", oin[si][:], io["oa"][c0:c0 + 128, :], [], [K("oin")], f"oin{si}")

        def step(t, si):
            K = lambda n: (n, si)
            c0 = t * 128
            gl = sm[si][:, 8:12]

            def mmR(e):
                ins = None
                for h in range(4):
                    ins = e.matmul(pV[:, h * 128:(h + 1) * 128], lhsT=kbgT[si][:, h, :],
                                   rhs=S4[:, h * 128:(h + 1) * 128], start=True, stop=True)
                return ins
            P.op("pe", mmR, [K("kbgT"), "S4"], ["pV"])
            P.op("dve", lambda e: e.tensor_tensor(out=r4[:], in0=vb[si][:], in1=v3(pV), op=ALU.subtract),
                 ["pV", K("vb")], ["r4"])
            P.op("pe", perhead(pV, RT[si], r4), [K("RT"), "r4"], ["pV"])
            P.op("act", lambda e: e.copy(out=vnew[:], in_=v3(pV)), ["pV"], ["vnew"])

            def mmO(e):
                ins = None
                for h in range(4):
                    e.matmul(pO[:, h * 128:(h + 1) * 128], lhsT=qdT[si][:, h, :],
                             rhs=S4[:, h * 128:(h + 1) * 128], start=True, stop=False)
                    ins = e.matmul(pO[:, h * 128:(h + 1) * 128], lhsT=qkT[si][:, h, :], rhs=vnew[:, h, :],
                                   start=False, stop=True)
                return ins
            P.op("pe", mmO, [K("qdT"), K("qkT"), "vnew", "S4"], ["pO"])

            def mmS(e):
                ins = None
                for h in range(4):
                    ins = e.matmul(pS[:, h * 128:(h + 1) * 128], lhsT=kdec[si][:, h, :], rhs=vnew[:, h, :],
                                   start=True, stop=True)
                return ins
            P.op("pe", mmS, [K("kdec"), "vnew"], ["pS"])
            P.op("dve", lambda e: e.tensor_tensor(out=v3(S4), in0=v3(S4), in1=bc(gl), op=ALU.mult),
                 ["S4", K("gl")], ["S4"])
            P.op("dve", lambda e: e.tensor_tensor(out=S4[:], in0=S4[:], in1=pS[:], op=ALU.add),
                 ["S4", "pS"], ["S4"])
            oi = t % 2
            if dirn == 0:
                P.op("act", lambda e: e.copy(out=osb[oi][:], in_=pO[:]), ["pO"], [("osb", oi)])
                P.dma("sp", io["oa"][c0:c0 + 128, :], osb[oi][:], [("osb", oi)], [("oa", t)], f"osb{oi}")
            else:
                P.op("dve", lambda e: e.tensor_tensor(out=osb[oi][:], in0=pO[:], in1=oin[si][:], op=ALU.add),
                     ["pO", K("oin")], [("osb", oi)])
                P.dma("sp", io["ob"][c0:c0 + 128, :], osb[oi][:], [("osb", oi)], [("ob", t)], f"osb{oi}")

        import os
        ntl = int(os.environ.get("P3_NT", NT))
        nostep = bool(int(os.environ.get("P3_NOSTEP", "0")))
        order = order[:ntl]
        prep(order[0], 0)
        for i, t in enumerate(order):
            pa = P.capture(lambda: prep(order[i + 1], (i + 1) % 2)) if i + 1 < len(order) else []
            pb = P.capture(lambda: step(t, i % 2)) if not nostep else []
            if pa:
                P.replay_interleaved(pa, pb)
            else:
                P.replay_interleaved(pb, [])
        P.dma("sp", io["st_out"][:, :], S4[:], ["S4"], ["st_out"], "c8")
        if fused and dirn == 0:
            P.op("pool", lambda e: e.collective_compute("AllGather", ALU.bypass, replica_groups=PAIRS,
                                                        ins=[io["st_out"][:, :]], outs=[io["sg"][:, :]]),
                 ["st_out"], ["sg"])
            P.op("sp", None, reads=["sg"])
        P.op("sp", None, reads=["st_out"] + [(("oa" if dirn == 0 else "ob"), t) for t in order if not nostep])
        P.emit(tag)


def phase3c(nc, io):
    with ExitStack() as st:
        P = Prog(nc)
        gg = sb(st, nc, "p3c_gg", [128, 128], F32)
        ob = [sb(st, nc, f"p3c_ob{i}", [128, 4, 128], F32) for i in range(2)]
        zz = [sb(st, nc, f"p3c_zz{i}", [128, 4, 128], F32) for i in range(2)]
        sq = sb(st, nc, "p3c_sq", [128, 4, 128], F32)
        ss = sb(st, nc, "p3c_ss", [128, 4], F32)
        P.dma("sp", gg[:], io["gdn_g"].partition_broadcast(128), [], ["gg"], "c0")
        for t in range(NT):
            i = t % 2
            c0 = t * 128
            P.dma("sp", ob[i][:].rearrange("p h d -> p (h d)"), io["ob"][c0:c0 + 128, :], [], [("ob", i)], f"ob{i}")
            P.dma("sp", zz[i][:].rearrange("p h d -> p (h d)"), io["zs"][c0:c0 + 128, :], [], [("zz", i)], f"zz{i}")
            P.op("dve", lambda e, i=i: e.tensor_tensor(out=sq[:], in0=ob[i][:], in1=ob[i][:], op=ALU.mult),
                 [("ob", i)], ["sq"])
            P.op("dve", lambda e: e.reduce_sum(out=ss[:], in_=sq[:], axis=AX.X), ["sq"], ["ss"])
            P.op("act", lambda e: e.activation(out=ss[:], in_=ss[:], func=AF.Ln, bias=RMS_EPS, scale=1.0 / 128),
                 ["ss"], ["ss1"])
            P.op("act", lambda e: e.activation(out=ss[:], in_=ss[:], func=AF.Exp, scale=-0.5), ["ss1"], ["ss2"])
            P.op("act", lambda e, i=i: e.activation(out=zz[i][:], in_=zz[i][:], func=AF.Silu), [("zz", i)], [("zz", i)])
            P.op("dve", lambda e, i=i: e.tensor_tensor(
                out=ob[i][:], in0=ob[i][:], in1=ss[:].unsqueeze(2).to_broadcast([128, 4, 128]), op=ALU.mult),
                [("ob", i), "ss2"], [("ob", i)])
            P.op("dve", lambda e, i=i: e.tensor_tensor(
                out=ob[i][:], in0=ob[i][:], in1=gg[:].unsqueeze(1).to_broadcast([128, 4, 128]), op=ALU.mult),
                [("ob", i), "gg"], [("ob", i)])
            P.op("dve", lambda e, i=i: e.tensor_tensor(out=ob[i][:], in0=ob[i][:], in1=zz[i][:], op=ALU.mult),
                 [("ob", i), ("zz", i)], [("ob", i)])
            P.dma("sp", io["og"][c0:c0 + 128, :], ob[i][:].rearrange("p h d -> p (h d)"), [("ob", i)],
                  [("og", t)], f"ob{i}")
        P.op("sp", None, reads=[("og", t) for t in range(NT)])
        P.emit("p3c")


def phase4(nc, io):
    with ExitStack() as st:
        P = Prog(nc)
        bias = [sb(st, nc, f"p4_bias{i}", [128, 8, 5, 128], F32) for i in range(2)]
        kwin = [sb(st, nc, f"p4_kwin{i}", [128, 4, 640], BF16) for i in range(2)]
        qw = [sb(st, nc, f"p4_qw{i}", [128, 4, 128], BF16) for i in range(2)]
        vwin = [sb(st, nc, f"p4_vwin{i}", [128, 5, 8, 65], BF16) for i in range(2)]
        tmp = [sb(st, nc, f"p4_tmp{i}", [128, 512], F32) for i in range(2)]
        ET = [sb(st, nc, f"p4_ET{i}", [128, 5, 8, 128], BF16) for i in range(2)]
        gna = sb(st, nc, "p4_gna", [128, 64], F32)
        rec = sb(st, nc, "p4_rec", [128, 8], F32)
        on = [sb(st, nc, f"p4_on{i}", [128, 8, 64], F32) for i in range(2)]
        sq = sb(st, nc, "p4_sq", [128, 8, 64], F32)
        ss = sb(st, nc, "p4_ss", [128, 8], F32)
        psS = [pst(st, nc, f"p4_psS{i}", [128, 512], F32) for i in range(4)]
        psO = [pst(st, nc, f"p4_psO{i}", [128, 512], F32) for i in range(4)]

        P.dma("sp", gna[:], io["na_g"].partition_broadcast(128), [], ["gna"], "c0")
        for i in range(2):
            P.op("pool", lambda e, i=i: e.memset(vwin[i][:], 1.0), [], [("vw", i, j) for j in range(5)])
        cnt = {"psS": 0, "tmp": 0}
        def part1(m):
            bi = m % 2
            ks = min(max(m - 2, 0), 29)
            var = m if m < 2 else 2
            bsl = 0 if var == 0 else (1 if var == 1 else 0)
            if m <= 2:
                P.dma("sp", bias[bsl][:].rearrange("p h j q -> p (h j q)"), io["nab"][var, :, :],
                      [], [("bias", bsl)], f"bias{bsl}")
            kw, qq, vw, et = kwin[bi], qw[bi], vwin[bi], ET[bi]
            P.dma("sp", kw[:], io["kn"][:, ks * 128:ks * 128 + 640].rearrange("(c p) t -> p c t", p=128),
                  [], [("kw", bi)], f"kw{bi}")
            P.dma("sp", qq[:], io["qn"][:, m * 128:(m + 1) * 128].rearrange("(c p) t -> p c t", p=128),
                  [], [("qw", bi)], f"qw{bi}")
            for j in range(5):
                P.dma("sp", vw[:, j, :, 0:64],
                      io["vn"][(ks + j) * 128:(ks + j + 1) * 128, :].rearrange("t (h d) -> t h d", h=8),
                      [], [("vw", bi, j)], f"vw{bi}_{j}")
            for hg in range(2):
                for j in range(5):
                    pi = cnt["psS"] % 4
                    cnt["psS"] += 1
                    pS = psS[pi]

                    def mm(e, hg=hg, j=j, pS=pS, kw=kw, qq=qq):
                        ins = None
                        for hh in range(4):
                            h = 2 * hh + hg
                            p0 = (h % 2) * 64
                            ins = e.matmul(pS[:, hh * 128:(hh + 1) * 128],
                                           lhsT=kw[p0:p0 + 64, h // 2, j * 128:(j + 1) * 128],
                                           rhs=qq[p0:p0 + 64, h // 2, :], start=True, stop=True)
                        return ins
                    P.op("pe", mm, [("kw", bi), ("qw", bi)], [("psS", pi)])
                    ti = cnt["tmp"] % 2
                    cnt["tmp"] += 1
                    t_ = tmp[ti]
                    P.op("dve", lambda e, t_=t_, pS=pS, hg=hg, j=j, bsl=bsl: e.tensor_tensor(
                        out=t_[:].rearrange("p (h q) -> p h q", h=4),
                        in0=pS[:].rearrange("p (h q) -> p h q", h=4),
                        in1=bias[bsl][:, hg * 4:(hg + 1) * 4, j, :], op=ALU.add),
                        [("psS", pi), ("bias", bsl)], [("tmp", ti)])
                    P.op("act", lambda e, t_=t_, et=et, hg=hg, j=j: e.activation(
                        out=et[:, j, hg * 4:(hg + 1) * 4, :],
                        in_=t_[:].rearrange("p (h q) -> p h q", h=4), func=AF.Exp),
                        [("tmp", ti)], [("ET", bi, hg, j)])
        def part2(m):
            bi = m % 2
            vw, et = vwin[bi], ET[bi]
            o_ = on[bi]
            for hg in range(2):
                pO = psO[(m % 2) * 2 + hg]
                kO = ("psO", (m % 2) * 2 + hg)

                def pv(e, hg=hg, pO=pO, et=et, vw=vw):
                    ins = None
                    for hh in range(4):
                        h = hg * 4 + hh
                        slot = (h % 2) * 4 + h // 2
                        for j in range(5):
                            ins = e.matmul(pO[:, hh * 65:(hh + 1) * 65], lhsT=et[:, j, slot, :],
                                           rhs=vw[:, j, h, :], start=(j == 0), stop=(j == 4))
                    return ins
                P.op("pe", pv, [("ET", bi, g_, j) for g_ in range(2) for j in range(5)] +
                     [("vw", bi, j) for j in range(5)], [kO])
                pv3 = pO[:, 0:260].rearrange("p (h d) -> p h d", h=4)
                P.op("dve", lambda e, pv3=pv3, hg=hg: e.reciprocal(
                    out=rec[:, hg * 4:(hg + 1) * 4], in_=pv3[:, :, 64]), [kO], [("rec", hg)])
                P.op("dve", lambda e, pv3=pv3, hg=hg, o_=o_: e.tensor_tensor(
                    out=o_[:, hg * 4:(hg + 1) * 4, :], in0=pv3[:, :, 0:64],
                    in1=rec[:, hg * 4:(hg + 1) * 4].unsqueeze(2).to_broadcast([128, 4, 64]), op=ALU.mult),
                    [kO, ("rec", hg)], [("on", bi, hg)])
            kon = [("on", bi, 0), ("on", bi, 1)]
            P.op("dve", lambda e, o_=o_: e.tensor_tensor(out=sq[:], in0=o_[:], in1=o_[:], op=ALU.mult),
                 kon, ["sq"])
            P.op("dve", lambda e: e.reduce_sum(out=ss[:], in_=sq[:], axis=AX.X), ["sq"], ["ss"])
            P.op("act", lambda e: e.activation(out=ss[:], in_=ss[:], func=AF.Ln, bias=RMS_EPS, scale=1.0 / 64),
                 ["ss"], ["ss1"])
            P.op("act", lambda e: e.activation(out=ss[:], in_=ss[:], func=AF.Exp, scale=-0.5),
                 ["ss1"], ["ss2"])
            P.op("dve", lambda e, o_=o_: e.tensor_tensor(
                out=o_[:], in0=o_[:], in1=ss[:].unsqueeze(2).to_broadcast([128, 8, 64]), op=ALU.mult),
                kon + ["ss2"], kon)
            P.op("dve", lambda e, o_=o_: e.tensor_tensor(
                out=o_[:], in0=o_[:], in1=gna[:].unsqueeze(1).to_broadcast([128, 8, 64]), op=ALU.mult),
                kon + ["gna"], kon)
            P.dma("sp", io["ona"][m * 128:(m + 1) * 128, :], o_[:].rearrange("p h d -> p (h d)"),
                  kon, [("ona", m)], f"on{bi}")
        part1(0)
        for m in range(NT):
            pa = P.capture(lambda: part1(m + 1)) if m + 1 < NT else []
            pb = P.capture(lambda: part2(m))
            if pa:
                P.replay_interleaved(pa, pb)
            else:
                P.replay_interleaved(pb, [])
        P.op("sp", None, reads=[("ona", m) for m in range(NT)])
        P.emit("p4")


def ln_rows(P, eng_aff, src, dst, stats, mv, rstd, gvec, bvec, ksrc, kdst, tag):
    for hh in range(2):
        P.op("dve", lambda e, hh=hh: e.bn_stats(out=stats[:, hh, :],
                                                 in_=src[:, hh * 512:(hh + 1) * 512]),
             [ksrc], [(tag, "st", hh)])
    P.op("dve", lambda e: e.bn_aggr(out=mv[:], in_=stats[:].rearrange("p a b -> p (a b)")),
         [(tag, "st", 0), (tag, "st", 1)], [(tag, "mv")])
    P.op("act", lambda e: e.activation(out=rstd[:], in_=mv[:, 1:2], func=AF.Ln, bias=LN_EPS),
         [(tag, "mv")], [(tag, "rs0")])
    P.op("act", lambda e: e.activation(out=rstd[:], in_=rstd[:], func=AF.Exp, scale=-0.5),
         [(tag, "rs0")], [(tag, "rs")])
    P.op("dve", lambda e: e.tensor_scalar(out=dst[:], in0=src[:], scalar1=mv[:, 0:1],
                                          scalar2=rstd[:, 0:1], op0=ALU.subtract, op1=ALU.mult),
         [ksrc, (tag, "mv"), (tag, "rs")], [kdst])
    P.op(eng_aff, lambda e: e.tensor_tensor(out=dst[:], in0=dst[:], in1=gvec[:], op=ALU.mult),
         [kdst, "v0", "v1", "v2", "v3"], [kdst])
    P.op(eng_aff, lambda e: e.tensor_tensor(out=dst[:], in0=dst[:], in1=bvec[:], op=ALU.add),
         [kdst, "v0", "v1", "v2", "v3"], [kdst])


def phase5(nc, io, resname="xn", outname="xo"):
    MT = 256
    NMT = T // MT
    with ExitStack() as st:
        P = Prog(nc)
        w1 = sb(st, nc, "p5_w1", [128, 8, DFF], BF16)
        w2 = sb(st, nc, "p5_w2", [128, 32, D], BF16)
        wo = sb(st, nc, "p5_wo", [128, 8, D], BF16)
        g1 = sb(st, nc, "p5_g1", [128, D], F32)
        bb1 = sb(st, nc, "p5_bb1", [128, D], F32)
        g2 = sb(st, nc, "p5_g2", [128, D], F32)
        bb2 = sb(st, nc, "p5_bb2", [128, D], F32)
        b1t = sb(st, nc, "p5_b1t", [128, 32], F32)
        b2r = sb(st, nc, "p5_b2r", [1, D], BF16)
        ones1 = sb(st, nc, "p5_ones1", [1, 128], BF16)
        ident = sb(st, nc, "p5_ident", [128, 128], F32)
        hT = sb(st, nc, "p5_hT", [128, 32, MT], BF16)
        xs = [sb(st, nc, f"p5_xs{i}", [128, D], F32) for i in range(4)]
        mixin = [sb(st, nc, f"p5_mi{i}", [128, D], F32) for i in range(1)]
        mixT = [sb(st, nc, f"p5_mT{i}", [128, 8, 128], BF16) for i in range(1)]
        x1T = sb(st, nc, "p5_x1T", [128, 8, MT], BF16)
        rl = [sb(st, nc, f"p5_rl{i}", [128, MT], F32) for i in range(2)]
        stats = [sb(st, nc, f"p5_stats{i}", [128, 2, 6], F32) for i in range(2)]
        mv = [sb(st, nc, f"p5_mv{i}", [128, 2], F32) for i in range(2)]
        rstd = [sb(st, nc, f"p5_rstd{i}", [128, 1], F32) for i in range(2)]
        psT = [pst(st, nc, f"p5_psT{i}", [128, 512], F32) for i in range(2)]
        psM = [pst(st, nc, f"p5_psM{i}", [128, 512], F32) for i in range(2)]
        psF = [pst(st, nc, f"p5_psF{i}", [128, 512], F32) for i in range(4)]

        P.dma("sp", ident[:], io["ident"][:, :], [], ["ident"], "c0")
        P.dma_multi("pool", [(wo[:, k, :], io["w_out"][k * 128:(k + 1) * 128, :]) for k in range(8)],
                    [("wo", k) for k in range(8)], "wo")
        P.dma("sp", g1[:], io["ln1_g"].partition_broadcast(128), [], ["v0"], "c1")
        P.dma("sp", bb1[:], io["ln1_b"].partition_broadcast(128), [], ["v1"], "c2")
        P.dma("sp", g2[:], io["ln2_g"].partition_broadcast(128), [], ["v2"], "c3")
        P.dma("sp", bb2[:], io["ln2_b"].partition_broadcast(128), [], ["v3"], "c4")
        P.dma("sp", b1t[:], io["b1t"][:, :], [], ["b1t"], "c5")
        P.dma("pool", b2r[:], io["b2"].rearrange("(o d) -> o d", o=1), [], ["b2r"], "c6")
        P.op("pool", lambda e: e.memset(ones1[:], 1.0), [], ["ones1"])
        P.dma_multi("pool", [(w1[:, k, :], io["w1"][k * 128:(k + 1) * 128, :]) for k in range(8)],
                    [("w1", k) for k in range(8)], "w1")
        for q4 in range(4):
            P.dma_multi("pool", [(w2[:, k, :], io["w2"][k * 128:(k + 1) * 128, :]) for k in range(q4 * 8, q4 * 8 + 8)],
                        [("w2", k) for k in range(q4 * 8, q4 * 8 + 8)], f"w2{q4}")
        wok = [("wo", k) for k in range(8)]
        w1k = [("w1", k) for k in range(8)]
        w2k = [("w2", k) for k in range(32)]
        cnt = {"psF": 0, "rl": 0}

        def transposes(src, ksrc, dstT, kdst, col0):
            for hb in range(2):
                pT = psT[hb]

                def tr(e, hb=hb, pT=pT):
                    ins = None
                    for kk in range(4):
                        k = hb * 4 + kk
                        ins = e.transpose(out=pT[:, kk * 128:(kk + 1) * 128],
                                          in_=src[:, k * 128:(k + 1) * 128], identity=ident[:])
                    return ins
                P.op("pe", tr, (ksrc if isinstance(ksrc, list) else [ksrc]) + ["ident"], [("psT", hb)])
                o = dstT[:, hb * 4:(hb + 1) * 4, col0:col0 + 128]
                i = pT[:].rearrange("p (k t) -> p k t", k=4)
                if hb == 0:
                    P.op("act", lambda e, o=o, i=i: e.copy(out=o, in_=i), [("psT", hb)], [(kdst, hb)])
                else:
                    P.op("dve", lambda e, o=o, i=i: e.tensor_copy(out=o, in_=i), [("psT", hb)], [(kdst, hb)])

        def stage_a1(mt):
            for sub in range(2):
                t0 = mt * MT + sub * 128
                si = (mt % 2) * 2 + sub
                mi = 0
                x_ = xs[si]
                m_ = mixin[mi]
                P.dma("sp", x_[:], io[resname][t0:t0 + 128, :], [], [("xs", si)], f"xs{si}")
                P.dma("sp", m_[:, 0:512], io["og"][t0:t0 + 128, :], [], [("mi", mi, 0)], f"mi{mi}a")
                P.dma("sp", m_[:, 512:1024], io["ona"][t0:t0 + 128, :], [], [("mi", mi, 1)], f"mi{mi}b")
                transposes(m_, [("mi", mi, 0), ("mi", mi, 1)], mixT[mi], ("mT", mi), 0)
                for half in range(2):
                    pM = psM[half]

                    def mm(e, half=half, pM=pM, mi=mi):
                        ins = None
                        for k in range(8):
                            ins = e.matmul(pM[:, :], lhsT=mixT[mi][:, k, :],
                                           rhs=wo[:, k, half * 512:(half + 1) * 512],
                                           start=(k == 0), stop=(k == 7))
                        return ins
                    P.op("pe", mm, [(("mT", mi), 0), (("mT", mi), 1)] + wok, [("psM", half)])
                    P.op("dve", lambda e, half=half, pM=pM, x_=x_: e.scalar_tensor_tensor(
                        out=x_[:, half * 512:(half + 1) * 512], in0=x_[:, half * 512:(half + 1) * 512],
                        scalar=ALPHA, in1=pM[:, :], op0=ALU.mult, op1=ALU.add),
                        [("xs", si), ("psM", half)], [("xs", si)])
                ln_rows(P, "dve", x_, x_, stats[0], mv[0], rstd[0], g1, bb1, ("xs", si), ("xs", si), "ln1")

        def stage_a2(mt):
            for sub in range(2):
                si = (mt % 2) * 2 + sub
                transposes(xs[si], ("xs", si), x1T, ("x1T", sub), sub * 128)

        def stage_b(mt):
            xk = [(("x1T", s_), h_) for s_ in range(2) for h_ in range(2)]
            for fc in range(32):
                pi = cnt["psF"] % 4
                cnt["psF"] += 1
                pF = psF[pi]

                def mm(e, fc=fc, pF=pF):
                    ins = None
                    for k in range(8):
                        ins = e.matmul(pF[:, 0:MT], lhsT=w1[:, k, fc * 128:(fc + 1) * 128],
                                       rhs=x1T[:, k, :], start=(k == 0), stop=(k == 7))
                    return ins
                P.op("pe", mm, xk + w1k, [("psF", pi)])
                ri = cnt["rl"] % 2
                cnt["rl"] += 1
                r_ = rl[ri]
                P.op("act", lambda e, r_=r_, pF=pF, fc=fc: e.activation(
                    out=r_[:], in_=pF[:, 0:MT], func=AF.Relu, bias=b1t[:, fc:fc + 1]),
                    [("psF", pi), "b1t"], [("rl", ri)])
                eng = "dve"
                P.op(eng, lambda e, r_=r_, fc=fc: e.tensor_tensor(
                    out=hT[:, fc, :], in0=r_[:], in1=r_[:], op=ALU.mult),
                    [("rl", ri)], [("hT", fc)])

        def stage_c(mt):
            hk = [("hT", fc) for fc in range(32)]
            for sub in range(2):
                t0 = mt * MT + sub * 128
                si = (mt % 2) * 2 + sub
                x_ = xs[si]
                for half in range(2):
                    pM = psM[half]

                    def mm(e, half=half, pM=pM, sub=sub):
                        ins = None
                        for fc in range(32):
                            ins = e.matmul(pM[:, :], lhsT=hT[:, fc, sub * 128:(sub + 1) * 128],
                                           rhs=w2[:, fc, half * 512:(half + 1) * 512],
                                           start=(fc == 0), stop=False)
                        ins = e.matmul(pM[:, :], lhsT=ones1[:, :], rhs=b2r[:, half * 512:(half + 1) * 512],
                                       start=False, stop=True)
                        return ins
                    P.op("pe", mm, hk + w2k + ["ones1", "b2r"], [("psM", half)])
                    P.op("dve", lambda e, half=half, pM=pM, x_=x_: e.scalar_tensor_tensor(
                        out=x_[:, half * 512:(half + 1) * 512], in0=x_[:, half * 512:(half + 1) * 512],
                        scalar=ALPHA, in1=pM[:, :], op0=ALU.mult, op1=ALU.add),
                        [("xs", si), ("psM", half)], [("xs", si)])
                ln_rows(P, "dve", x_, x_, stats[1], mv[1], rstd[1], g2, bb2, ("xs", si), ("xs", si), "ln2")
                P.dma("sp", io[outname][t0:t0 + 128, :], x_[:], [("xs", si)], [("xo", t0 // 128)], f"xs{si}")

        stage_a1(0)
        stage_a2(0)
        for mt in range(NMT):
            stage_b(mt)
            if mt + 1 < NMT:
                stage_a1(mt + 1)
            stage_c(mt)
            if mt + 1 < NMT:
                stage_a2(mt + 1)
        P.op("sp", None, reads=[("xo", i) for i in range(NT)])
        P.emit("p5")


PAIRS = [[0, 1], [2, 3], [4, 5], [6, 7]]


def phase_halo(nc, io):
    with ExitStack() as st:
        P = Prog(nc)
        jm = sb(st, nc, "px_j", [128, 128], F32)
        selt = sb(st, nc, "px_sel", [128, 2], F32)
        pa = [sb(st, nc, f"px_a{i}", [128, D], F32) for i in range(2)]
        pb = [sb(st, nc, f"px_b{i}", [128, D], F32) for i in range(2)]
        ro = [sb(st, nc, f"px_r{i}", [128, D], F32) for i in range(2)]
        ps = [pst(st, nc, f"px_ps{i}", [128, 512], F32) for i in range(4)]
        P.dma("sp", jm[:], io["jmat"][:, :], [], ["jm"], "c0")
        P.dma("sp", selt[:], io["sel"].partition_broadcast(128), [], ["selt"], "c1")
        P.op("pool", lambda e: e.collective_compute("AllGather", ALU.bypass, replica_groups=PAIRS,
                                                    ins=[io["x1"][T - 256:T, :]], outs=[io["hg"][:, :]]),
             [], ["hg"])
        for i in range(2):
            P.dma("sp", pa[i][:], io["hg"][i * 128:(i + 1) * 128, :], ["hg"], [("pa", i)], f"pa{i}")
            P.dma("sp", pb[i][:], io["hg"][256 + i * 128:256 + (i + 1) * 128, :], ["hg"], [("pb", i)], f"pb{i}")
            P.op("dve", lambda e, i=i: e.tensor_scalar(out=pa[i][:], in0=pa[i][:], scalar1=selt[:, 0:1],
                                                       scalar2=None, op0=ALU.mult), [("pa", i), "selt"], [("pa", i)])
            P.op("dve", lambda e, i=i: e.scalar_tensor_tensor(out=pa[i][:], in0=pb[i][:], scalar=selt[:, 1:2],
                                                              in1=pa[i][:], op0=ALU.mult, op1=ALU.add),
                 [("pa", i), ("pb", i), "selt"], [("pa", i)])
            for hh in range(2):
                p_ = ps[i * 2 + hh]
                P.op("pe", lambda e, i=i, hh=hh, p_=p_: e.matmul(p_[:, :], lhsT=jm[:, :],
                                                                  rhs=pa[i][:, hh * 512:(hh + 1) * 512],
                                                                  start=True, stop=True),
                     [("pa", i), "jm"], [("ps", i, hh)])
                P.op("act", lambda e, i=i, hh=hh, p_=p_: e.copy(out=ro[i][:, hh * 512:(hh + 1) * 512], in_=p_[:, :]),
                     [("ps", i, hh)], [("ro", i, hh)])
            dst0 = T + (1 - i) * 128
            P.dma("sp", io["x1"][dst0:dst0 + 128, :], ro[i][:], [("ro", i, 0), ("ro", i, 1)], [("halo", i)], f"ro{i}")
        P.op("sp", None, reads=[("halo", 0), ("halo", 1)])
        P.emit("px")


SCRATCH = {
    "xn": ([T, D], F32), "hq": ([1536, 4100], F32), "qn": ([512, T], BF16),
    "kn": ([512, TH], BF16), "vn": ([TH, 512], BF16), "zs": ([T, 512], F32),
    "gb": ([T, 16], F32), "og": ([T, 512], F32), "ona": ([T, 512], F32),
    "xo": ([T, D], F32), "qg": ([512, T], BF16), "kg": ([512, T], BF16),
    "ktok": ([T, 512], BF16), "vtok": ([T, 512], BF16), "oa": ([T, 512], F32),
    "ob": ([T, 512], F32), "st_out": ([128, 512], F32), "sg": ([256, 512], F32),
    "x1": ([TH, D], F32), "hg": ([512, D], F32),
}
NPDT = {F32: np.float32}


def build(phases, in_specs, out_names, layer0=True):
    nc = bass.Bass("TRN2", target_bir_lowering=False)
    io = {}
    for name, (shape, dt) in in_specs.items():
        io[name] = nc.dram_tensor(name, list(shape), dt, kind="ExternalInput").ap()
    for name, (shape, dt) in SCRATCH.items():
        if name in io:
            continue
        kind = "ExternalOutput" if name in out_names else "Internal"
        io[name] = nc.dram_tensor(name, list(shape), dt, kind=kind).ap()
    for ph in phases:
        if ph == "p1":
            phase1(nc, io, layer0)
        if ph == "p5":
            phase5(nc, io)
        if ph == "p4":
            phase4(nc, io)
        if ph == "p3a":
            phase3(nc, io, 0, "p3a")
        if ph == "p3b":
            phase3(nc, io, 1, "p3b")
        if ph == "p3c":
            phase3c(nc, io)
        if ph == "p2":
            phase2(nc, io, range(0, 4), "p2a")
            phase2(nc, io, range(4, 8), "p2b")
    return nc


def core_bs(c):
    return c // 2, c % 2


def host_layer_inputs(inp, l, c, x_full_loc):
    b, s = core_bs(c)
    w_in = np.array(inp["w_in"][l], dtype=np.float32, copy=True)
    a_log = np.asarray(inp["a_log"][l], np.float32)
    dt_b = np.asarray(inp["dt_bias"][l], np.float32)
    convw = np.asarray(inp["conv_w"][l], np.float32)
    if s == 1:
        a = w_in[:, 2048:2056].copy()
        bb = w_in[:, 2056:2064].copy()
        w_in[:, 2048:2052] = a[:, 4:8]
        w_in[:, 2052:2056] = a[:, 0:4]
        w_in[:, 2056:2060] = bb[:, 4:8]
        w_in[:, 2060:2064] = bb[:, 0:4]
        a_log = a_log[::-1]
        dt_b = dt_b[::-1]
        convw = convw[::-1]
    d = {
        "x": None if x_full_loc is None else np.ascontiguousarray(x_full_loc),
        "w_in": w_in,
        "gpar": np.ascontiguousarray(np.concatenate([a_log.reshape(-1), dt_b.reshape(-1)])),
        "convw": np.ascontiguousarray(convw.T),
        "ident": np.eye(128, dtype=np.float32),
    }
    return d


def na_bias_tables(rpb_l, s):
    out = np.empty((3, 128, 8, 5, 128), np.float32)
    for vi, m in enumerate((0, 1, 5)):
        ks = min(max(m - 2, 0), 29)
        qi = m * 128 + np.arange(128)
        ki = ks * 128 + np.arange(640)
        tq = qi if s == 0 else 8191 - qi
        tk = ki if s == 0 else 8191 - ki
        rq, cq = tq // 64, tq % 64
        rk, ck = tk // 64, tk % 64
        r0 = np.clip(rq - 4, 0, 120)
        w0 = np.clip(cq - 8, 0, 48)
        valid = ((rk[:, None] >= r0[None, :]) & (rk[:, None] < r0[None, :] + 8) &
                 (ck[:, None] >= w0[None, :]) & (ck[:, None] < w0[None, :] + 16))
        dr = np.clip(rk[:, None] - rq[None, :] + 7, 0, 14)
        dc = np.clip(ck[:, None] - cq[None, :] + 15, 0, 30)
        b = rpb_l[:, dr, dc]
        b = np.where(valid[None], b, np.float32(NEG)).astype(np.float32)
        b = b[[0, 2, 4, 6, 1, 3, 5, 7]]
        out[vi] = b.reshape(8, 5, 128, 128).transpose(2, 0, 1, 3)
    return np.ascontiguousarray(out.reshape(3, 128, 8 * 5 * 128))


def gdn_consts():
    i = np.arange(128)
    lt = (i[:, None] <= i[None, :]).astype(np.float32)
    xs = (i[:, None] > i[None, :]).astype(np.float32)
    nti = np.where(i[None, :] < i[:, None], np.float32(NEG), np.float32(0))
    ns = np.where(i[:, None] <= i[None, :], np.float32(NEG), np.float32(0))
    d = {"LTA": lt, "XSA": xs, "NTIA": np.tile(nti, (1, 4)), "NSA": np.tile(ns, (1, 4)),
         "LTB": lt.T, "XSB": xs.T, "NTIB": np.tile(nti.T, (1, 4)), "NSB": np.tile(ns.T, (1, 4)),
         "I4f": np.tile(np.eye(128, dtype=np.float32), (1, 4))}
    return {k: np.ascontiguousarray(v, dtype=np.float32) for k, v in d.items()}


_BF = ml_dtypes.bfloat16

L1_OUT = ["xn", "zs", "gb", "qg", "kg", "ktok", "vtok", "oa", "ona", "st_out"]


def _specs(maps):
    sp = {}
    for k, v in maps[0].items():
        sp[k] = (list(v.shape), BF16 if v.dtype == _BF else F32)
    return sp


def _run(phases, maps, outs, layer0=True):
    nc = build(phases, _specs(maps), outs, layer0=layer0)
    res = run_bass_kernel_spmd(nc, maps, core_ids=list(range(8)))
    return res.results


def kernel_unfused(x, ln_in_g, ln_in_b, w_in, conv_w, a_log, dt_bias, gdn_norm_g, rpb, na_norm_g,
                   w_out, ln1_g, ln1_b, w1, b1, w2, b2, ln2_g, ln2_b):
    inp = dict(x=x, w_in=w_in, conv_w=conv_w, a_log=a_log, dt_bias=dt_bias)
    x = np.asarray(x, np.float32)
    consts = gdn_consts()
    ident = np.eye(128, dtype=np.float32)
    zeros_st = np.zeros((128, 512), np.float32)
    xloc = []
    for c in range(8):
        b, s = core_bs(c)
        xs = x[b] if s == 0 else x[b][::-1]
        xloc.append(np.ascontiguousarray(xs[:TH]))
    for l in range(2):
        maps = []
        for c in range(8):
            b, s = core_bs(c)
            m = host_layer_inputs(inp, l, c, xloc[c])
            if l == 0:
                m["ln_in_g"] = np.asarray(ln_in_g, np.float32)
                m["ln_in_b"] = np.asarray(ln_in_b, np.float32)
            m["nab"] = na_bias_tables(np.asarray(rpb[l], np.float32), s)
            m["na_g"] = np.asarray(na_norm_g[l], np.float32)
            for k in ("LTA", "XSA", "NTIA", "NSA", "I4f"):
                m[k] = consts[k]
            m["st_in"] = zeros_st
            maps.append(m)
        outs1 = [o for o in L1_OUT if not (o == "xn" and l > 0)]
        r1 = _run(["p1", "p2", "p4", "p3a"], maps, outs1, layer0=(l == 0))
        maps2 = []
        for c in range(8):
            r = r1[c]
            m = {k: np.ascontiguousarray(r[k]) for k in ("qg", "kg", "ktok", "vtok", "gb", "oa", "zs", "ona")}
            m["xn"] = np.ascontiguousarray(r["xn"]) if l == 0 else np.ascontiguousarray(xloc[c][:T])
            m["st_in"] = np.ascontiguousarray(r1[c ^ 1]["st_out"])
            for k in ("LTB", "XSB", "NTIB", "NSB", "I4f"):
                m[k] = consts[k]
            m["ident"] = ident
            m["gdn_g"] = np.asarray(gdn_norm_g[l], np.float32)
            m["w_out"] = np.asarray(w_out[l], np.float32)
            m["w1"] = np.asarray(w1[l], np.float32)
            m["w2"] = np.asarray(w2[l], np.float32)
            m["b1t"] = np.ascontiguousarray(np.asarray(b1[l], np.float32).reshape(32, 128).T)
            m["b2"] = np.asarray(b2[l], np.float32)
            m["ln1_g"] = np.asarray(ln1_g[l], np.float32)
            m["ln1_b"] = np.asarray(ln1_b[l], np.float32)
            m["ln2_g"] = np.asarray(ln2_g[l], np.float32)
            m["ln2_b"] = np.asarray(ln2_b[l], np.float32)
            maps2.append(m)
        r2 = _run(["p3b", "p3c", "p5"], maps2, ["xo"])
        xo = [np.asarray(r2[c]["xo"], np.float32) for c in range(8)]
        xloc = [np.ascontiguousarray(np.concatenate([xo[c], xo[c ^ 1][::-1][:TH - T]], 0)) for c in range(8)]
    out = np.empty((4, 8192, D), np.float32)
    for c in range(8):
        b, s = core_bs(c)
        if s == 0:
            out[b, :T] = xloc[c][:T]
        else:
            out[b, T:] = xloc[c][:T][::-1]
    return out


LAYER_IN = ["w_in", "gpar", "convw", "nab", "na_g", "gdn_g", "w_out", "w1", "w2", "b1t", "b2",
            "ln1_g", "ln1_b", "ln2_g", "ln2_b"]


def build_fused(in_specs):
    nc = bass.Bass("TRN2", target_bir_lowering=False)
    io = {}
    for name, (shape, dt) in in_specs.items():
        io[name] = nc.dram_tensor(name, list(shape), dt, kind="ExternalInput").ap()
    for name, (shape, dt) in SCRATCH.items():
        io[name] = nc.dram_tensor(name, list(shape), dt, kind="Internal").ap()
    io["out"] = nc.dram_tensor("out", [T, D], F32, kind="ExternalOutput").ap()
    for l in range(2):
        iol = dict(io)
        for k in LAYER_IN:
            iol[k] = io[f"{k}_l{l}"]
        phase1(nc, iol, l == 0, "x" if l == 0 else "x1")
        phase2(nc, iol, range(0, 4), f"p2a{l}")
        phase2(nc, iol, range(4, 8), f"p2b{l}")
        phase4(nc, iol)
        phase3(nc, iol, 0, f"p3a{l}", fused=True)
        phase3(nc, iol, 1, f"p3b{l}", fused=True)
        phase3c(nc, iol)
        if l == 0:
            phase5(nc, iol, "xn", "x1")
            phase_halo(nc, iol)
        else:
            phase5(nc, iol, "x1", "out")
    return nc


def kernel(x, ln_in_g, ln_in_b, w_in, conv_w, a_log, dt_bias, gdn_norm_g, rpb, na_norm_g,
           w_out, ln1_g, ln1_b, w1, b1, w2, b2, ln2_g, ln2_b):
    inp = dict(w_in=w_in, conv_w=conv_w, a_log=a_log, dt_bias=dt_bias)
    x = np.asarray(x, np.float32)
    consts = gdn_consts()
    f = lambda a: np.ascontiguousarray(np.asarray(a, np.float32))
    maps = []
    for c in range(8):
        b, s = core_bs(c)
        xs = x[b] if s == 0 else x[b][::-1]
        m = {"x": np.ascontiguousarray(xs[:TH]), "ln_in_g": f(ln_in_g), "ln_in_b": f(ln_in_b),
             "ident": np.eye(128, dtype=np.float32), "jmat": np.ascontiguousarray(np.eye(128, dtype=np.float32)[::-1]),
             "sel": np.array([1.0, 0.0] if s == 1 else [0.0, 1.0], np.float32)}
        m.update(consts)
        for l in range(2):
            hl = host_layer_inputs(inp, l, c, None)
            m[f"w_in_l{l}"] = hl["w_in"]
            m[f"gpar_l{l}"] = hl["gpar"]
            m[f"convw_l{l}"] = hl["convw"]
            m[f"nab_l{l}"] = na_bias_tables(f(rpb[l]), s)
            m[f"na_g_l{l}"] = f(na_norm_g[l])
            m[f"gdn_g_l{l}"] = f(gdn_norm_g[l])
            m[f"w_out_l{l}"] = f(w_out[l])
            m[f"w1_l{l}"] = f(w1[l])
            m[f"w2_l{l}"] = f(w2[l])
            m[f"b1t_l{l}"] = np.ascontiguousarray(f(b1[l]).reshape(32, 128).T)
            m[f"b2_l{l}"] = f(b2[l])
            m[f"ln1_g_l{l}"] = f(ln1_g[l])
            m[f"ln1_b_l{l}"] = f(ln1_b[l])
            m[f"ln2_g_l{l}"] = f(ln2_g[l])
            m[f"ln2_b_l{l}"] = f(ln2_b[l])
        maps.append(m)
    nc = build_fused(_specs(maps))
    res = run_bass_kernel_spmd(nc, maps, core_ids=list(range(8)))
    out = np.empty((4, 8192, D), np.float32)
    for c in range(8):
        b, s = core_bs(c)
        o = np.asarray(res.results[c]["out"], np.float32)
        if s == 0:
            out[b, :T] = o
        else:
            out[b, T:] = o[::-1]
    return out
```

```python
import numpy as np
import ml_dtypes
from contextlib import ExitStack
import concourse.bass as bass
import concourse.mybir as mybir
from concourse.bass_utils import run_bass_kernel_spmd

F32 = mybir.dt.float32
BF16 = mybir.dt.bfloat16
ALU = mybir.AluOpType
AF = mybir.ActivationFunctionType
AX = mybir.AxisListType

D = 1024
T = 4096
TH = 4352
NT = 32
DIN = 3600
DFF = 4096
ALPHA = 4.0 ** 0.25
LN_EPS = 1e-5
RMS_EPS = 1e-6
NEG = -30000.0


class Prog:
    ENGS = ("pe", "act", "dve", "pool", "sp")

    def __init__(self, nc):
        self.nc = nc
        self.ops = {e: [] for e in self.ENGS}
        self.cnt = {}
        self.seen = {e: {} for e in self.ENGS}
        self.W = {}
        self.R = {}
        self.awaited = {}
        self._cap = None

    def capture(self, fn):
        saved, self._cap = self._cap, []
        fn()
        lst, self._cap = self._cap, saved
        return lst

    def replay_interleaved(self, a, b):
        if not b:
            for o in a:
                self.op(*o)
            return
        stride = max(1, len(a) // len(b))
        bi = 0
        for i, o in enumerate(a):
            self.op(*o)
            if (i + 1) % stride == 0 and bi < len(b):
                self.op(*b[bi])
                bi += 1
        for o in b[bi:]:
            self.op(*o)

    def op(self, eng, fn, reads=(), writes=(), dma=None):
        if self._cap is not None:
            self._cap.append((eng, fn, tuple(reads), tuple(writes), dma))
            return
        assert fn is not None or not writes
        own = "eng:" + eng
        waits = {}

        def merge(d, skip_own):
            if d:
                for s, c in d.items():
                    if skip_own and s == own and not dma:
                        continue
                    if waits.get(s, 0) < c:
                        waits[s] = c

        for k in reads:
            merge(self.W.get(k), False)
        for k in writes:
            merge(self.W.get(k), True)
            merge(self.R.get(k), True)
        sem = ("dma:" + dma) if dma else own
        for s, c in waits.items():
            if self.seen[eng].get(s, 0) >= c:
                continue
            self.seen[eng][s] = c
            self.ops[eng].append(("wait", s, c))
            self.awaited.setdefault(s, set()).add(c)
        n = self.cnt.get(sem, 0) + 1
        self.cnt[sem] = n
        self.ops[eng].append(("op", fn, sem, n))
        for k in reads:
            self.R.setdefault(k, {})[sem] = n
        for k in writes:
            self.W[k] = {sem: n}
            self.R[k] = {}

    def dma(self, eng, out, in_, reads, writes, group):
        self.op(eng, lambda e: e.dma_start(out=out, in_=in_), reads=reads,
                writes=writes, dma=group)

    def dma_multi(self, eng, pairs, keys, group):
        for (out, in_), k in zip(pairs, keys):
            self.dma(eng, out, in_, [], [k], group)
        sem = "dma:" + group
        for k in keys:
            self.W[k] = {sem: self.cnt[sem]}

    def emit(self, name):
        nc = self.nc
        cmap = {}
        for s, cs in self.awaited.items():
            for i, c in enumerate(sorted(cs)):
                cmap[(s, c)] = i + 1
        allsems = set(self.awaited)
        for s, n in self.cnt.items():
            if s.startswith("dma:"):
                allsems.add(s)
                for c in range(1, n + 1):
                    cmap[(s, c)] = c
        with ExitStack() as st:
            st.enter_context(nc.cleanup_on_exit())
            sems = {}
            for s in sorted(allsems):
                _UID[0] += 1
                sems[s] = nc.alloc_semaphore(name=f"{name}_{_UID[0]}_" + s.replace(":", "_"))
            block = st.enter_context(nc.Block())

            def make(ename):
                def body(e):
                    for o in self.ops[ename]:
                        if o[0] == "wait":
                            _, s, c = o
                            mult = 16 if s.startswith("dma:") else 1
                            e.wait_ge(sems[s], cmap[(s, c)] * mult)
                        else:
                            _, fn, s, n = o
                            if fn is None:
                                continue
                            ins = fn(e)
                            if (s, n) in cmap:
                                ins.then_inc(
                                    sems[s], 16 if s.startswith("dma:") else 1)
                return body

            block.tensor(make("pe"))
            block.scalar(make("act"))
            block.vector(make("dve"))
            block.gpsimd(make("pool"))
            block.sync(make("sp"))


_UID = [0]


def sb(st, nc, name, shape, dt):
    _UID[0] += 1
    return st.enter_context(nc.sbuf_tensor(f"{name}_u{_UID[0]}", list(shape), dt))


def pst(st, nc, name, shape, dt):
    _UID[0] += 1
    return st.enter_context(nc.psum_tensor(f"{name}_u{_UID[0]}", list(shape), dt))


def phase1(nc, io, layer0, xname="x"):
    x_d = io[xname]
    with ExitStack() as st:
        P = Prog(nc)
        win = sb(st, nc, "p1_win", [128, 8, DIN], BF16)
        ident = sb(st, nc, "p1_ident", [128, 128], F32)
        lng = sb(st, nc, "p1_lng", [128, D], F32)
        lnb = sb(st, nc, "p1_lnb", [128, D], F32)
        gpar = sb(st, nc, "p1_gpar", [128, 16], F32)
        nea = sb(st, nc, "p1_nea", [128, 8], F32)
        zpad = sb(st, nc, "p1_zpad", [128, 2], F32)
        xt = [sb(st, nc, f"p1_xt{i}", [128, D], F32) for i in range(3)]
        xn = [sb(st, nc, f"p1_xn{i}", [128, D], F32) for i in range(2)]
        stats = sb(st, nc, "p1_stats", [128, 2, 6], F32)
        mv = sb(st, nc, "p1_mv", [128, 2], F32)
        rstd = sb(st, nc, "p1_rstd", [128, 1], F32)
        xT = [sb(st, nc, f"p1_xT{i}", [128, 8, 512], BF16) for i in range(2)]
        fo = [sb(st, nc, f"p1_fo{i}", [128, 512], F32) for i in range(3)]
        fob = [sb(st, nc, f"p1_fob{i}", [128, 512], BF16) for i in range(3)]
        to = [sb(st, nc, f"p1_to{i}", [128, 512], F32) for i in range(2)]
        tob = [sb(st, nc, f"p1_tob{i}", [128, 512], BF16) for i in range(2)]
        ab = [sb(st, nc, f"p1_ab{i}", [128, 16], F32) for i in range(2)]
        abt = [sb(st, nc, f"p1_abt{i}", [128, 16], F32) for i in range(2)]
        psT = [pst(st, nc, f"p1_psT{i}", [128, 512], F32) for i in range(2)]
        psF = [pst(st, nc, f"p1_psF{i}", [128, 512], F32) for i in range(3)]
        psK = [pst(st, nc, f"p1_psK{i}", [128, 512], F32) for i in range(2)]
        psA = pst(st, nc, "p1_psA", [128, 16], F32)

        for k in range(8):
            P.dma("pool", win[:, k, :], io["w_in"][k * 128:(k + 1) * 128, :],
                  [], [("win", k)], f"win{k}")
        P.dma("sp", ident[:], io["ident"][:, :], [], ["ident"], "c0")
        if layer0:
            P.dma("sp", lng[:], io["ln_in_g"].partition_broadcast(128), [], ["lng"], "c2")
            P.dma("sp", lnb[:], io["ln_in_b"].partition_broadcast(128), [], ["lnb"], "c3")
        P.dma("sp", gpar[:], io["gpar"].partition_broadcast(128), [], ["gpar"], "c4")
        P.op("act", lambda e: e.activation(out=nea[:], in_=gpar[:, 0:8], func=AF.Exp),
             ["gpar"], ["nea0"])
        P.op("dve", lambda e: e.tensor_scalar(out=nea[:], in0=nea[:], scalar1=-1.0,
                                              scalar2=None, op0=ALU.mult),
             ["nea0"], ["nea"])
        P.op("pool", lambda e: e.memset(zpad[:], 0.0), [], ["zpad"])
        for c in range(12):
            P.dma("sp", io["hq"][c * 128:(c + 1) * 128, 0:2], zpad[:], ["zpad"],
                  [("hq", c, -1)], "zp")

        nmac = 9
        cnt = {"xt": 0, "xn": 0, "fo": 0, "to": 0, "psF": 0, "psK": 0, "ab": 0}
        for mt in range(nmac):
            halo = mt == 8
            nsub = 2 if halo else 4
            ntok = nsub * 128
            tok0 = mt * 512
            xTm = xT[mt % 2]
            kxT = ("xT", mt % 2)
            for sub in range(nsub):
                t0 = tok0 + sub * 128
                xi = cnt["xt"] % 3
                cnt["xt"] += 1
                xtile = xt[xi]
                P.dma("sp", xtile[:], x_d[t0:t0 + 128, :], [], [("xt", xi)], f"xt{xi}")
                src = xtile
                ksrc = ("xt", xi)
                if layer0:
                    ni = cnt["xn"] % 2
                    cnt["xn"] += 1
                    xnt = xn[ni]
                    for hh in range(2):
                        P.op("dve", lambda e, hh=hh, xtile=xtile: e.bn_stats(
                            out=stats[:, hh, :], in_=xtile[:, hh * 512:(hh + 1) * 512]),
                            [("xt", xi)], [("stats", hh)])
                    P.op("dve", lambda e: e.bn_aggr(out=mv[:], in_=stats[:].rearrange("p a b -> p (a b)")),
                         [("stats", 0), ("stats", 1)], ["mv"])
                    P.op("act", lambda e: e.activation(
                        out=rstd[:], in_=mv[:, 1:2], func=AF.Ln, bias=LN_EPS), ["mv"], ["rstd"])
                    P.op("act", lambda e: e.activation(
                        out=rstd[:], in_=rstd[:], func=AF.Exp, scale=-0.5), ["rstd"], ["rstd"])
                    P.op("dve", lambda e, xtile=xtile, xnt=xnt: e.tensor_scalar(
                        out=xnt[:], in0=xtile[:], scalar1=mv[:, 0:1], scalar2=rstd[:, 0:1],
                        op0=ALU.subtract, op1=ALU.mult),
                        [("xt", xi), "mv", "rstd"], [("xn", ni)])
                    P.op("dve", lambda e, xnt=xnt: e.tensor_tensor(
                        out=xnt[:], in0=xnt[:], in1=lng[:], op=ALU.mult),
                        [("xn", ni), "lng"], [("xn", ni)])
                    P.op("dve", lambda e, xnt=xnt: e.tensor_tensor(
                        out=xnt[:], in0=xnt[:], in1=lnb[:], op=ALU.add),
                        [("xn", ni), "lnb"], [("xn", ni)])
                    if not halo:
                        P.dma("sp", io["xn"][t0:t0 + 128, :], xnt[:], [("xn", ni)],
                              [("xn_d", t0 // 128)], f"xns{ni}")
                    src = xnt
                    ksrc = ("xn", ni)
                idm, kid = ident, "ident"
                for hb in range(2):
                    pT = psT[hb]

                    def tr(e, hb=hb, pT=pT, src=src, idm=idm):
                        ins = None
                        for kk in range(4):
                            k = hb * 4 + kk
                            ins = e.transpose(out=pT[:, kk * 128:(kk + 1) * 128],
                                              in_=src[:, k * 128:(k + 1) * 128],
                                              identity=idm[:])
                        return ins
                    P.op("pe", tr, [ksrc, kid], [("psT", hb)])
                    outap = xTm[:, hb * 4:(hb + 1) * 4, sub * 128:(sub + 1) * 128]
                    inap = pT[:].rearrange("p (k t) -> p k t", k=4)
                    if hb == 0:
                        P.op("act", lambda e, o=outap, i=inap: e.copy(out=o, in_=i),
                             [("psT", hb)], [(kxT, sub, hb)])
                    else:
                        P.op("dve", lambda e, o=outap, i=inap: e.tensor_copy(out=o, in_=i),
                             [("psT", hb)], [(kxT, sub, hb)])
            xkeys = [(kxT, s_, h_) for s_ in range(nsub) for h_ in range(2)]
            wkeys = [("win", k) for k in range(8)]

            fm = [(c * 128, "hq", c) for c in range(12)]
            if not halo:
                fm += [(2064 + c * 128, "q", c) for c in range(4)]
            fm += [(2576 + c * 128, "k", c) for c in range(4)]
            for (col, kind, c) in fm:
                n = ntok
                if halo and kind == "hq":
                    n = 2
                pi = cnt["psF"] % 3
                cnt["psF"] += 1
                pF = psF[pi]

                def mm(e, col=col, n=n, pF=pF, xTm=xTm):
                    ins = None
                    for k in range(8):
                        ins = e.matmul(pF[:, 0:n], lhsT=win[:, k, col:col + 128],
                                       rhs=xTm[:, k, 0:n], start=(k == 0), stop=(k == 7))
                    return ins
                P.op("pe", mm, xkeys + wkeys, [("psF", pi)])
                fi = cnt["fo"] % 3
                cnt["fo"] += 1
                if kind == "hq":
                    fbuf = fo[fi]
                    P.op("act", lambda e, fbuf=fbuf, pF=pF, n=n: e.copy(out=fbuf[:, 0:n], in_=pF[:, 0:n]),
                         [("psF", pi)], [("fo", fi)])
                    P.dma("sp", io["hq"][c * 128:(c + 1) * 128, 2 + tok0:2 + tok0 + n],
                          fbuf[:, 0:n], [("fo", fi)], [("hq", c, mt)], f"fo{fi}")
                else:
                    fbuf = fob[fi]
                    sc = 0.125 if kind == "q" else 1.0
                    P.op("act", lambda e, fbuf=fbuf, pF=pF, n=n, sc=sc: e.activation(
                        out=fbuf[:, 0:n], in_=pF[:, 0:n], func=AF.Copy, scale=sc),
                        [("psF", pi)], [("fob", fi)])
                    dst = io["qn"] if kind == "q" else io["kn"]
                    P.dma("sp", dst[c * 128:(c + 1) * 128, tok0:tok0 + n], fbuf[:, 0:n],
                          [("fob", fi)], [(kind + "n", c, mt)], f"fob{fi}")

            for sub in range(nsub):
                t0 = tok0 + sub * 128
                tk = [("v", 3088)]
                if not halo:
                    tk = [("z", 1536), ("v", 3088), ("ab", 2048)]
                for (kind, col) in tk:
                    if kind == "ab":
                        def mm(e, sub=sub, xTm=xTm):
                            ins = None
                            for k in range(8):
                                ins = e.matmul(psA[:, 0:16], lhsT=xTm[:, k, sub * 128:(sub + 1) * 128],
                                               rhs=win[:, k, 2048:2064], start=(k == 0), stop=(k == 7))
                            return ins
                        P.op("pe", mm, xkeys + wkeys, ["psA"])
                        ai = cnt["ab"] % 2
                        cnt["ab"] += 1
                        a_, at_ = ab[ai], abt[ai]
                        P.op("dve", lambda e, at_=at_: e.tensor_copy(out=at_[:, 8:16], in_=psA[:, 8:16]),
                             ["psA"], [("abt2", ai)])
                        P.op("dve", lambda e, at_=at_: e.tensor_tensor(
                            out=at_[:, 0:8], in0=psA[:, 0:8], in1=gpar[:, 8:16], op=ALU.add),
                            ["psA", "gpar"], [("abt", ai)])
                        P.op("act", lambda e, at_=at_: e.activation(
                            out=at_[:, 8:16], in_=at_[:, 8:16], func=AF.Exp, scale=-1.0),
                            [("abt2", ai)], [("abt2", ai)])
                        P.op("act", lambda e, at_=at_: e.activation(
                            out=at_[:, 0:8], in_=at_[:, 0:8], func=AF.Exp),
                            [("abt", ai)], [("abt", ai)])
                        P.op("act", lambda e, at_=at_: e.activation(
                            out=at_[:, 0:8], in_=at_[:, 0:8], func=AF.Ln, bias=1.0),
                            [("abt", ai)], [("abt", ai)])
                        P.op("dve", lambda e, at_=at_, a_=a_: e.tensor_tensor(
                            out=a_[:, 0:8], in0=at_[:, 0:8], in1=nea[:], op=ALU.mult),
                            [("abt", ai), "nea"], [("ab", ai)])
                        P.op("dve", lambda e, at_=at_: e.tensor_scalar(
                            out=at_[:, 8:16], in0=at_[:, 8:16], scalar1=1.0, scalar2=None,
                            op0=ALU.add), [("abt2", ai)], [("abt2", ai)])
                        P.op("dve", lambda e, at_=at_, a_=a_: e.reciprocal(
                            out=a_[:, 8:16], in_=at_[:, 8:16]),
                            [("abt2", ai), ("ab", ai)], [("ab", ai)])
                        P.dma("sp", io["gb"][t0:t0 + 128, :], a_[:], [("ab", ai)],
                              [("gb", t0 // 128)], f"ab{ai}")
                        continue
                    pi = cnt["psK"] % 2
                    cnt["psK"] += 1
                    pK = psK[pi]

                    def mm(e, sub=sub, xTm=xTm, pK=pK, col=col):
                        ins = None
                        for k in range(8):
                            ins = e.matmul(pK[:, :], lhsT=xTm[:, k, sub * 128:(sub + 1) * 128],
                                           rhs=win[:, k, col:col + 512], start=(k == 0), stop=(k == 7))
                        return ins
                    P.op("pe", mm, xkeys + wkeys, [("psK", pi)])
                    ti = cnt["to"] % 2
                    cnt["to"] += 1
                    if kind == "z":
                        tb = to[ti]
                        P.op("act", lambda e, tb=tb, pK=pK: e.copy(out=tb[:], in_=pK[:]),
                             [("psK", pi)], [("to", ti)])
                        P.dma("sp", io["zs"][t0:t0 + 128, :], tb[:], [("to", ti)],
                              [("zs", t0 // 128)], f"to{ti}")
                    else:
                        tb = tob[ti]
                        P.op("dve", lambda e, tb=tb, pK=pK: e.tensor_copy(out=tb[:], in_=pK[:]),
                             [("psK", pi)], [("tob", ti)])
                        P.dma("sp", io["vn"][t0:t0 + 128, :], tb[:], [("tob", ti)],
                              [("vn", t0 // 128)], f"tob{ti}")
        allk = [k for k in P.W if isinstance(k, tuple) and k[0] in
                ("hq", "qn", "kn", "vn", "zs", "gb", "xn_d")]
        P.op("sp", None, reads=allk)
        P.emit("p1")


def phase2(nc, io, mts=range(8), tag="p2"):
    with ExitStack() as st:
        P = Prog(nc)
        cw = sb(st, nc, "p2_cw", [128, 12, 5], F32)
        onesb = sb(st, nc, "p2_ones", [128, 128], BF16)
        identb = sb(st, nc, "p2_identb", [128, 128], BF16)
        hw = [sb(st, nc, f"p2_hw{i}", [128, 516], F32) for i in range(3)]
        acc = [sb(st, nc, f"p2_acc{i}", [128, 512], F32) for i in range(2)]
        ptmp = sb(st, nc, "p2_ptmp", [128, 512], F32)
        sil = [sb(st, nc, f"p2_sil{i}", [128, 512], F32) for i in range(12)]
        sqb = [sb(st, nc, f"p2_sq{i}", [128, 512], BF16) for i in range(8)]
        lnv = [sb(st, nc, f"p2_ln{i}", [128, 512], F32) for i in range(8)]
        nb = [sb(st, nc, f"p2_nb{i}", [128, 512], BF16) for i in range(12)]
        tk = [sb(st, nc, f"p2_tk{i}", [128, 512], BF16) for i in range(2)]
        psN = [pst(st, nc, f"p2_psN{i}", [128, 512], F32) for i in range(4)]
        psT = [pst(st, nc, f"p2_psT{i}", [128, 512], BF16) for i in range(2)]
        P.dma_multi("sp", [(cw[:, c, :], io["convw"][c * 128:(c + 1) * 128, :]) for c in range(12)],
                    [("cw", c) for c in range(12)], "c0")
        P.op("pool", lambda e: e.memset(onesb[:], 1.0), [], ["ones"])
        P.dma("pool", identb[:], io["ident"][:, :], [], ["identb"], "c1")
        cnt = {"hw": 0, "acc": 0, "sq": 0, "psN": 0, "psT": 0, "tk": 0}
        stop = ""
        for mt in mts:
            tok0 = mt * 512
            def chunk(c):
                hi = cnt["hw"] % 3
                cnt["hw"] += 1
                h_ = hw[hi]
                P.dma("sp", h_[:], io["hq"][c * 128:(c + 1) * 128, tok0:tok0 + 516], [], [("hw", hi)], f"hw{hi}")
                ai = cnt["acc"] % 2
                cnt["acc"] += 1
                a_ = acc[ai]
                eng = "dve"
                P.op(eng, lambda e, a_=a_, h_=h_, c=c: e.tensor_scalar(
                    out=a_[:], in0=h_[:, 0:512], scalar1=cw[:, c, 0:1], scalar2=None, op0=ALU.mult),
                    [("hw", hi), ("cw", c)], [("acc", ai)])
                for i in range(1, 5):
                    if eng == "dve":
                        P.op(eng, lambda e, a_=a_, h_=h_, c=c, i=i: e.scalar_tensor_tensor(
                            out=a_[:], in0=h_[:, i:i + 512], scalar=cw[:, c, i:i + 1], in1=a_[:],
                            op0=ALU.mult, op1=ALU.add), [("hw", hi), ("cw", c), ("acc", ai)], [("acc", ai)])
                    else:
                        P.op(eng, lambda e, h_=h_, c=c, i=i: e.tensor_scalar(
                            out=ptmp[:], in0=h_[:, i:i + 512], scalar1=cw[:, c, i:i + 1], scalar2=None,
                            op0=ALU.mult), [("hw", hi), ("cw", c)], ["ptmp"])
                        P.op(eng, lambda e, a_=a_: e.tensor_tensor(out=a_[:], in0=a_[:], in1=ptmp[:], op=ALU.add),
                             ["ptmp", ("acc", ai)], [("acc", ai)])
                if c < 8:
                    P.op("act", lambda e, a_=a_, c=c: e.activation(out=sil[c][:], in_=a_[:], func=AF.Silu),
                         [("acc", ai)], [("sil", c)])
                else:
                    P.op("act", lambda e, a_=a_, c=c: e.activation(out=nb[c][:], in_=a_[:], func=AF.Silu),
                         [("acc", ai)], [("nb", c)])

            for c in range(0, 12, 2):
                ca = P.capture(lambda: chunk(c))
                cb = P.capture(lambda: chunk(c + 1))
                P.replay_interleaved(ca, cb)
            if stop == "s1":
                break
            for c in range(8):
                P.op("act", lambda e, c=c: e.activation(out=sqb[c][:], in_=sil[c][:], func=AF.Square),
                     [("sil", c)], [("sq", c)])
            for c in range(8):
                si = c
                pi = cnt["psN"] % 4
                cnt["psN"] += 1
                pN = psN[pi]
                P.op("pe", lambda e, pN=pN, si=si: e.matmul(pN[:, :], lhsT=onesb[:, :], rhs=sqb[si][:, :],
                                                             start=True, stop=True),
                     [("sq", si), "ones"], [("psN", pi)])
                P.op("act", lambda e, pN=pN, c=c: e.activation(out=lnv[c][:], in_=pN[:], func=AF.Ln, bias=RMS_EPS),
                     [("psN", pi)], [("ln", c)])
            if stop == "s2":
                break
            for c in range(12):
                if c < 8:
                    P.op("act", lambda e, c=c: e.activation(out=lnv[c][:], in_=lnv[c][:], func=AF.Exp, scale=-0.5),
                         [("ln", c)], [("ln", c)])
                    sc = 128.0 ** -0.5 if c < 4 else 1.0
                    P.op("dve", lambda e, c=c, sc=sc: e.scalar_tensor_tensor(
                        out=nb[c][:], in0=sil[c][:], scalar=sc, in1=lnv[c][:], op0=ALU.mult, op1=ALU.mult),
                        [("sil", c), ("ln", c)], [("nb", c)])
                    dst = io["qg"] if c < 4 else io["kg"]
                    cc = c % 4
                    P.dma("sp", dst[cc * 128:(cc + 1) * 128, tok0:tok0 + 512], nb[c][:], [("nb", c)],
                          [("qkg", c, mt)], f"nb{c}")
            if stop == "s3":
                break
            for kind, base, dname in (("k", 4, "ktok"), ("v", 8, "vtok")):
                for sub in range(4):
                    pi = cnt["psT"] % 2
                    cnt["psT"] += 1
                    pT = psT[pi]

                    def tr(e, pT=pT, base=base, sub=sub):
                        ins = None
                        for h in range(4):
                            ins = e.transpose(out=pT[:, h * 128:(h + 1) * 128],
                                              in_=nb[base + h][:, sub * 128:(sub + 1) * 128],
                                              identity=identb[:])
                        return ins
                    P.op("pe", tr, [("nb", base + h) for h in range(4)] + ["identb"], [("psT", pi)])
                    ti = cnt["tk"] % 2
                    cnt["tk"] += 1
                    if ti == 0:
                        P.op("act", lambda e, pT=pT, ti=ti: e.copy(out=tk[ti][:], in_=pT[:]),
                             [("psT", pi)], [("tk", ti)])
                    else:
                        P.op("dve", lambda e, pT=pT, ti=ti: e.tensor_copy(out=tk[ti][:], in_=pT[:]),
                             [("psT", pi)], [("tk", ti)])
                    t0 = tok0 + sub * 128
                    P.dma("sp", io[dname][t0:t0 + 128, :], tk[ti][:], [("tk", ti)],
                          [(dname, t0 // 128)], f"tk{ti}")
        allk = [k for k in P.W if isinstance(k, tuple) and k[0] in ("qkg", "ktok", "vtok")]
        P.op("sp", None, reads=allk)
        P.emit(tag)


def phase3(nc, io, dirn, tag, fused=False):
    sfx = "A" if dirn == 0 else "B"
    with ExitStack() as st:
        P = Prog(nc)
        LT = sb(st, nc, "p3_LT", [128, 128], F32)
        XS = sb(st, nc, "p3_XS", [128, 128], F32)
        NTI = sb(st, nc, "p3_NTI", [128, 512], F32)
        NS = sb(st, nc, "p3_NS", [128, 512], F32)
        ident = sb(st, nc, "p3_ident", [128, 128], F32)
        identb = sb(st, nc, "p3_identb", [128, 128], BF16)
        I4f = sb(st, nc, "p3_I4f", [128, 512], F32)
        ones = sb(st, nc, "p3_ones", [128, 128], F32)
        S4 = sb(st, nc, "p3_S4", [128, 512], F32)
        Sbf = sb(st, nc, "p3_Sbf", [128, 512], BF16)
        NS_ = 2
        qT4 = [sb(st, nc, f"p3_qT{i}", [128, 4, 128], BF16) for i in range(NS_)]
        kT4 = [sb(st, nc, f"p3_kT{i}", [128, 4, 128], BF16) for i in range(NS_)]
        kt4 = [sb(st, nc, f"p3_kt{i}", [128, 4, 128], BF16) for i in range(NS_)]
        vt4 = [sb(st, nc, f"p3_vt{i}", [128, 4, 128], BF16) for i in range(NS_)]
        gb = [sb(st, nc, f"p3_gb{i}", [128, 16], F32) for i in range(NS_)]
        sm = [sb(st, nc, f"p3_sm{i}", [128, 32], F32) for i in range(NS_)]
        Y4 = sb(st, nc, "p3_Y4", [128, 4, 128], F32)
        GTi = [sb(st, nc, f"p3_GTi{i}", [128, 512], F32) for i in range(NS_)]
        Gs = sb(st, nc, "p3_Gs", [128, 4, 128], F32)
        CH = F32
        Mb = [sb(st, nc, f"p3_M{i}", [128, 4, 128], CH) for i in range(2)]
        MTb = [sb(st, nc, f"p3_MT{i}", [128, 4, 128], CH) for i in range(2)]
        Xb = [sb(st, nc, f"p3_X{i}", [128, 4, 128], CH) for i in range(3)]
        Dg4 = sb(st, nc, "p3_Dg4", [128, 4, 128], F32)
        kbgT = [sb(st, nc, f"p3_kbgT{i}", [128, 4, 128], F32) for i in range(NS_)]
        vb = [sb(st, nc, f"p3_vb{i}", [128, 4, 128], F32) for i in range(NS_)]
        r4 = sb(st, nc, "p3_r4", [128, 4, 128], F32)
        kdec = [sb(st, nc, f"p3_kdec{i}", [128, 4, 128], BF16) for i in range(NS_)]
        nwT = [sb(st, nc, f"p3_nwT{i}", [128, 4, 128], BF16) for i in range(NS_)]
        qkT = [sb(st, nc, f"p3_qkT{i}", [128, 4, 128], BF16) for i in range(NS_)]
        qdT = [sb(st, nc, f"p3_qdT{i}", [128, 4, 128], F32) for i in range(NS_)]
        RT = [sb(st, nc, f"p3_RT{i}", [128, 4, 128], F32) for i in range(NS_)]
        vnew = sb(st, nc, "p3_vnew", [128, 4, 128], BF16)
        osb = [sb(st, nc, f"p3_o{i}", [128, 512], F32) for i in range(2)]
        oin = [sb(st, nc, f"p3_oin{i}", [128, 512], F32) for i in range(2)]
        b0 = pst(st, nc, "p3_b0", [128, 512], F32)
        b1 = pst(st, nc, "p3_b1", [128, 512], F32)
        b2 = pst(st, nc, "p3_b2", [128, 512], F32)
        b3 = pst(st, nc, "p3_b3", [128, 512], F32)
        pV = pst(st, nc, "p3_pV", [128, 512], F32)
        pO = pst(st, nc, "p3_pO", [128, 512], F32)
        pS = pst(st, nc, "p3_pS", [128, 512], F32)
        tb = pst(st, nc, "p3_tb", [128, 512], F32)

        P.dma("sp", LT[:], io["LT" + sfx][:, :], [], ["LT"], "c0")
        P.dma("sp", XS[:], io["XS" + sfx][:, :], [], ["XS"], "c1")
        P.dma("sp", NTI[:], io["NTI" + sfx][:, :], [], ["NTI"], "c2")
        P.dma("sp", NS[:], io["NS" + sfx][:, :], [], ["NS"], "c3")
        P.dma("sp", ident[:], io["ident"][:, :], [], ["ident"], "c4")
        P.dma("pool", identb[:], io["ident"][:, :], [], ["identb"], "c5")
        P.dma("sp", I4f[:], io["I4f"][:, :], [], ["I4f"], "c6")
        P.op("pool", lambda e: e.memset(ones[:], 1.0), [], ["ones"])
        if not fused:
            P.dma("sp", S4[:], io["st_in"][:, :], [], ["S4"], "c7")
        elif dirn == 0:
            P.op("pool", lambda e: e.memset(S4[:], 0.0), [], ["S4"])
        else:
            sga = sb(st, nc, "p3_sga", [128, 512], F32)
            sgb = sb(st, nc, "p3_sgb", [128, 512], F32)
            selt = sb(st, nc, "p3_sel", [128, 2], F32)
            P.dma("sp", sga[:], io["sg"][0:128, :], [], ["sga"], "c7")
            P.dma("sp", sgb[:], io["sg"][128:256, :], [], ["sgb"], "c9")
            P.dma("sp", selt[:], io["sel"].partition_broadcast(128), [], ["selt"], "c10")
            P.op("dve", lambda e: e.tensor_scalar(out=S4[:], in0=sga[:], scalar1=selt[:, 0:1], scalar2=None,
                                                  op0=ALU.mult), ["sga", "selt"], ["S4"])
            P.op("dve", lambda e: e.scalar_tensor_tensor(out=S4[:], in0=sgb[:], scalar=selt[:, 1:2], in1=S4[:],
                                                         op0=ALU.mult, op1=ALU.add), ["sgb", "selt", "S4"], ["S4"])

        def v3(t):
            return t[:].rearrange("p (h x) -> p h x", h=4)

        def bc(ap4):
            return ap4.unsqueeze(2).to_broadcast([128, 4, 128])

        def perhead(ps, lhs, rhs, twice=None):
            def f(e):
                ins = None
                for h in range(4):
                    ins = e.matmul(ps[:, h * 128:(h + 1) * 128], lhsT=lhs[:, h, :], rhs=rhs[:, h, :],
                                   start=True, stop=True)
                return ins
            return f

        order = list(range(NT)) if dirn == 0 else list(range(NT - 1, -1, -1))
        xcnt = [0]

        import os
        cut = int(os.environ.get("P3_CUT", "99"))

        def prep(t, si):
            K = lambda n: (n, si)
            c0 = t * 128
            P.dma("sp", qT4[si][:], io["qg"][:, c0:c0 + 128].rearrange("(h d) t -> d h t", h=4), [], [K("qT")], f"qT{si}")
            P.dma("sp", kT4[si][:], io["kg"][:, c0:c0 + 128].rearrange("(h d) t -> d h t", h=4), [], [K("kT")], f"kT{si}")
            P.dma("sp", kt4[si][:].rearrange("p h d -> p (h d)"), io["ktok"][c0:c0 + 128, :], [], [K("kt")], f"kt{si}")
            P.dma("sp", vt4[si][:].rearrange("p h d -> p (h d)"), io["vtok"][c0:c0 + 128, :], [], [K("vt")], f"vt{si}")
            P.dma("sp", gb[si][:], io["gb"][c0:c0 + 128, :], [], [K("gb")], f"gb{si}")
            g4 = gb[si][:, 4 * dirn:4 * dirn + 4]
            be4 = gb[si][:, 8 + 4 * dirn:12 + 4 * dirn]
            s_ = sm[si]
            eg, dk, gl, bg, nbeta, tmp4 = (s_[:, 0:4], s_[:, 4:8], s_[:, 8:12], s_[:, 12:16],
                                           s_[:, 16:20], s_[:, 20:24])
            def mmc(e):
                e.matmul(b0[:, 0:4], lhsT=LT[:, :], rhs=g4, start=True, stop=True)
                return e.matmul(b0[:, 4:8], lhsT=ones[:, :], rhs=g4, start=True, stop=True)
            P.op("pe", mmc, [K("gb"), "LT", "ones"], ["b0"])
            gcs = s_[:, 24:32]
            P.op("dve", lambda e: e.tensor_copy(out=gcs, in_=b0[:, 0:8]), ["b0"], [K("gcs")])
            P.op("act", lambda e: e.activation(out=eg, in_=gcs[:, 0:4], func=AF.Exp), [K("gcs")], [K("eg")])
            P.op("act", lambda e: e.activation(out=gl, in_=gcs[:, 4:8], func=AF.Exp), [K("gcs")], [K("gl")])
            P.op("dve", lambda e: e.tensor_tensor(out=tmp4, in0=gcs[:, 4:8], in1=gcs[:, 0:4], op=ALU.subtract),
                 [K("gcs")], [K("tmp4")])
            P.op("act", lambda e: e.activation(out=dk, in_=tmp4, func=AF.Exp), [K("tmp4")], [K("dk")])
            P.op("dve", lambda e: e.tensor_tensor(out=bg, in0=be4, in1=eg, op=ALU.mult), [K("gb"), K("eg")], [K("bg")])
            P.op("dve", lambda e: e.tensor_scalar(out=nbeta, in0=be4, scalar1=-1.0, scalar2=None, op0=ALU.mult),
                 [K("gb")], [K("nbeta")])
            if cut <= 1:
                return
            P.op("dve", lambda e: e.tensor_tensor(out=Y4[:], in0=LT[:].unsqueeze(1).to_broadcast([128, 4, 128]),
                                                   in1=bc(g4), op=ALU.mult), [K("gb"), "LT"], ["Y4"])
            def mm1(e):
                e.matmul(b1[:, :], lhsT=ident[:, :], rhs=NTI[:, :], start=True, stop=False)
                return e.matmul(b1[:, :], lhsT=XS[:, :], rhs=Y4[:].rearrange("p h c -> p (h c)"), start=False, stop=True)
            P.op("pe", mm1, ["Y4", "XS", "NTI", "ident"], ["b1"])
            P.op("act", lambda e: e.activation(out=GTi[si][:], in_=b1[:], func=AF.Exp), ["b1"], [K("GTi")])
            if cut <= 2:
                return
            def mm2(e):
                ins = e.matmul(b2[:, :], lhsT=ident[:, :], rhs=NS[:, :], start=True, stop=False)
                for h in range(4):
                    ins = e.matmul(b2[:, h * 128:(h + 1) * 128], lhsT=Y4[:, h, :], rhs=XS[:, :],
                                   start=False, stop=(h == 3))
                return ins
            P.op("pe", mm2, ["Y4", "XS", "NS", "ident"], ["b2"])
            P.op("act", lambda e: e.activation(out=Gs[:].rearrange("p h c -> p (h c)"), in_=b2[:], func=AF.Exp),
                 ["b2"], ["Gs"])
            P.op("dve", lambda e: e.tensor_tensor(out=Gs[:], in0=Gs[:], in1=bc(nbeta), op=ALU.mult),
                 ["Gs", K("nbeta")], ["Gs"])
            if cut <= 3:
                return
            P.op("pe", perhead(b3, kT4[si], kT4[si]), [K("kT")], ["b3"])
            P.op("dve", lambda e: e.tensor_tensor(out=Mb[0][:], in0=v3(b3), in1=Gs[:], op=ALU.mult),
                 ["b3", "Gs"], ["M0"])

            if cut <= 4:
                return

            def trM(e):
                ins = None
                for h in range(4):
                    ins = e.transpose(out=tb[:, h * 128:(h + 1) * 128], in_=Mb[0][:, h, :], identity=ident[:])
                return ins
            var = os.environ.get("P3_VAR", "")
            P.op("pe", trM, ["M0", "ident"], ["tb"])
            if var != "noact":
                P.op("act", lambda e: e.copy(out=MTb[0][:], in_=v3(tb)), ["tb"], ["MT0"])
            xi = xcnt[0] % 3
            if var != "nodve":
                P.op("dve", lambda e, xi=xi: e.tensor_tensor(out=Xb[xi][:], in0=MTb[0][:], in1=v3(I4f), op=ALU.add),
                     ["MT0", "I4f"], [("X", xi)])
            if cut <= 5:
                return
            cur = 0
            for lvl in range(1, 7):
                nxt = 1 - cur
                P.op("pe", perhead(b1, MTb[cur], Mb[cur]), [f"M{cur}", f"MT{cur}"], ["b1"])
                P.op("act", lambda e, nxt=nxt: e.copy(out=Mb[nxt][:], in_=v3(b1)), ["b1"], [f"M{nxt}"])
                if lvl < 6:
                    P.op("pe", perhead(b2, Mb[cur], MTb[cur]), [f"M{cur}", f"MT{cur}"], ["b2"])
                    P.op("dve", lambda e, nxt=nxt: e.tensor_copy(out=MTb[nxt][:], in_=v3(b2)), ["b2"], [f"MT{nxt}"])
                P.op("pe", perhead(b3, Mb[nxt], Xb[xi]), [f"M{nxt}", ("X", xi)], ["b3"])
                xn_ = (xi + 1) % 3
                last = lvl == 6
                dst = RT[si] if last else Xb[xn_]
                kd = K("RT") if last else ("X", xn_)
                P.op("dve", lambda e, dst=dst, xi=xi: e.tensor_tensor(out=dst[:], in0=v3(b3), in1=Xb[xi][:], op=ALU.add),
                     ["b3", ("X", xi)], [kd])
                xi = xn_
                cur = nxt
            xcnt[0] = xi + 1
            if cut <= 6:
                return
            def _f(e):
                ins = None
                for h in range(4):
                    ins = e.activation(out=vb[si][:, h, :], in_=vt4[si][:, h, :], func=AF.Copy, scale=be4[:, h:h + 1])
                return ins
            P.op("act", _f, [K("vt"), K("gb")], [K("vb")])
            def _f(e):
                ins = None
                for h in range(4):
                    ins = e.activation(out=kdec[si][:, h, :], in_=kt4[si][:, h, :], func=AF.Copy, scale=dk[:, h:h + 1])
                return ins
            P.op("act", _f, [K("kt"), K("dk")], [K("kdec")])
            P.op("pe", perhead(b2, kT4[si], qT4[si]), [K("kT"), K("qT")], ["b2"])
            P.op("dve", lambda e: e.tensor_tensor(out=qkT[si][:], in0=v3(b2), in1=v3(GTi[si]), op=ALU.mult),
                 ["b2", K("GTi")], [K("qkT")])
            def _f(e):
                ins = None
                for h in range(4):
                    ins = e.activation(out=Dg4[:, h, :], in_=ident[:, :], func=AF.Copy, scale=eg[:, h:h + 1])
                return ins
            P.op("act", _f, ["ident", K("eg")], ["Dg4"])
            P.op("pe", lambda e: e.matmul(b3[:, :], lhsT=ones[:, :], rhs=Dg4[:].rearrange("p h c -> p (h c)"),
                                          start=True, stop=True), ["Dg4", "ones"], ["b3"])
            P.op("dve", lambda e: e.tensor_tensor(out=qdT[si][:], in0=v3(b3), in1=qT4[si][:], op=ALU.mult),
                 ["b3", K("qT")], [K("qdT")])
            def _f(e):
                ins = None
                for h in range(4):
                    ins = e.activation(out=Dg4[:, h, :], in_=ident[:, :], func=AF.Copy, scale=bg[:, h:h + 1])
                return ins
            P.op("act", _f, ["ident", K("bg")], ["Dg4"])
            P.op("pe", lambda e: e.matmul(b1[:, :], lhsT=ones[:, :], rhs=Dg4[:].rearrange("p h c -> p (h c)"),
                                          start=True, stop=True), ["Dg4", "ones"], ["b1"])
            P.op("dve", lambda e: e.tensor_tensor(out=kbgT[si][:], in0=v3(b1), in1=kT4[si][:], op=ALU.mult),
                 ["b1", K("kT")], [K("kbgT")])
            if dirn == 1:
                P.dma("sp", oin[si][:], io["oa"][c0:c0 + 128, :], [], [K("oin")], f"oin{si}")

        def step(t, si):
            K = lambda n: (n, si)
            c0 = t * 128
            gl = sm[si][:, 8:12]

            def mmR(e):
                ins = None
                for h in range(4):
                    ins = e.matmul(pV[:, h * 128:(h + 1) * 128], lhsT=kbgT[si][:, h, :],
                                   rhs=S4[:, h * 128:(h + 1) * 128], start=True, stop=True)
                return ins
            P.op("pe", mmR, [K("kbgT"), "S4"], ["pV"])
            P.op("dve", lambda e: e.tensor_tensor(out=r4[:], in0=vb[si][:], in1=v3(pV), op=ALU.subtract),
                 ["pV", K("vb")], ["r4"])
            P.op("pe", perhead(pV, RT[si], r4), [K("RT"), "r4"], ["pV"])
            P.op("act", lambda e: e.copy(out=vnew[:], in_=v3(pV)), ["pV"], ["vnew"])

            def mmO(e):
                ins = None
                for h in range(4):
                    e.matmul(pO[:, h * 128:(h + 1) * 128], lhsT=qdT[si][:, h, :],
                             rhs=S4[:, h * 128:(h + 1) * 128], start=True, stop=False)
                    ins = e.matmul(pO[:, h * 128:(h + 1) * 128], lhsT=qkT[si][:, h, :], rhs=vnew[:, h, :],
                                   start=False, stop=True)
                return ins
            P.op("pe", mmO, [K("qdT"), K("qkT"), "vnew", "S4"], ["pO"])

            def mmS(e):
                ins = None
                for h in range(4):
                    ins = e.matmul(pS[:, h * 128:(h + 1) * 128], lhsT=kdec[si][:, h, :], rhs=vnew[:, h, :],
                                   start=True, stop=True)
                return ins
            P.op("pe", mmS, [K("kdec"), "vnew"], ["pS"])
            P.op("dve", lambda e: e.tensor_tensor(out=v3(S4), in0=v3(S4), in1=bc(gl), op=ALU.mult),
                 ["S4", K("gl")], ["S4"])
            P.op("dve", lambda e: e.tensor_tensor(out=S4[:], in0=S4[:], in1=pS[:], op=ALU.add),
                 ["S4", "pS"], ["S4"])
            oi = t % 2
            if dirn == 0:
                P.op("act", lambda e: e.copy(out=osb[oi][:], in_=pO[:]), ["pO"], [("osb", oi)])
                P.dma("sp", io["oa"][c0:c0 + 128, :], osb[oi][:], [("osb", oi)], [("oa", t)], f"osb{oi}")
            else:
                P.op("dve", lambda e: e.tensor_tensor(out=osb[oi][:], in0=pO[:], in1=oin[si][:], op=ALU.add),
                     ["pO", K("oin")], [("osb", oi)])
                P.dma("sp", io["ob"][c0:c0 + 128, :], osb[oi][:], [("osb", oi)], [("ob", t)], f"osb{oi}")

        import os
        ntl = int(os.environ.get("P3_NT", NT))
        nostep = bool(int(os.environ.get("P3_NOSTEP", "0")))
        order = order[:ntl]
        prep(order[0], 0)
        for i, t in enumerate(order):
            pa = P.capture(lambda: prep(order[i + 1], (i + 1) % 2)) if i + 1 < len(order) else []
            pb = P.capture(lambda: step(t, i % 2)) if not nostep else []
            if pa:
                P.replay_interleaved(pa, pb)
            else:
                P.replay_interleaved(pb, [])
        P.dma("sp", io["st_out"][:, :], S4[:], ["S4"], ["st_out"], "c8")
        if fused and dirn == 0:
            P.op("pool", lambda e: e.collective_compute("AllGather", ALU.bypass, replica_groups=PAIRS,
                                                        ins=[io["st_out"][:, :]], outs=[io["sg"][:, :]]),
                 ["st_out"], ["sg"])
            P.op("sp", None, reads=["sg"])
        P.op("sp", None, reads=["st_out"] + [(("oa" if dirn == 0 else "ob"), t) for t in order if not nostep])
        P.emit(tag)


def phase3c(nc, io):
    with ExitStack() as st:
        P = Prog(nc)
        gg = sb(st, nc, "p3c_gg", [128, 128], F32)
        ob = [sb(st, nc, f"p3c_ob{i}", [128, 4, 128], F32) for i in range(2)]
        zz = [sb(st, nc, f"p3c_zz{i}", [128, 4, 128], F32) for i in range(2)]
        sq = [sb(st, nc, f"p3c_sq{i}", [128, 4, 128], F32) for i in range(2)]
        ss = [sb(st, nc, f"p3c_ss{i}", [128, 4], F32) for i in range(2)]
        P.dma("sp", gg[:], io["gdn_g"].partition_broadcast(128), [], ["gg"], "c0")

        def part1(t):
            i = t % 2
            c0 = t * 128
            P.dma("sp", ob[i][:].rearrange("p h d -> p (h d)"), io["ob"][c0:c0 + 128, :], [], [("ob", i)], f"ob{i}")
            P.dma("sp", zz[i][:].rearrange("p h d -> p (h d)"), io["zs"][c0:c0 + 128, :], [], [("zz", i)], f"zz{i}")
            P.op("dve", lambda e: e.tensor_tensor(out=sq[i][:], in0=ob[i][:], in1=ob[i][:], op=ALU.mult),
                 [("ob", i)], [("sq", i)])
            P.op("dve", lambda e: e.reduce_sum(out=ss[i][:], in_=sq[i][:], axis=AX.X), [("sq", i)], [("ss", i)])
            P.op("act", lambda e: e.activation(out=ss[i][:], in_=ss[i][:], func=AF.Ln, bias=RMS_EPS, scale=1.0 / 128),
                 [("ss", i)], [("ss", i)])
            P.op("act", lambda e: e.activation(out=ss[i][:], in_=ss[i][:], func=AF.Exp, scale=-0.5),
                 [("ss", i)], [("ss", i)])
            P.op("act", lambda e: e.activation(out=zz[i][:], in_=zz[i][:], func=AF.Silu), [("zz", i)], [("zz", i)])

        def part2(t):
            i = t % 2
            c0 = t * 128
            P.op("dve", lambda e: e.tensor_tensor(
                out=ob[i][:], in0=ob[i][:], in1=ss[i][:].unsqueeze(2).to_broadcast([128, 4, 128]), op=ALU.mult),
                [("ob", i), ("ss", i)], [("ob", i)])
            P.op("dve", lambda e: e.tensor_tensor(
                out=ob[i][:], in0=ob[i][:], in1=gg[:].unsqueeze(1).to_broadcast([128, 4, 128]), op=ALU.mult),
                [("ob", i), "gg"], [("ob", i)])
            P.op("dve", lambda e: e.tensor_tensor(out=ob[i][:], in0=ob[i][:], in1=zz[i][:], op=ALU.mult),
                 [("ob", i), ("zz", i)], [("ob", i)])
            P.dma("sp", io["og"][c0:c0 + 128, :], ob[i][:].rearrange("p h d -> p (h d)"), [("ob", i)],
                  [("og", t)], f"ob{i}")

        part1(0)
        for t in range(NT):
            pa = P.capture(lambda: part1(t + 1)) if t + 1 < NT else []
            pb = P.capture(lambda: part2(t))
            if pa:
                P.replay_interleaved(pa, pb)
            else:
                P.replay_interleaved(pb, [])
        P.op("sp", None, reads=[("og", t) for t in range(NT)])
        P.emit("p3c")


def phase4(nc, io):
    with ExitStack() as st:
        P = Prog(nc)
        bias = [sb(st, nc, f"p4_bias{i}", [128, 8, 5, 128], F32) for i in range(2)]
        kwin = [sb(st, nc, f"p4_kwin{i}", [128, 4, 640], BF16) for i in range(2)]
        qw = [sb(st, nc, f"p4_qw{i}", [128, 4, 128], BF16) for i in range(2)]
        vwin = [sb(st, nc, f"p4_vwin{i}", [128, 5, 8, 65], BF16) for i in range(2)]
        tmp = [sb(st, nc, f"p4_tmp{i}", [128, 512], F32) for i in range(2)]
        ET = [sb(st, nc, f"p4_ET{i}", [128, 5, 8, 128], BF16) for i in range(2)]
        gna = sb(st, nc, "p4_gna", [128, 64], F32)
        rec = sb(st, nc, "p4_rec", [128, 8], F32)
        on = [sb(st, nc, f"p4_on{i}", [128, 8, 64], F32) for i in range(2)]
        sq = sb(st, nc, "p4_sq", [128, 8, 64], F32)
        ss = sb(st, nc, "p4_ss", [128, 8], F32)
        psS = [pst(st, nc, f"p4_psS{i}", [128, 512], F32) for i in range(4)]
        psO = [pst(st, nc, f"p4_psO{i}", [128, 512], F32) for i in range(4)]

        P.dma("sp", gna[:], io["na_g"].partition_broadcast(128), [], ["gna"], "c0")
        for i in range(2):
            P.op("pool", lambda e, i=i: e.memset(vwin[i][:], 1.0), [], [("vw", i, j) for j in range(5)])
        cnt = {"psS": 0, "tmp": 0}
        def part1(m):
            bi = m % 2
            ks = min(max(m - 2, 0), 29)
            var = m if m < 2 else 2
            bsl = 0 if var == 0 else (1 if var == 1 else 0)
            if m <= 2:
                P.dma("sp", bias[bsl][:].rearrange("p h j q -> p (h j q)"), io["nab"][var, :, :],
                      [], [("bias", bsl)], f"bias{bsl}")
            kw, qq, vw, et = kwin[bi], qw[bi], vwin[bi], ET[bi]
            P.dma("sp", kw[:], io["kn"][:, ks * 128:ks * 128 + 640].rearrange("(c p) t -> p c t", p=128),
                  [], [("kw", bi)], f"kw{bi}")
            P.dma("sp", qq[:], io["qn"][:, m * 128:(m + 1) * 128].rearrange("(c p) t -> p c t", p=128),
                  [], [("qw", bi)], f"qw{bi}")
            for j in range(5):
                P.dma("sp", vw[:, j, :, 0:64],
                      io["vn"][(ks + j) * 128:(ks + j + 1) * 128, :].rearrange("t (h d) -> t h d", h=8),
                      [], [("vw", bi, j)], f"vw{bi}_{j}")
            for hg in range(2):
                for j in range(5):
                    pi = cnt["psS"] % 4
                    cnt["psS"] += 1
                    pS = psS[pi]

                    def mm(e, hg=hg, j=j, pS=pS, kw=kw, qq=qq):
                        ins = None
                        for hh in range(4):
                            h = 2 * hh + hg
                            p0 = (h % 2) * 64
                            ins = e.matmul(pS[:, hh * 128:(hh + 1) * 128],
                                           lhsT=kw[p0:p0 + 64, h // 2, j * 128:(j + 1) * 128],
                                           rhs=qq[p0:p0 + 64, h // 2, :], start=True, stop=True)
                        return ins
                    P.op("pe", mm, [("kw", bi), ("qw", bi)], [("psS", pi)])
                    ti = cnt["tmp"] % 2
                    cnt["tmp"] += 1
                    t_ = tmp[ti]
                    P.op("dve", lambda e, t_=t_, pS=pS, hg=hg, j=j, bsl=bsl: e.tensor_tensor(
                        out=t_[:].rearrange("p (h q) -> p h q", h=4),
                        in0=pS[:].rearrange("p (h q) -> p h q", h=4),
                        in1=bias[bsl][:, hg * 4:(hg + 1) * 4, j, :], op=ALU.add),
                        [("psS", pi), ("bias", bsl)], [("tmp", ti)])
                    P.op("act", lambda e, t_=t_, et=et, hg=hg, j=j: e.activation(
                        out=et[:, j, hg * 4:(hg + 1) * 4, :],
                        in_=t_[:].rearrange("p (h q) -> p h q", h=4), func=AF.Exp),
                        [("tmp", ti)], [("ET", bi, hg, j)])
        def part2(m):
            bi = m % 2
            vw, et = vwin[bi], ET[bi]
            o_ = on[bi]
            for hg in range(2):
                pO = psO[(m % 2) * 2 + hg]
                kO = ("psO", (m % 2) * 2 + hg)

                def pv(e, hg=hg, pO=pO, et=et, vw=vw):
                    ins = None
                    for hh in range(4):
                        h = hg * 4 + hh
                        slot = (h % 2) * 4 + h // 2
                        for j in range(5):
                            ins = e.matmul(pO[:, hh * 65:(hh + 1) * 65], lhsT=et[:, j, slot, :],
                                           rhs=vw[:, j, h, :], start=(j == 0), stop=(j == 4))
                    return ins
                P.op("pe", pv, [("ET", bi, g_, j) for g_ in range(2) for j in range(5)] +
                     [("vw", bi, j) for j in range(5)], [kO])
                pv3 = pO[:, 0:260].rearrange("p (h d) -> p h d", h=4)
                P.op("dve", lambda e, pv3=pv3, hg=hg: e.reciprocal(
                    out=rec[:, hg * 4:(hg + 1) * 4], in_=pv3[:, :, 64]), [kO], [("rec", hg)])
                P.op("dve", lambda e, pv3=pv3, hg=hg, o_=o_: e.tensor_tensor(
                    out=o_[:, hg * 4:(hg + 1) * 4, :], in0=pv3[:, :, 0:64],
                    in1=rec[:, hg * 4:(hg + 1) * 4].unsqueeze(2).to_broadcast([128, 4, 64]), op=ALU.mult),
                    [kO, ("rec", hg)], [("on", bi, hg)])
            kon = [("on", bi, 0), ("on", bi, 1)]
            P.op("dve", lambda e, o_=o_: e.tensor_tensor(out=sq[:], in0=o_[:], in1=o_[:], op=ALU.mult),
                 kon, ["sq"])
            P.op("dve", lambda e: e.reduce_sum(out=ss[:], in_=sq[:], axis=AX.X), ["sq"], ["ss"])
            P.op("act", lambda e: e.activation(out=ss[:], in_=ss[:], func=AF.Ln, bias=RMS_EPS, scale=1.0 / 64),
                 ["ss"], ["ss1"])
            P.op("act", lambda e: e.activation(out=ss[:], in_=ss[:], func=AF.Exp, scale=-0.5),
                 ["ss1"], ["ss2"])
            P.op("dve", lambda e, o_=o_: e.tensor_tensor(
                out=o_[:], in0=o_[:], in1=ss[:].unsqueeze(2).to_broadcast([128, 8, 64]), op=ALU.mult),
                kon + ["ss2"], kon)
            P.op("dve", lambda e, o_=o_: e.tensor_tensor(
                out=o_[:], in0=o_[:], in1=gna[:].unsqueeze(1).to_broadcast([128, 8, 64]), op=ALU.mult),
                kon + ["gna"], kon)
            P.dma("sp", io["ona"][m * 128:(m + 1) * 128, :], o_[:].rearrange("p h d -> p (h d)"),
                  kon, [("ona", m)], f"on{bi}")
        part1(0)
        for m in range(NT):
            pa = P.capture(lambda: part1(m + 1)) if m + 1 < NT else []
            pb = P.capture(lambda: part2(m))
            if pa:
                P.replay_interleaved(pa, pb)
            else:
                P.replay_interleaved(pb, [])
        P.op("sp", None, reads=[("ona", m) for m in range(NT)])
        P.emit("p4")


def ln_rows(P, eng_aff, src, dst, stats, mv, rstd, gvec, bvec, ksrc, kdst, tag):
    for hh in range(2):
        P.op("dve", lambda e, hh=hh: e.bn_stats(out=stats[:, hh, :],
                                                 in_=src[:, hh * 512:(hh + 1) * 512]),
             [ksrc], [(tag, "st", hh)])
    P.op("dve", lambda e: e.bn_aggr(out=mv[:], in_=stats[:].rearrange("p a b -> p (a b)")),
         [(tag, "st", 0), (tag, "st", 1)], [(tag, "mv")])
    P.op("act", lambda e: e.activation(out=rstd[:], in_=mv[:, 1:2], func=AF.Ln, bias=LN_EPS),
         [(tag, "mv")], [(tag, "rs0")])
    P.op("act", lambda e: e.activation(out=rstd[:], in_=rstd[:], func=AF.Exp, scale=-0.5),
         [(tag, "rs0")], [(tag, "rs")])
    P.op("dve", lambda e: e.tensor_scalar(out=dst[:], in0=src[:], scalar1=mv[:, 0:1],
                                          scalar2=rstd[:, 0:1], op0=ALU.subtract, op1=ALU.mult),
         [ksrc, (tag, "mv"), (tag, "rs")], [kdst])
    P.op(eng_aff, lambda e: e.tensor_tensor(out=dst[:], in0=dst[:], in1=gvec[:], op=ALU.mult),
         [kdst, "v0", "v1", "v2", "v3"], [kdst])
    P.op(eng_aff, lambda e: e.tensor_tensor(out=dst[:], in0=dst[:], in1=bvec[:], op=ALU.add),
         [kdst, "v0", "v1", "v2", "v3"], [kdst])


def phase5(nc, io, resname="xn", outname="xo"):
    MT = 256
    NMT = T // MT
    with ExitStack() as st:
        P = Prog(nc)
        w1 = sb(st, nc, "p5_w1", [128, 8, DFF], BF16)
        w2 = sb(st, nc, "p5_w2", [128, 32, D], BF16)
        wo = sb(st, nc, "p5_wo", [128, 8, D], BF16)
        g1 = sb(st, nc, "p5_g1", [128, D], F32)
        bb1 = sb(st, nc, "p5_bb1", [128, D], F32)
        g2 = sb(st, nc, "p5_g2", [128, D], F32)
        bb2 = sb(st, nc, "p5_bb2", [128, D], F32)
        b1t = sb(st, nc, "p5_b1t", [128, 32], F32)
        b2r = sb(st, nc, "p5_b2r", [1, D], BF16)
        ones1 = sb(st, nc, "p5_ones1", [1, 128], BF16)
        ident = sb(st, nc, "p5_ident", [128, 128], F32)
        hT = sb(st, nc, "p5_hT", [128, 32, MT], BF16)
        xs = [sb(st, nc, f"p5_xs{i}", [128, D], F32) for i in range(4)]
        mixin = [sb(st, nc, f"p5_mi{i}", [128, D], F32) for i in range(1)]
        mixT = [sb(st, nc, f"p5_mT{i}", [128, 8, 128], BF16) for i in range(1)]
        x1T = sb(st, nc, "p5_x1T", [128, 8, MT], BF16)
        rl = [sb(st, nc, f"p5_rl{i}", [128, MT], F32) for i in range(2)]
        stats = [sb(st, nc, f"p5_stats{i}", [128, 2, 6], F32) for i in range(2)]
        mv = [sb(st, nc, f"p5_mv{i}", [128, 2], F32) for i in range(2)]
        rstd = [sb(st, nc, f"p5_rstd{i}", [128, 1], F32) for i in range(2)]
        psT = [pst(st, nc, f"p5_psT{i}", [128, 512], F32) for i in range(2)]
        psM = [pst(st, nc, f"p5_psM{i}", [128, 512], F32) for i in range(2)]
        psF = [pst(st, nc, f"p5_psF{i}", [128, 512], F32) for i in range(4)]

        P.dma("sp", ident[:], io["ident"][:, :], [], ["ident"], "c0")
        P.dma_multi("pool", [(wo[:, k, :], io["w_out"][k * 128:(k + 1) * 128, :]) for k in range(8)],
                    [("wo", k) for k in range(8)], "wo")
        P.dma("sp", g1[:], io["ln1_g"].partition_broadcast(128), [], ["v0"], "c1")
        P.dma("sp", bb1[:], io["ln1_b"].partition_broadcast(128), [], ["v1"], "c2")
        P.dma("sp", g2[:], io["ln2_g"].partition_broadcast(128), [], ["v2"], "c3")
        P.dma("sp", bb2[:], io["ln2_b"].partition_broadcast(128), [], ["v3"], "c4")
        P.dma("sp", b1t[:], io["b1t"][:, :], [], ["b1t"], "c5")
        P.dma("pool", b2r[:], io["b2"].rearrange("(o d) -> o d", o=1), [], ["b2r"], "c6")
        P.op("pool", lambda e: e.memset(ones1[:], 1.0), [], ["ones1"])
        P.dma_multi("pool", [(w1[:, k, :], io["w1"][k * 128:(k + 1) * 128, :]) for k in range(8)],
                    [("w1", k) for k in range(8)], "w1")
        for q4 in range(4):
            P.dma_multi("pool", [(w2[:, k, :], io["w2"][k * 128:(k + 1) * 128, :]) for k in range(q4 * 8, q4 * 8 + 8)],
                        [("w2", k) for k in range(q4 * 8, q4 * 8 + 8)], f"w2{q4}")
        wok = [("wo", k) for k in range(8)]
        w1k = [("w1", k) for k in range(8)]
        w2k = [("w2", k) for k in range(32)]
        cnt = {"psF": 0, "rl": 0}

        def transposes(src, ksrc, dstT, kdst, col0):
            for hb in range(2):
                pT = psT[hb]

                def tr(e, hb=hb, pT=pT):
                    ins = None
                    for kk in range(4):
                        k = hb * 4 + kk
                        ins = e.transpose(out=pT[:, kk * 128:(kk + 1) * 128],
                                          in_=src[:, k * 128:(k + 1) * 128], identity=ident[:])
                    return ins
                P.op("pe", tr, (ksrc if isinstance(ksrc, list) else [ksrc]) + ["ident"], [("psT", hb)])
                o = dstT[:, hb * 4:(hb + 1) * 4, col0:col0 + 128]
                i = pT[:].rearrange("p (k t) -> p k t", k=4)
                if hb == 0:
                    P.op("act", lambda e, o=o, i=i: e.copy(out=o, in_=i), [("psT", hb)], [(kdst, hb)])
                else:
                    P.op("dve", lambda e, o=o, i=i: e.tensor_copy(out=o, in_=i), [("psT", hb)], [(kdst, hb)])

        def stage_a1(mt):
            for sub in range(2):
                t0 = mt * MT + sub * 128
                si = (mt % 2) * 2 + sub
                mi = 0
                x_ = xs[si]
                m_ = mixin[mi]
                P.dma("sp", x_[:], io[resname][t0:t0 + 128, :], [], [("xs", si)], f"xs{si}")
                P.dma("sp", m_[:, 0:512], io["og"][t0:t0 + 128, :], [], [("mi", mi, 0)], f"mi{mi}a")
                P.dma("sp", m_[:, 512:1024], io["ona"][t0:t0 + 128, :], [], [("mi", mi, 1)], f"mi{mi}b")
                transposes(m_, [("mi", mi, 0), ("mi", mi, 1)], mixT[mi], ("mT", mi), 0)
                for half in range(2):
                    pM = psM[half]

                    def mm(e, half=half, pM=pM, mi=mi):
                        ins = None
                        for k in range(8):
                            ins = e.matmul(pM[:, :], lhsT=mixT[mi][:, k, :],
                                           rhs=wo[:, k, half * 512:(half + 1) * 512],
                                           start=(k == 0), stop=(k == 7))
                        return ins
                    P.op("pe", mm, [(("mT", mi), 0), (("mT", mi), 1)] + wok, [("psM", half)])
                    P.op("dve", lambda e, half=half, pM=pM, x_=x_: e.scalar_tensor_tensor(
                        out=x_[:, half * 512:(half + 1) * 512], in0=x_[:, half * 512:(half + 1) * 512],
                        scalar=ALPHA, in1=pM[:, :], op0=ALU.mult, op1=ALU.add),
                        [("xs", si), ("psM", half)], [("xs", si)])
                ln_rows(P, "dve", x_, x_, stats[0], mv[0], rstd[0], g1, bb1, ("xs", si), ("xs", si), "ln1")

        def stage_a2(mt):
            for sub in range(2):
                si = (mt % 2) * 2 + sub
                transposes(xs[si], ("xs", si), x1T, ("x1T", sub), sub * 128)

        def stage_b(mt):
            xk = [(("x1T", s_), h_) for s_ in range(2) for h_ in range(2)]
            for fc in range(32):
                pi = cnt["psF"] % 4
                cnt["psF"] += 1
                pF = psF[pi]

                def mm(e, fc=fc, pF=pF):
                    ins = None
                    for k in range(8):
                        ins = e.matmul(pF[:, 0:MT], lhsT=w1[:, k, fc * 128:(fc + 1) * 128],
                                       rhs=x1T[:, k, :], start=(k == 0), stop=(k == 7))
                    return ins
                P.op("pe", mm, xk + w1k, [("psF", pi)])
                ri = cnt["rl"] % 2
                cnt["rl"] += 1
                r_ = rl[ri]
                P.op("act", lambda e, r_=r_, pF=pF, fc=fc: e.activation(
                    out=r_[:], in_=pF[:, 0:MT], func=AF.Relu, bias=b1t[:, fc:fc + 1]),
                    [("psF", pi), "b1t"], [("rl", ri)])
                eng = "dve"
                P.op(eng, lambda e, r_=r_, fc=fc: e.tensor_tensor(
                    out=hT[:, fc, :], in0=r_[:], in1=r_[:], op=ALU.mult),
                    [("rl", ri)], [("hT", fc)])

        def stage_c(mt):
            hk = [("hT", fc) for fc in range(32)]
            for sub in range(2):
                t0 = mt * MT + sub * 128
                si = (mt % 2) * 2 + sub
                x_ = xs[si]
                for half in range(2):
                    pM = psM[half]

                    def mm(e, half=half, pM=pM, sub=sub):
                        ins = None
                        for fc in range(32):
                            ins = e.matmul(pM[:, :], lhsT=hT[:, fc, sub * 128:(sub + 1) * 128],
                                           rhs=w2[:, fc, half * 512:(half + 1) * 512],
                                           start=(fc == 0), stop=False)
                        ins = e.matmul(pM[:, :], lhsT=ones1[:, :], rhs=b2r[:, half * 512:(half + 1) * 512],
                                       start=False, stop=True)
                        return ins
                    P.op("pe", mm, hk + w2k + ["ones1", "b2r"], [("psM", half)])
                    P.op("dve", lambda e, half=half, pM=pM, x_=x_: e.scalar_tensor_tensor(
                        out=x_[:, half * 512:(half + 1) * 512], in0=x_[:, half * 512:(half + 1) * 512],
                        scalar=ALPHA, in1=pM[:, :], op0=ALU.mult, op1=ALU.add),
                        [("xs", si), ("psM", half)], [("xs", si)])
                ln_rows(P, "dve", x_, x_, stats[1], mv[1], rstd[1], g2, bb2, ("xs", si), ("xs", si), "ln2")
                P.dma("sp", io[outname][t0:t0 + 128, :], x_[:], [("xs", si)], [("xo", t0 // 128)], f"xs{si}")

        stage_a1(0)
        stage_a2(0)
        for mt in range(NMT):
            stage_b(mt)
            if mt + 1 < NMT:
                stage_a1(mt + 1)
            stage_c(mt)
            if mt + 1 < NMT:
                stage_a2(mt + 1)
        P.op("sp", None, reads=[("xo", i) for i in range(NT)])
        P.emit("p5")


PAIRS = [[0, 1], [2, 3], [4, 5], [6, 7]]


def phase_halo(nc, io):
    with ExitStack() as st:
        P = Prog(nc)
        jm = sb(st, nc, "px_j", [128, 128], F32)
        selt = sb(st, nc, "px_sel", [128, 2], F32)
        pa = [sb(st, nc, f"px_a{i}", [128, D], F32) for i in range(2)]
        pb = [sb(st, nc, f"px_b{i}", [128, D], F32) for i in range(2)]
        ro = [sb(st, nc, f"px_r{i}", [128, D], F32) for i in range(2)]
        ps = [pst(st, nc, f"px_ps{i}", [128, 512], F32) for i in range(4)]
        P.dma("sp", jm[:], io["jmat"][:, :], [], ["jm"], "c0")
        P.dma("sp", selt[:], io["sel"].partition_broadcast(128), [], ["selt"], "c1")
        P.op("pool", lambda e: e.collective_compute("AllGather", ALU.bypass, replica_groups=PAIRS,
                                                    ins=[io["x1"][T - 256:T, :]], outs=[io["hg"][:, :]]),
             [], ["hg"])
        for i in range(2):
            P.dma("sp", pa[i][:], io["hg"][i * 128:(i + 1) * 128, :], ["hg"], [("pa", i)], f"pa{i}")
            P.dma("sp", pb[i][:], io["hg"][256 + i * 128:256 + (i + 1) * 128, :], ["hg"], [("pb", i)], f"pb{i}")
            P.op("dve", lambda e, i=i: e.tensor_scalar(out=pa[i][:], in0=pa[i][:], scalar1=selt[:, 0:1],
                                                       scalar2=None, op0=ALU.mult), [("pa", i), "selt"], [("pa", i)])
            P.op("dve", lambda e, i=i: e.scalar_tensor_tensor(out=pa[i][:], in0=pb[i][:], scalar=selt[:, 1:2],
                                                              in1=pa[i][:], op0=ALU.mult, op1=ALU.add),
                 [("pa", i), ("pb", i), "selt"], [("pa", i)])
            for hh in range(2):
                p_ = ps[i * 2 + hh]
                P.op("pe", lambda e, i=i, hh=hh, p_=p_: e.matmul(p_[:, :], lhsT=jm[:, :],
                                                                  rhs=pa[i][:, hh * 512:(hh + 1) * 512],
                                                                  start=True, stop=True),
                     [("pa", i), "jm"], [("ps", i, hh)])
                P.op("act", lambda e, i=i, hh=hh, p_=p_: e.copy(out=ro[i][:, hh * 512:(hh + 1) * 512], in_=p_[:, :]),
                     [("ps", i, hh)], [("ro", i, hh)])
            dst0 = T + (1 - i) * 128
            P.dma("sp", io["x1"][dst0:dst0 + 128, :], ro[i][:], [("ro", i, 0), ("ro", i, 1)], [("halo", i)], f"ro{i}")
        P.op("sp", None, reads=[("halo", 0), ("halo", 1)])
        P.emit("px")


SCRATCH = {
    "xn": ([T, D], F32), "hq": ([1536, 4100], F32), "qn": ([512, T], BF16),
    "kn": ([512, TH], BF16), "vn": ([TH, 512], BF16), "zs": ([T, 512], F32),
    "gb": ([T, 16], F32), "og": ([T, 512], F32), "ona": ([T, 512], F32),
    "xo": ([T, D], F32), "qg": ([512, T], BF16), "kg": ([512, T], BF16),
    "ktok": ([T, 512], BF16), "vtok": ([T, 512], BF16), "oa": ([T, 512], F32),
    "ob": ([T, 512], F32), "st_out": ([128, 512], F32), "sg": ([256, 512], F32),
    "x1": ([TH, D], F32), "hg": ([512, D], F32),
}
NPDT = {F32: np.float32}


def build(phases, in_specs, out_names, layer0=True):
    nc = bass.Bass("TRN2", target_bir_lowering=False)
    io = {}
    for name, (shape, dt) in in_specs.items():
        io[name] = nc.dram_tensor(name, list(shape), dt, kind="ExternalInput").ap()
    for name, (shape, dt) in SCRATCH.items():
        if name in io:
            continue
        kind = "ExternalOutput" if name in out_names else "Internal"
        io[name] = nc.dram_tensor(name, list(shape), dt, kind=kind).ap()
    for ph in phases:
        if ph == "p1":
            phase1(nc, io, layer0)
        if ph == "p5":
            phase5(nc, io)
        if ph == "p4":
            phase4(nc, io)
        if ph == "p3a":
            phase3(nc, io, 0, "p3a")
        if ph == "p3b":
            phase3(nc, io, 1, "p3b")
        if ph == "p3c":
            phase3c(nc, io)
        if ph == "p2":
            phase2(nc, io, range(0, 4), "p2a")
            phase2(nc, io, range(4, 8), "p2b")
    return nc


def core_bs(c):
    return c // 2, c % 2


def host_layer_inputs(inp, l, c, x_full_loc):
    b, s = core_bs(c)
    w_in = np.array(inp["w_in"][l], dtype=np.float32, copy=True)
    a_log = np.asarray(inp["a_log"][l], np.float32)
    dt_b = np.asarray(inp["dt_bias"][l], np.float32)
    convw = np.asarray(inp["conv_w"][l], np.float32)
    if s == 1:
        a = w_in[:, 2048:2056].copy()
        bb = w_in[:, 2056:2064].copy()
        w_in[:, 2048:2052] = a[:, 4:8]
        w_in[:, 2052:2056] = a[:, 0:4]
        w_in[:, 2056:2060] = bb[:, 4:8]
        w_in[:, 2060:2064] = bb[:, 0:4]
        a_log = a_log[::-1]
        dt_b = dt_b[::-1]
        convw = convw[::-1]
    d = {
        "x": None if x_full_loc is None else np.ascontiguousarray(x_full_loc),
        "w_in": w_in,
        "gpar": np.ascontiguousarray(np.concatenate([a_log.reshape(-1), dt_b.reshape(-1)])),
        "convw": np.ascontiguousarray(convw.T),
        "ident": np.eye(128, dtype=np.float32),
    }
    return d


def na_bias_tables(rpb_l, s):
    out = np.empty((3, 128, 8, 5, 128), np.float32)
    for vi, m in enumerate((0, 1, 5)):
        ks = min(max(m - 2, 0), 29)
        qi = m * 128 + np.arange(128)
        ki = ks * 128 + np.arange(640)
        tq = qi if s == 0 else 8191 - qi
        tk = ki if s == 0 else 8191 - ki
        rq, cq = tq // 64, tq % 64
        rk, ck = tk // 64, tk % 64
        r0 = np.clip(rq - 4, 0, 120)
        w0 = np.clip(cq - 8, 0, 48)
        valid = ((rk[:, None] >= r0[None, :]) & (rk[:, None] < r0[None, :] + 8) &
                 (ck[:, None] >= w0[None, :]) & (ck[:, None] < w0[None, :] + 16))
        dr = np.clip(rk[:, None] - rq[None, :] + 7, 0, 14)
        dc = np.clip(ck[:, None] - cq[None, :] + 15, 0, 30)
        b = rpb_l[:, dr, dc]
        b = np.where(valid[None], b, np.float32(NEG)).astype(np.float32)
        b = b[[0, 2, 4, 6, 1, 3, 5, 7]]
        out[vi] = b.reshape(8, 5, 128, 128).transpose(2, 0, 1, 3)
    return np.ascontiguousarray(out.reshape(3, 128, 8 * 5 * 128))


def gdn_consts():
    i = np.arange(128)
    lt = (i[:, None] <= i[None, :]).astype(np.float32)
    xs = (i[:, None] > i[None, :]).astype(np.float32)
    nti = np.where(i[None, :] < i[:, None], np.float32(NEG), np.float32(0))
    ns = np.where(i[:, None] <= i[None, :], np.float32(NEG), np.float32(0))
    d = {"LTA": lt, "XSA": xs, "NTIA": np.tile(nti, (1, 4)), "NSA": np.tile(ns, (1, 4)),
         "LTB": lt.T, "XSB": xs.T, "NTIB": np.tile(nti.T, (1, 4)), "NSB": np.tile(ns.T, (1, 4)),
         "I4f": np.tile(np.eye(128, dtype=np.float32), (1, 4))}
    return {k: np.ascontiguousarray(v, dtype=np.float32) for k, v in d.items()}


_BF = ml_dtypes.bfloat16

L1_OUT = ["xn", "zs", "gb", "qg", "kg", "ktok", "vtok", "oa", "ona", "st_out"]


def _specs(maps):
    sp = {}
    for k, v in maps[0].items():
        sp[k] = (list(v.shape), BF16 if v.dtype == _BF else F32)
    return sp


def _run(phases, maps, outs, layer0=True):
    nc = build(phases, _specs(maps), outs, layer0=layer0)
    res = run_bass_kernel_spmd(nc, maps, core_ids=list(range(8)))
    return res.results


def kernel_unfused(x, ln_in_g, ln_in_b, w_in, conv_w, a_log, dt_bias, gdn_norm_g, rpb, na_norm_g,
                   w_out, ln1_g, ln1_b, w1, b1, w2, b2, ln2_g, ln2_b):
    inp = dict(x=x, w_in=w_in, conv_w=conv_w, a_log=a_log, dt_bias=dt_bias)
    x = np.asarray(x, np.float32)
    consts = gdn_consts()
    ident = np.eye(128, dtype=np.float32)
    zeros_st = np.zeros((128, 512), np.float32)
    xloc = []
    for c in range(8):
        b, s = core_bs(c)
        xs = x[b] if s == 0 else x[b][::-1]
        xloc.append(np.ascontiguousarray(xs[:TH]))
    for l in range(2):
        maps = []
        for c in range(8):
            b, s = core_bs(c)
            m = host_layer_inputs(inp, l, c, xloc[c])
            if l == 0:
                m["ln_in_g"] = np.asarray(ln_in_g, np.float32)
                m["ln_in_b"] = np.asarray(ln_in_b, np.float32)
            m["nab"] = na_bias_tables(np.asarray(rpb[l], np.float32), s)
            m["na_g"] = np.asarray(na_norm_g[l], np.float32)
            for k in ("LTA", "XSA", "NTIA", "NSA", "I4f"):
                m[k] = consts[k]
            m["st_in"] = zeros_st
            maps.append(m)
        outs1 = [o for o in L1_OUT if not (o == "xn" and l > 0)]
        r1 = _run(["p1", "p2", "p4", "p3a"], maps, outs1, layer0=(l == 0))
        maps2 = []
        for c in range(8):
            r = r1[c]
            m = {k: np.ascontiguousarray(r[k]) for k in ("qg", "kg", "ktok", "vtok", "gb", "oa", "zs", "ona")}
            m["xn"] = np.ascontiguousarray(r["xn"]) if l == 0 else np.ascontiguousarray(xloc[c][:T])
            m["st_in"] = np.ascontiguousarray(r1[c ^ 1]["st_out"])
            for k in ("LTB", "XSB", "NTIB", "NSB", "I4f"):
                m[k] = consts[k]
            m["ident"] = ident
            m["gdn_g"] = np.asarray(gdn_norm_g[l], np.float32)
            m["w_out"] = np.asarray(w_out[l], np.float32)
            m["w1"] = np.asarray(w1[l], np.float32)
            m["w2"] = np.asarray(w2[l], np.float32)
            m["b1t"] = np.ascontiguousarray(np.asarray(b1[l], np.float32).reshape(32, 128).T)
            m["b2"] = np.asarray(b2[l], np.float32)
            m["ln1_g"] = np.asarray(ln1_g[l], np.float32)
            m["ln1_b"] = np.asarray(ln1_b[l], np.float32)
            m["ln2_g"] = np.asarray(ln2_g[l], np.float32)
            m["ln2_b"] = np.asarray(ln2_b[l], np.float32)
            maps2.append(m)
        r2 = _run(["p3b", "p3c", "p5"], maps2, ["xo"])
        xo = [np.asarray(r2[c]["xo"], np.float32) for c in range(8)]
        xloc = [np.ascontiguousarray(np.concatenate([xo[c], xo[c ^ 1][::-1][:TH - T]], 0)) for c in range(8)]
    out = np.empty((4, 8192, D), np.float32)
    for c in range(8):
        b, s = core_bs(c)
        if s == 0:
            out[b, :T] = xloc[c][:T]
        else:
            out[b, T:] = xloc[c][:T][::-1]
    return out


LAYER_IN = ["w_in", "gpar", "convw", "nab", "na_g", "gdn_g", "w_out", "w1", "w2", "b1t", "b2",
            "ln1_g", "ln1_b", "ln2_g", "ln2_b"]


def build_fused(in_specs):
    nc = bass.Bass("TRN2", target_bir_lowering=False)
    io = {}
    for name, (shape, dt) in in_specs.items():
        io[name] = nc.dram_tensor(name, list(shape), dt, kind="ExternalInput").ap()
    for name, (shape, dt) in SCRATCH.items():
        io[name] = nc.dram_tensor(name, list(shape), dt, kind="Internal").ap()
    io["out"] = nc.dram_tensor("out", [T, D], F32, kind="ExternalOutput").ap()
    for l in range(2):
        iol = dict(io)
        for k in LAYER_IN:
            iol[k] = io[f"{k}_l{l}"]
        phase1(nc, iol, l == 0, "x" if l == 0 else "x1")
        phase2(nc, iol, range(0, 4), f"p2a{l}")
        phase2(nc, iol, range(4, 8), f"p2b{l}")
        phase4(nc, iol)
        phase3(nc, iol, 0, f"p3a{l}", fused=True)
        phase3(nc, iol, 1, f"p3b{l}", fused=True)
        phase3c(nc, iol)
        if l == 0:
            phase5(nc, iol, "xn", "x1")
            phase_halo(nc, iol)
        else:
            phase5(nc, iol, "x1", "out")
    return nc


def kernel(x, ln_in_g, ln_in_b, w_in, conv_w, a_log, dt_bias, gdn_norm_g, rpb, na_norm_g,
           w_out, ln1_g, ln1_b, w1, b1, w2, b2, ln2_g, ln2_b):
    inp = dict(w_in=w_in, conv_w=conv_w, a_log=a_log, dt_bias=dt_bias)
    x = np.asarray(x, np.float32)
    consts = gdn_consts()
    f = lambda a: np.ascontiguousarray(np.asarray(a, np.float32))
    maps = []
    for c in range(8):
        b, s = core_bs(c)
        xs = x[b] if s == 0 else x[b][::-1]
        m = {"x": np.ascontiguousarray(xs[:TH]), "ln_in_g": f(ln_in_g), "ln_in_b": f(ln_in_b),
             "ident": np.eye(128, dtype=np.float32), "jmat": np.ascontiguousarray(np.eye(128, dtype=np.float32)[::-1]),
             "sel": np.array([1.0, 0.0] if s == 1 else [0.0, 1.0], np.float32)}
        m.update(consts)
        for l in range(2):
            hl = host_layer_inputs(inp, l, c, None)
            m[f"w_in_l{l}"] = hl["w_in"]
            m[f"gpar_l{l}"] = hl["gpar"]
            m[f"convw_l{l}"] = hl["convw"]
            m[f"nab_l{l}"] = na_bias_tables(f(rpb[l]), s)
            m[f"na_g_l{l}"] = f(na_norm_g[l])
            m[f"gdn_g_l{l}"] = f(gdn_norm_g[l])
            m[f"w_out_l{l}"] = f(w_out[l])
            m[f"w1_l{l}"] = f(w1[l])
            m[f"w2_l{l}"] = f(w2[l])
            m[f"b1t_l{l}"] = np.ascontiguousarray(f(b1[l]).reshape(32, 128).T)
            m[f"b2_l{l}"] = f(b2[l])
            m[f"ln1_g_l{l}"] = f(ln1_g[l])
            m[f"ln1_b_l{l}"] = f(ln1_b[l])
            m[f"ln2_g_l{l}"] = f(ln2_g[l])
            m[f"ln2_b_l{l}"] = f(ln2_b[l])
        maps.append(m)
    nc = build_fused(_specs(maps))
    res = run_bass_kernel_spmd(nc, maps, core_ids=list(range(8)))
    out = np.empty((4, 8192, D), np.float32)
    for c in range(8):
        b, s = core_bs(c)
        o = np.asarray(res.results[c]["out"], np.float32)
        if s == 0:
            out[b, :T] = o
        else:
            out[b, T:] = o[::-1]
    return out
```
